# Optimizing a Trainium2 kernel written in Bass

```python
import jax, jax.numpy as jnp
from jax import lax
import numpy as np

D_MODEL = 2048
BATCH = 32
SEQ = 256
DEPTH = 1
DEC_BATCH = 8
DEC_SEQ = 4096
PAST_LEN = 256

GRID_W = 64
MIX_W = D_MODEL
MLA_W = D_MODEL // 2
RET_W = MIX_W - MLA_W
MLA_NOPE = 128
ROPE_DIM = 64
QK_HEAD = MLA_NOPE + ROPE_DIM
MLA_V = 128
MLA_HEADS = MLA_W // MLA_V
Q_LORA = 384
KV_LORA = 256
RET_DV = 256
RET_DK = 128
RET_HEADS = RET_W // RET_DV
RET_QK = RET_HEADS * RET_DK
RET_CHUNK = 128
Q_BLOCK = 128
ROPE_BASE = 10000.0
EPS = 1e-6
IN_SIZES = (Q_LORA, KV_LORA, ROPE_DIM, MLA_W, RET_QK, RET_QK, RET_W, RET_W)
IN_COLS = Q_LORA + KV_LORA + ROPE_DIM + MLA_W + 2 * RET_QK + 2 * RET_W

kernel_name = 'hybrid_mla_retention_diffusion_step'


def rms_norm(x, w):
    xf = x.astype(jnp.float32)
    y = xf * lax.rsqrt(jnp.mean(xf * xf, axis=-1, keepdims=True) + EPS)
    return (y * w.astype(jnp.float32)).astype(x.dtype)


def rope_angles(pos, dim):
    half = dim // 2
    freqs = ROPE_BASE ** (-jnp.arange(half, dtype=jnp.float32) / half)
    return pos.astype(jnp.float32)[:, None] * freqs[None, :]


def apply_rotary(x, ang):
    cos = jnp.cos(ang)[None, :, None, :]
    sin = jnp.sin(ang)[None, :, None, :]
    xf = x.astype(jnp.float32)
    x1, x2 = jnp.split(xf, 2, axis=-1)
    return jnp.concatenate([x1 * cos - x2 * sin, x2 * cos + x1 * sin], axis=-1).astype(x.dtype)


def axial_rope_on_head(x, ang_row, ang_col):
    nope, rope = x[..., :MLA_NOPE], x[..., MLA_NOPE:]
    half = ROPE_DIM // 2
    rope = jnp.concatenate([apply_rotary(rope[..., :half], ang_row),
                            apply_rotary(rope[..., half:], ang_col)], axis=-1)
    return jnp.concatenate([nope, rope], axis=-1)


def adaln(x, mod, norm_w):
    shift, scale, gate = jnp.split(mod, 3, axis=-1)
    h = rms_norm(x, norm_w) * (1.0 + scale[:, None, :]) + shift[:, None, :]
    return h, gate[:, None, :]


def input_branches(h, w_in, q_norm_w, w_uq, qk_q_w, kv_norm_w):
    B, L, _ = h.shape
    offs = [int(v) for v in np.cumsum(IN_SIZES)[:-1]]
    q_lat, ckv_raw, k_rope, g_a, rq, rk, rv, g_b = jnp.split(h @ w_in, offs, axis=-1)
    q = (rms_norm(q_lat, q_norm_w) @ w_uq).reshape(B, L, MLA_HEADS, QK_HEAD)
    q = rms_norm(q, qk_q_w)
    ckv = rms_norm(ckv_raw, kv_norm_w)
    rq = rq.reshape(B, L, RET_HEADS, RET_DK)
    rk = rk.reshape(B, L, RET_HEADS, RET_DK) * (RET_DK ** -0.5)
    rv = rv.reshape(B, L, RET_HEADS, RET_DV)
    return q, ckv, k_rope, g_a, rq, rk, rv, g_b


def mla_keys_values(ckv, k_rope, w_uk, w_uv, qk_k_w):
    B, L, _ = ckv.shape
    k_nope = (ckv @ w_uk).reshape(B, L, MLA_HEADS, MLA_NOPE)
    k_r = jnp.broadcast_to(k_rope[:, :, None, :], (B, L, MLA_HEADS, ROPE_DIM))
    k = rms_norm(jnp.concatenate([k_nope, k_r], axis=-1), qk_k_w)
    v = (ckv @ w_uv).reshape(B, L, MLA_HEADS, MLA_V)
    return k, v


def block_attention(q, k, v):
    B, Lq, H, Dh = q.shape
    nblk = Lq // Q_BLOCK
    qb = q.reshape(B, nblk, Q_BLOCK, H, Dh).transpose(1, 0, 2, 3, 4)
    scale = Dh ** -0.5

    def one_block(qi):
        s = jnp.einsum('bqhd,bkhd->bhqk', qi, k).astype(jnp.float32) * scale
        p = jax.nn.softmax(s, axis=-1)
        return jnp.einsum('bhqk,bkhe->bqhe', p.astype(v.dtype), v)

    out = lax.map(one_block, qb)
    return out.transpose(1, 0, 2, 3, 4).reshape(B, Lq, H, v.shape[-1])


def retention_scan(q, k, v, log_gamma, s0, strict):
    B, L, H, DK = q.shape
    DV = v.shape[-1]
    C = RET_CHUNK
    n = L // C
    lg = log_gamma.astype(jnp.float32)
    idx = jnp.arange(C, dtype=jnp.float32)
    diff = idx[:, None] - idx[None, :]
    mask = (diff > 0) if strict else (diff >= 0)
    intra = jnp.where(mask[None], jnp.exp(jnp.where(mask, diff, 0.0)[None] * lg[:, None, None]), 0.0)
    q_dec = jnp.exp((idx[:, None] + 1.0) * lg[None, :])
    k_dec = jnp.exp((C - 1.0 - idx)[:, None] * lg[None, :])
    chunk_dec = jnp.exp(C * lg)

    def chunks(a):
        return a.reshape(B, n, C, H, a.shape[-1]).transpose(1, 0, 2, 3, 4).astype(jnp.float32)

    def step(S, xs):
        qc, kc, vc = xs
        s = jnp.einsum('bihd,bjhd->bhij', qc, kc) * intra[None]
        o = (jnp.einsum('bhij,bjhe->bihe', s, vc)
             + jnp.einsum('bihd,bhde->bihe', qc * q_dec[None, :, :, None], S))
        S = (S * chunk_dec[None, :, None, None]
             + jnp.einsum('bjhd,bjhe->bhde', kc * k_dec[None, :, :, None], vc))
        return S, o

    S, o = lax.scan(step, s0.astype(jnp.float32), (chunks(q), chunks(k), chunks(v)))
    o = o.transpose(1, 0, 2, 3, 4).reshape(B, L, H, DV).astype(v.dtype)
    return o, S


def bidir_retention(q, k, v, lg_f, lg_b, s_f, s_b):
    o_f, S_f = retention_scan(q, k, v, lg_f, s_f, False)
    o_b, S_b = retention_scan(jnp.flip(q, 1), jnp.flip(k, 1), jnp.flip(v, 1), lg_b, s_b, True)
    return o_f + jnp.flip(o_b, 1), S_f, S_b


def merge_output(attn, ret, g_a, g_b, gn_w, w_out):
    B, L = attn.shape[:2]
    ret = rms_norm(ret, gn_w)
    mix = jnp.concatenate([jax.nn.silu(g_a) * attn.reshape(B, L, MLA_W),
                           jax.nn.silu(g_b) * ret.reshape(B, L, RET_W)], axis=-1)
    return mix @ w_out


def setup_inputs(seed: int = 0) -> dict:
    key = jax.random.key(seed)
    ks = jax.random.split(key, 24)
    nrm = jax.random.normal
    f32 = jnp.float32
    a0 = jnp.asarray(np.log(-np.log1p(-2.0 ** (-5.0 - np.arange(RET_HEADS)))), dtype=f32)
    return {
        'x_prompt': nrm(ks[0], (BATCH, SEQ, D_MODEL), f32),
        'x_sample': nrm(ks[1], (DEC_BATCH, DEC_SEQ, D_MODEL), f32),
        'c': nrm(ks[2], (DEC_BATCH, D_MODEL), f32),
        'cache_mla_ckv': nrm(ks[3], (DEC_BATCH, DEPTH, PAST_LEN, KV_LORA), f32),
        'cache_mla_krope': nrm(ks[4], (DEC_BATCH, DEPTH, PAST_LEN, ROPE_DIM), f32),
        'state_ret_fwd': 0.1 * nrm(ks[5], (DEC_BATCH, DEPTH, RET_HEADS, RET_DK, RET_DV), f32),
        'state_ret_bwd': 0.1 * nrm(ks[6], (DEC_BATCH, DEPTH, RET_HEADS, RET_DK, RET_DV), f32),
        'c_ctx': nrm(ks[7], (D_MODEL,), f32),
        'norm_w': 1.0 + 0.01 * nrm(ks[8], (DEPTH, D_MODEL), f32),
        'w_mod': 0.5 * D_MODEL ** -0.5 * nrm(ks[9], (DEPTH, D_MODEL, 3 * D_MODEL), f32),
        'b_mod': 0.01 * nrm(ks[10], (DEPTH, 3 * D_MODEL), f32),
        'w_in': D_MODEL ** -0.5 * nrm(ks[11], (DEPTH, D_MODEL, IN_COLS), f32),
        'mla_q_norm_w': 1.0 + 0.01 * nrm(ks[12], (DEPTH, Q_LORA), f32),
        'mla_w_uq': Q_LORA ** -0.5 * nrm(ks[13], (DEPTH, Q_LORA, MLA_HEADS * QK_HEAD), f32),
        'mla_kv_norm_w': 1.0 + 0.01 * nrm(ks[14], (DEPTH, KV_LORA), f32),
        'mla_w_uk': KV_LORA ** -0.5 * nrm(ks[15], (DEPTH, KV_LORA, MLA_HEADS * MLA_NOPE), f32),
        'mla_w_uv': KV_LORA ** -0.5 * nrm(ks[16], (DEPTH, KV_LORA, MLA_W), f32),
        'mla_qk_q_w': 1.0 + 0.01 * nrm(ks[17], (DEPTH, QK_HEAD), f32),
        'mla_qk_k_w': 1.0 + 0.01 * nrm(ks[18], (DEPTH, QK_HEAD), f32),
        'ret_log_decay_fwd': a0[None, :] + 0.01 * nrm(ks[19], (DEPTH, RET_HEADS), f32),
        'ret_log_decay_bwd': a0[None, :] + 0.01 * nrm(ks[20], (DEPTH, RET_HEADS), f32),
        'ret_gn_w': 1.0 + 0.01 * nrm(ks[21], (DEPTH, RET_HEADS, RET_DV), f32),
        'w_out': MIX_W ** -0.5 * nrm(ks[22], (DEPTH, MIX_W, D_MODEL), f32),
    }


def reference(x_prompt, x_sample, c, cache_mla_ckv, cache_mla_krope, state_ret_fwd, state_ret_bwd,
              c_ctx, norm_w, w_mod, b_mod, w_in, mla_q_norm_w, mla_w_uq, mla_kv_norm_w, mla_w_uk,
              mla_w_uv, mla_qk_q_w, mla_qk_k_w, ret_log_decay_fwd, ret_log_decay_bwd, ret_gn_w, w_out):
    L_lat = x_sample.shape[1]
    ROWS = L_lat // GRID_W
    row = jnp.repeat(jnp.arange(ROWS), GRID_W)
    col = jnp.tile(jnp.arange(GRID_W), ROWS)
    ang_row = rope_angles(row, ROPE_DIM // 2)
    ang_col = rope_angles(col, ROPE_DIM // 2)
    ang_ret = rope_angles(jnp.arange(L_lat), RET_DK)

    xp = x_prompt
    Bp = x_prompt.shape[0]
    ckv_list, krope_list, sf_list, sb_list = [], [], [], []
    for l in range(DEPTH):
        lg_f = -jnp.exp(ret_log_decay_fwd[l].astype(jnp.float32))
        lg_b = -jnp.exp(ret_log_decay_bwd[l].astype(jnp.float32))
        mod = jax.nn.silu(c_ctx)[None, :] @ w_mod[l] + b_mod[l]
        h, gate = adaln(xp, mod, norm_w[l])
        q, ckv, k_rope, g_a, rq, rk, rv, g_b = input_branches(
            h, w_in[l], mla_q_norm_w[l], mla_w_uq[l], mla_qk_q_w[l], mla_kv_norm_w[l])
        k, v = mla_keys_values(ckv, k_rope, mla_w_uk[l], mla_w_uv[l], mla_qk_k_w[l])
        attn = block_attention(q, k, v)
        zero_state = jnp.zeros((Bp, RET_HEADS, RET_DK, RET_DV), jnp.float32)
        ret, s_f, s_b = bidir_retention(rq, rk, rv, lg_f, lg_b, zero_state, zero_state)
        xp = xp + gate * merge_output(attn, ret, g_a, g_b, ret_gn_w[l], w_out[l])
        ckv_list.append(ckv)
        krope_list.append(k_rope)
        sf_list.append(s_f)
        sb_list.append(s_b)

    xs = x_sample
    for l in range(DEPTH):
        lg_f = -jnp.exp(ret_log_decay_fwd[l].astype(jnp.float32))
        lg_b = -jnp.exp(ret_log_decay_bwd[l].astype(jnp.float32))
        mod = jax.nn.silu(c) @ w_mod[l] + b_mod[l]
        h, gate = adaln(xs, mod, norm_w[l])
        q, ckv, k_rope, g_a, rq, rk, rv, g_b = input_branches(
            h, w_in[l], mla_q_norm_w[l], mla_w_uq[l], mla_qk_q_w[l], mla_kv_norm_w[l])
        q = axial_rope_on_head(q, ang_row, ang_col)
        k_lat, v_lat = mla_keys_values(ckv, k_rope, mla_w_uk[l], mla_w_uv[l], mla_qk_k_w[l])
        k_lat = axial_rope_on_head(k_lat, ang_row, ang_col)
        k_ctx, v_ctx = mla_keys_values(cache_mla_ckv[:, l], cache_mla_krope[:, l],
                                       mla_w_uk[l], mla_w_uv[l], mla_qk_k_w[l])
        attn = block_attention(q, jnp.concatenate([k_lat, k_ctx], axis=1),
                               jnp.concatenate([v_lat, v_ctx], axis=1))
        rq = apply_rotary(rq, ang_ret)
        rk = apply_rotary(rk, ang_ret)
        ret, _, _ = bidir_retention(rq, rk, rv, lg_f, lg_b, state_ret_fwd[:, l], state_ret_bwd[:, l])
        xs = xs + gate * merge_output(attn, ret, g_a, g_b, ret_gn_w[l], w_out[l])

    new_ckv = jnp.stack(ckv_list, axis=1)
    new_krope = jnp.stack(krope_list, axis=1)
    new_sf = jnp.stack(sf_list, axis=1)
    new_sb = jnp.stack(sb_list, axis=1)
    return (xp, xs, new_ckv, new_krope, new_sf, new_sb)
```

```python
import numpy as np
from contextlib import ExitStack
import concourse.bass as bass
import concourse.mybir as mybir
from concourse.bass_utils import run_bass_kernel_spmd

F32 = mybir.dt.float32
BF16 = mybir.dt.bfloat16
F32R = mybir.dt.float32r
ALU = mybir.AluOpType
AF = mybir.ActivationFunctionType

EPS = 1e-6
NDSEM = 40
ENGS = ['pe', 'act', 'dve', 'pool', 'sp']
DEBUG = False
LIMITS = {}


class Tile:
    __slots__ = ('ap', 'name', 'lw', 'rd', 'excl')

    def __init__(self, ap, name, excl=False):
        self.ap = ap
        self.name = name
        self.lw = None
        self.rd = {}
        self.excl = excl

    def __getitem__(self, k):
        return View(self.ap[k], self)

    def v(self):
        return View(self.ap, self)


class View:
    __slots__ = ('ap', 'tile')

    def __init__(self, ap, tile):
        self.ap = ap
        self.tile = tile

    def __getitem__(self, k):
        return View(self.ap[k], self.tile)

    def re(self, pat, **kw):
        return View(self.ap.rearrange(pat, **kw), self.tile)

    def bc(self, shape):
        return View(self.ap.to_broadcast(list(shape)), self.tile)

    def un(self, axis):
        return View(self.ap.unsqueeze(axis), self.tile)


class Op:
    __slots__ = ('id', 'eng', 'fn', 'deps', 'sig', 'dma', 'dsem', 'dval')


class Prog:
    def __init__(self, nc):
        self.nc = nc
        self.ops = []
        self.by_eng = {e: [] for e in ENGS}
        self.ndma = 0
        self.dma_ops = []
        self.pending = {e: set() for e in ENGS}
        self.last_compute = {e: None for e in ENGS}

    def add(self, eng, fn, reads=(), writes=(), dma=False):
        op = Op()
        op.id = len(self.ops)
        op.eng = eng
        op.fn = fn
        op.dma = dma
        op.sig = None
        deps = set()
        for t in reads:
            if t.lw is not None:
                deps.add(t.lw)
            if t.excl:
                for k, v in t.rd.items():
                    if k != eng and k != 'dma':
                        deps.add(v)
        for t in writes:
            if t.lw is not None:
                deps.add(t.lw)
            for k, v in t.rd.items():
                if k == 'dma':
                    deps.update(v)
                else:
                    deps.add(v)
        deps |= self.pending[eng]
        self.pending[eng] = set()
        if dma:
            j = self.ndma
            self.ndma += 1
            op.dsem = j % NDSEM
            op.dval = 16 * (j // NDSEM + 1)
            if j >= NDSEM:
                deps.add(self.dma_ops[j - NDSEM].id)
            self.dma_ops.append(op)
        deps.discard(op.id)
        if (not dma) and eng == 'pe':
            deps = {d for d in deps if self.ops[d].dma or self.ops[d].eng != 'pe'}
        op.deps = deps
        for t in reads:
            if dma:
                t.rd.setdefault('dma', []).append(op.id)
            else:
                t.rd[eng] = op.id
        for t in writes:
            t.lw = op.id
            t.rd = {}
        self.ops.append(op)
        self.by_eng[eng].append(op)
        if not dma:
            self.last_compute[eng] = op.id
        return op

    def barrier(self):
        s = set()
        for e in ENGS:
            if self.last_compute[e] is not None:
                s.add(self.last_compute[e])
        for op in self.dma_ops[-NDSEM:]:
            s.add(op.id)
        for e in ENGS:
            self.pending[e] |= s

    def mm(self, out, lhsT, rhs, start=True, stop=True):
        o, l, r = out.ap, lhsT.ap, rhs.ap
        rd = [lhsT.tile, rhs.tile] + ([] if start else [out.tile])
        self.add('pe', lambda e: e.matmul(o, l, r, start=start, stop=stop), rd, [out.tile])

    def tr(self, out, in_, ident):
        o, i, d = out.ap, in_.ap, ident.ap
        self.add('pe', lambda e: e.transpose(o, i, d), [in_.tile, ident.tile], [out.tile])

    def act(self, out, in_, func, bias=None, scale=None, accum=None):
        o, i = out.ap, in_.ap
        kw = {}
        rd = [in_.tile]
        wr = [out.tile]
        if bias is not None:
            if isinstance(bias, View):
                kw['bias'] = bias.ap
                rd.append(bias.tile)
            else:
                kw['bias'] = bias
        if scale is not None:
            if isinstance(scale, View):
                kw['scale'] = scale.ap
                rd.append(scale.tile)
            else:
                kw['scale'] = scale
        if accum is not None:
            kw['accum_out'] = accum.ap
            wr.append(accum.tile)
        self.add('act', lambda e: e.activation(o, i, func, **kw), rd, wr)

    def tt(self, eng, out, a, b, op):
        o, x, y = out.ap, a.ap, b.ap
        self.add(eng, lambda e: e.tensor_tensor(out=o, in0=x, in1=y, op=op), [a.tile, b.tile], [out.tile])

    def ts(self, eng, out, a, s1, s2, op0, op1=None):
        o, x = out.ap, a.ap
        rd = [a.tile]

        def cv(s):
            if isinstance(s, View):
                rd.append(s.tile)
                return s.ap
            return s
        v1 = cv(s1)
        v2 = cv(s2) if s2 is not None else None
        if op1 is None:
            self.add(eng, lambda e: e.tensor_scalar(o, x, v1, None, op0), rd, [out.tile])
        else:
            self.add(eng, lambda e: e.tensor_scalar(o, x, v1, v2, op0, op1), rd, [out.tile])

    def stt(self, eng, out, in0, scalar, in1, op0, op1):
        o, x, y = out.ap, in0.ap, in1.ap
        rd = [in0.tile, in1.tile]
        if isinstance(scalar, View):
            rd.append(scalar.tile)
            sc = scalar.ap
        else:
            sc = scalar
        self.add(eng, lambda e: e.scalar_tensor_tensor(out=o, in0=x, scalar=sc, in1=y, op0=op0, op1=op1),
                 rd, [out.tile])

    def cp(self, eng, out, in_, scale=None):
        o, i = out.ap, in_.ap
        if eng == 'act':
            if scale is None:
                self.add('act', lambda e: e.activation(o, i, AF.Copy), [in_.tile], [out.tile])
            else:
                self.add('act', lambda e: e.activation(o, i, AF.Copy, scale=scale), [in_.tile], [out.tile])
        else:
            if scale is None:
                self.add(eng, lambda e: e.tensor_copy(out=o, in_=i), [in_.tile], [out.tile])
            else:
                self.add(eng, lambda e: e.tensor_scalar(o, i, scale, None, ALU.mult), [in_.tile], [out.tile])

    def recip(self, out, in_):
        o, i = out.ap, in_.ap
        self.add('dve', lambda e: e.reciprocal(out=o, in_=i), [in_.tile], [out.tile])

    def memset(self, eng, out, val):
        o = out.ap
        self.add(eng, lambda e: e.memset(o, val), [], [out.tile])

    def dma(self, out, in_, q='sp'):
        o, i = out.ap, in_.ap
        self.add(q, lambda e: e.dma_start(out=o, in_=i), [in_.tile], [out.tile], dma=True)

    def emit(self):
        nc = self.nc
        self.barrier()
        self.add('sp', lambda e: e.nop(), [], [])
        needed = set()
        for op in self.ops:
            needed |= op.deps
        cnt = {e: 0 for e in ENGS}
        for op in self.ops:
            if op.dma:
                continue
            if op.id in needed:
                cnt[op.eng] += 1
                op.sig = cnt[op.eng]
        with ExitStack() as st:
            sems = {e: st.enter_context(nc.semaphore(f"sem_{e}")) for e in ENGS}
            dsems = [st.enter_context(nc.semaphore(f"dsem{i}")) for i in range(NDSEM)]
            block = st.enter_context(nc.Block())
            ops = self.ops

            def run_engine(ename, h):
                waited = {}
                for op in self.by_eng[ename]:
                    req = {}
                    for d in op.deps:
                        p = ops[d]
                        if p.dma:
                            key = ('d', p.dsem)
                            val = p.dval
                        else:
                            key = ('e', p.eng)
                            val = p.sig
                        if val > req.get(key, 0):
                            req[key] = val
                    for key, val in req.items():
                        if waited.get(key, 0) < val:
                            sem = dsems[key[1]] if key[0] == 'd' else sems[key[1]]
                            h.wait_ge(sem, val)
                            waited[key] = val
                    ins = op.fn(h)
                    if op.dma:
                        ins.then_inc(dsems[op.dsem], 16)
                    elif op.sig is not None:
                        ins.then_inc(sems[ename], 1)

            @block.tensor
            def _(h):
                run_engine('pe', h)

            @block.scalar
            def _(h):
                run_engine('act', h)

            @block.vector
            def _(h):
                run_engine('dve', h)

            @block.gpsimd
            def _(h):
                run_engine('pool', h)

            @block.sync
            def _(h):
                run_engine('sp', h)


D = 2048
LS = 4096
LP = 256
NPS = 4
NTOK = LS + NPS * LP
PAST = 256
NKEY = NTOK + PAST
TB = 512
NBLK = NTOK // TB
NSB = LS // TB
H = 8
RH = 4
WIN_COLS = 5184
SLABS = [(0, 512), (512, 256), (768, 512), (1280, 512), (1792, 512), (2304, 512),
         (2816, 512), (3328, 512), (3840, 512), (4352, 512), (4864, 320)]
ARENA_BYTES = 207 * 1024

V_QNW = 0
V_KVW = 3
V_WQN = 5
V_WQR = 6
V_WQP = 7
V_WKN = 8
V_WKR = 9
V_WKP = 10
V_GNW = 11
NV = 19
RT_D1 = 0
RT_D2 = 128
RT_QDF = 256
RT_QDB = 384
RT_KDF = 512
RT_KDB = 513
RT_C = 514
NRT = 515


def build_program(debug=False, stop_after=None):
    nc = bass.Bass("TRN2", target_bir_lowering=False)
    P = Prog(nc)

    def din(name, shape, dt=F32):
        return Tile(nc.dram_tensor(name, list(shape), dt, kind="ExternalInput").ap(), name)

    def dout(name, shape, dt=F32):
        return Tile(nc.dram_tensor(name, list(shape), dt, kind="ExternalOutput").ap(), name)

    def dscr(name, shape, dt=BF16):
        kind = "ExternalOutput" if debug else "Internal"
        return Tile(nc.dram_tensor(name, list(shape), dt, kind=kind).ap(), name)

    xs = din("xs", [LS, D])
    xp = din("xp", [NPS * LP, D])
    cT = din("cT", [128, 32])
    w_mod = din("w_mod", [D, 3 * D])
    b_mod = din("b_mod", [1, 3 * D])
    norm_w = din("norm_w", [1, D])
    w_in = din("w_in", [D, WIN_COLS])
    w_uq = din("w_uq", [384, 2048])
    w_uk = din("w_uk", [256, 1024])
    w_uv = din("w_uv", [256, 1024])
    w_out = din("w_out", [D, D])
    vecs_d = din("vecs", [128, NV])
    kvw_row = din("kvw_row", [1, 256])
    cache_ckv = din("cache_ckv", [PAST, 256])
    cache_kr = din("cache_kr", [PAST, 64])
    st_f = din("st_f", [RH, 128, 256])
    st_b = din("st_b", [RH, 128, 256])
    lgd = din("lgd", [1, 8])
    ident_d = din("ident", [128, 128])
    perm_d = din("permret", [128, 128])
    rettab_d = din("rettab", [128, NRT])
    mcos_d = din("mcos", [64, LS])
    msin_d = din("msin", [64, LS])
    rcos_d = din("rcos", [128, LS])
    rsin_d = din("rsin", [128, LS])
    y_s = dout("y_s", [LS, D])
    y_p = dout("y_p", [NPS * LP, D])
    o_ckv = dout("o_ckv", [NPS * LP, 256])
    o_kr = dout("o_kr", [NPS * LP, 64])
    o_sf = dout("o_sf", [NPS, RH, 128, 256])
    o_sb = dout("o_sb", [NPS, RH, 128, 256])
    WIN = dscr("WIN", [D, WIN_COLS])
    WUQ = dscr("WUQ", [384, 2048])
    WUK = dscr("WUK", [256, 1024])
    WUV = dscr("WUV", [256, 1024])
    WOUT = dscr("WOUT", [D, D])
    MODS = dscr("MODS", [6, 128, D], F32)
    QN = dscr("QN", [H, 128, NTOK])
    QR = dscr("QR", [H, 64, NTOK])
    KN = dscr("KN", [H, 128, NKEY])
    KR = dscr("KR", [H, 64, NKEY])
    VV = dscr("VV", [NKEY, 1024])
    GA = dscr("GA", [1024, NTOK])
    GB = dscr("GB", [1024, NTOK])
    RQ = dscr("RQ", [RH, 128, NTOK])
    RK = dscr("RK", [RH, 128, NTOK])
    RV = dscr("RV", [NTOK, 1024])
    MIX = dscr("MIX", [D, NTOK])

    st = ExitStack()
    arena = st.enter_context(nc.sbuf_tensor("arena", [128, ARENA_BYTES // 2], BF16))
    psh = [st.enter_context(nc.psum_tensor(f"ps{i}", [128, 512], F32)) for i in range(8)]
    psb = [Tile(h_.ap(), f"ps{i}", excl=True) for i, h_ in enumerate(psh)]
    psb_bf = [h_.bitcast(BF16).ap() for h_ in psh]

    bump = [0]

    def alloc(name, shape, dt=F32):
        esz = 4 if dt == F32 else 2
        n = 1
        for s in shape[1:]:
            n *= s
        nbytes = (n * esz + 63) // 64 * 64
        off = bump[0]
        bump[0] += nbytes
        assert bump[0] <= ARENA_BYTES, f"arena overflow at {name}: {bump[0]}"
        ap = arena[0:shape[0], off // 2: off // 2 + n * esz // 2]
        if dt == F32:
            ap = ap.bitcast(F32)
        if len(shape) == 3:
            ap = ap.rearrange("p (a b) -> p a b", a=shape[1])
        elif len(shape) == 4:
            ap = ap.rearrange("p (a b c) -> p a b c", a=shape[1], b=shape[2])
        return Tile(ap, name)

    rot = {}

    def bank(lo=0, hi=6, key='g'):
        k = (key, lo, hi)
        i = rot.get(k, lo)
        rot[k] = lo + (i - lo + 1) % (hi - lo)
        return psb[i]

    eng_rr = [0]

    identf = alloc("identf", [128, 128], F32)
    identb = alloc("identb", [128, 128], BF16)
    onesb = alloc("onesb", [128, 128], BF16)
    permb = alloc("permb", [128, 128], BF16)
    vecs = alloc("vecs", [128, NV], F32)
    epsc = alloc("epsc", [128, 1], F32)
    rtab = alloc("rtab", [128, NRT], F32)
    Mh = alloc("Mh", [128, RH, 128], F32)
    QDF = alloc("QDF", [128, RH, 128], F32)
    QDB = alloc("QDB", [128, RH, 128], F32)
    dcols = alloc("dcols", [128, 16], F32)
    nee = alloc("nee", [128, 8], F32)
    wuq = alloc("wuq", [128, 3, 2048], BF16)
    wuk = alloc("wuk", [128, 2, 1024], BF16)
    wuv = alloc("wuv", [128, 2, 1024], BF16)
    persist_mark = bump[0]

    P.dma(identf.v(), ident_d.v())
    P.dma(vecs.v(), vecs_d.v())
    P.dma(rtab.v(), rettab_d.v())
    P.cp('dve', identb.v(), identf.v())
    P.memset('pool', onesb.v(), 1.0)
    P.memset('pool', epsc.v(), EPS)

    stg_f = [alloc(f"stgf{i}", [128, 2048], F32) for i in range(7)]
    stg_b = [alloc(f"stgb{i}", [128, 2048], BF16) for i in range(7)]
    cvi = [0]

    def convert_pieces(src, dst, R, C):
        out = []
        for r in range(R // 128):
            c0 = 0
            while c0 < C:
                w = min(2048, C - c0)

                def piece(r=r, c0=c0, w=w):
                    n = len(stg_f)
                    i = cvi[0] % n
                    cvi[0] += 1
                    P.dma(stg_f[i][:, 0:w], src[r * 128:(r + 1) * 128, c0:c0 + w])
                    P.cp(['dve', 'act'][i % 2], stg_b[i][:, 0:w], stg_f[i][:, 0:w])
                    P.dma(dst[r * 128:(r + 1) * 128, c0:c0 + w], stg_b[i][:, 0:w])
                out.append(piece)
                c0 += w
        return out

    def convert(src, dst, R, C):
        for p_ in convert_pieces(src, dst, R, C):
            p_()

    P.dma(stg_f[0][:, 0:128], perm_d.v())
    P.cp('dve', permb.v(), stg_f[0][:, 0:128])

    pieces = []

    def add_pieces(src, dst, R, C):
        for r in range(R // 128):
            c0 = 0
            while c0 < C:
                w = min(2048, C - c0)
                pieces.append((src, dst, r, c0, w))
                c0 += w
    add_pieces(w_in, WIN, D, WIN_COLS)
    add_pieces(w_uq, WUQ, 384, 2048)
    add_pieces(w_uk, WUK, 256, 1024)
    add_pieces(w_uv, WUV, 256, 1024)
    NST = len(stg_f)
    ld_i = [0]
    cv_i = [0]

    def conv_load():
        if ld_i[0] < len(pieces):
            src, dst, r, c0, w = pieces[ld_i[0]]
            b_ = ld_i[0] % NST
            P.dma(stg_f[b_][:, 0:w], src[r * 128:(r + 1) * 128, c0:c0 + w])
            ld_i[0] += 1

    def conv_step():
        if cv_i[0] < len(pieces):
            src, dst, r, c0, w = pieces[cv_i[0]]
            b_ = cv_i[0] % NST
            P.cp(['dve', 'act'][cv_i[0] % 2], stg_b[b_][:, 0:w], stg_f[b_][:, 0:w])
            P.dma(dst[r * 128:(r + 1) * 128, c0:c0 + w], stg_b[b_][:, 0:w])
            cv_i[0] += 1
        conv_load()

    ct = alloc("ct", [128, 32], F32)
    sc = alloc("sc", [128, 32], F32)
    screp = alloc("screp", [128, 2, 16, 128], BF16)
    MW = 256
    NMB = 3 * D // MW
    wm = [alloc(f"wm{i}", [128, 16, MW], F32) for i in range(3)]
    wmb = [alloc(f"wmb{i}", [128, 16, MW], BF16) for i in range(2)]
    bm = [alloc(f"bm{i}", [128, MW], F32) for i in range(3)]
    nwb = [alloc(f"nwb{i}", [128, MW], F32) for i in range(3)]
    mo = [alloc(f"mo{i}", [128, MW], F32) for i in range(4)]
    P.dma(ct.v(), cT.v())
    P.act(sc.v(), ct.v(), AF.Silu)
    for v in range(2):
        P.cp('dve', screp[:, v], sc[:, v * 16:(v + 1) * 16].un(2).bc([128, 16, 128]))
    w_mod_v = w_mod.v().re("(c p) n -> p c n", p=128)

    def mod_load(cb):
        c0 = (cb * MW) % D
        P.dma(wm[cb % 3].v(), w_mod_v[:, :, cb * MW:(cb + 1) * MW])
        P.dma(bm[cb % 3].v(), View(b_mod.ap[0:1, cb * MW:(cb + 1) * MW].partition_broadcast(128), b_mod).re("p o n -> p (o n)"))
        if (cb * MW) // D == 1:
            P.dma(nwb[cb % 3].v(), View(norm_w.ap[0:1, c0:c0 + MW].partition_broadcast(128), norm_w).re("p o n -> p (o n)"))

    mod_load(0)
    mod_load(1)
    for _ in range(NST - 1):
        conv_load()
    moi = 0
    for cb in range(NMB):
        kind = (cb * MW) // D
        c0 = (cb * MW) % D
        if cb + 2 < NMB:
            mod_load(cb + 2)
        for _ in range(3):
            conv_step()
        wmt, bmt, nwt, wb_ = wm[cb % 3], bm[cb % 3], nwb[cb % 3], wmb[cb % 2]
        P.cp('dve', wb_[:, 0:8, :], wmt[:, 0:8, :])
        P.cp('act', wb_[:, 8:16, :], wmt[:, 8:16, :])
        for v in range(2):
            pb = bank()
            for c in range(16):
                P.mm(pb[:, 0:MW], screp[:, v, c, :], wb_[:, c, :], start=(c == 0), stop=(c == 15))
            m = mo[moi % 4]
            moi += 1
            P.tt('dve', m.v(), pb[:, 0:MW], bmt.v(), ALU.add)
            if kind == 1:
                P.stt('dve', m.v(), m.v(), 1.0, nwt.v(), ALU.add, ALU.mult)
            P.dma(MODS[kind * 2 + v][:, c0:c0 + MW], m.v())
    while cv_i[0] < len(pieces):
        conv_step()
    P.dma(wuq.v(), WUQ.v().re("(c p) n -> p c n", p=128))
    P.dma(wuk.v(), WUK.v().re("(c p) n -> p c n", p=128))
    P.dma(wuv.v(), WUV.v().re("(c p) n -> p c n", p=128))

    lgt = alloc("lgt", [128, 8], F32)
    ee = alloc("ee", [128, 8], F32)
    e1 = alloc("e1", [128, 128], F32)
    e2 = alloc("e2", [128, 128], F32)
    P.dma(lgt.v(), View(lgd.ap[0:1, :].partition_broadcast(128), lgd).re("p o n -> p (o n)"))
    P.act(ee.v(), lgt.v(), AF.Exp)
    P.ts('dve', nee.v(), ee.v(), -1.0, None, ALU.mult)
    for h in range(RH):
        nf = nee[:, h:h + 1]
        nb_ = nee[:, 4 + h:5 + h]
        P.act(e1.v(), rtab[:, RT_D1:RT_D1 + 128], AF.Exp, scale=nf)
        P.act(e2.v(), rtab[:, RT_D2:RT_D2 + 128], AF.Exp, scale=nb_)
        P.tt('dve', Mh[:, h, :], e1.v(), e2.v(), ALU.mult)
        P.act(QDF[:, h, :], rtab[:, RT_QDF:RT_QDF + 128], AF.Exp, scale=nf)
        P.act(QDB[:, h, :], rtab[:, RT_QDB:RT_QDB + 128], AF.Exp, scale=nb_)
        P.act(dcols[:, h:h + 1], rtab[:, RT_KDF:RT_KDF + 1], AF.Exp, scale=nf)
        P.act(dcols[:, 4 + h:5 + h], rtab[:, RT_KDB:RT_KDB + 1], AF.Exp, scale=nb_)
        P.act(dcols[:, 8 + h:9 + h], rtab[:, RT_C:RT_C + 1], AF.Exp, scale=nf)
        P.act(dcols[:, 12 + h:13 + h], rtab[:, RT_C:RT_C + 1], AF.Exp, scale=nb_)

    P.barrier()
    bump[0] = persist_mark
    if stop_after == 'setup':
        P.emit()
        st.close()
        return nc

    xt = [alloc(f"xt{i}", [128, D], F32) for i in range(2)]
    hn = [alloc(f"hn{i}", [128, D], BF16) for i in range(2)]
    hTs = [alloc(f"hT{i}", [128, 16, TB], BF16) for i in range(2)]
    wsl = [alloc(f"wsl{i}", [128, 16, 512], BF16) for i in range(2)]
    Acur = alloc("Acur", [128, D], F32)
    Bcur = alloc("Bcur", [128, D], F32)
    qraw = alloc("qraw", [128, 3, TB], F32)
    ckraw = alloc("ckraw", [128, 2, TB], F32)
    krraw = alloc("krraw", [64, TB], F32)
    kpraw = alloc("kpraw", [64, TB], F32)
    krot = alloc("krot", [64, TB], F32)
    kp2 = alloc("kp2", [64, TB], F32)
    sq = [alloc(f"sq{i}", [128, TB], BF16) for i in range(4)]
    sqkr = alloc("sqkr", [64, TB], BF16)
    msq = [alloc(f"msq{i}", [128, TB], F32) for i in range(2)]
    ff = [alloc(f"ff{i}", [128, TB], F32) for i in range(3)]
    qln = alloc("qln", [128, 3, TB], BF16)
    ckvn = alloc("ckvn", [128, 2, TB], BF16)
    ob = [alloc(f"ob{i}", [128, TB], BF16) for i in range(6)]
    t1 = [alloc(f"t1{i}", [128, TB], F32) for i in range(2)]
    t2 = [alloc(f"t2{i}", [128, TB], F32) for i in range(2)]
    mc = alloc("mc", [64, TB], F32)
    msn = alloc("msn", [64, TB], F32)
    rc = alloc("rc", [128, TB], F32)
    rs = alloc("rs", [128, TB], F32)
    vt = [alloc(f"vt{i}", [128, 1024], BF16) for i in range(2)]
    rawb = [alloc(f"rawb{i}", [128, TB], BF16) for i in range(2)]
    rvt = [alloc(f"rvt{i}", [128, 512], BF16) for i in range(2)]
    ckout = [alloc(f"ckout{i}", [128, 320], F32) for i in range(2)]
    kvwbc = alloc("kvwbc", [128, 256], F32)
    ssc = alloc("ssc", [128, 8], F32)
    ctx_in = ckout

    cnt = {'sq': 0, 'ms': 0, 'ff': 0, 'ob': 0, 't': 0, 'vt': 0, 'rawb': 0, 'rvt': 0, 'ck': 0, 'ssc': 0}

    def nxt(lst, key):
        i = cnt[key]
        cnt[key] += 1
        return lst[i % len(lst)]

    P.dma(kvwbc.v(), View(kvw_row.ap[0:1, :].partition_broadcast(128), kvw_row).re("p o n -> p (o n)"))

    def vcol(c, np_=128):
        return vecs[0:np_, c:c + 1]

    def rms_bc(sqs, nfeat, nt):
        sb = bank(6, 8, 's')
        for i, (v, K) in enumerate(sqs):
            P.mm(sb[:, 0:nt], onesb[0:K, :], v, start=(i == 0), stop=(i == len(sqs) - 1))
        ms = nxt(msq, 'ms')
        P.act(ms[:, 0:nt], sb[:, 0:nt], AF.Ln, scale=1.0 / nfeat, bias=epsc[:, 0:1])
        f = nxt(ff, 'ff')
        P.act(f[:, 0:nt], ms[:, 0:nt], AF.Exp, scale=-0.5)
        return f

    def kv_heads(nt, rope, koff):
        if rope:
            P.stt('dve', krot[0:64, 0:nt], krraw[0:64, 0:nt], vcol(V_WKR, 64), mc[0:64, 0:nt], ALU.mult, ALU.mult)
            P.stt('dve', kp2[0:64, 0:nt], kpraw[0:64, 0:nt], vcol(V_WKP, 64), msn[0:64, 0:nt], ALU.mult, ALU.mult)
            P.tt('pool', krot[0:64, 0:nt], krot[0:64, 0:nt], kp2[0:64, 0:nt], ALU.add)
        else:
            P.ts('dve', krot[0:64, 0:nt], krraw[0:64, 0:nt], vcol(V_WKR, 64), None, ALU.mult)
        def k_stage1(h):
            bn = bank()
            for c in range(2):
                P.mm(bn[:, 0:nt], wuk[:, c, h * 128:(h + 1) * 128], ckvn[:, c, 0:nt], start=(c == 0), stop=(c == 1))
            sqa = nxt(sq, 'sq')
            P.act(sqa[:, 0:nt], bn[:, 0:nt], AF.Square)
            return (h, bn, sqa)

        def k_stage2(ctx):
            h, bn, sqa = ctx
            f = rms_bc([(sqa[:, 0:nt], 128), (sqkr[0:64, 0:nt], 64)], 192, nt)
            o1 = nxt(ob, 'ob')
            P.stt('dve', o1[:, 0:nt], bn[:, 0:nt], vcol(V_WKN), f[:, 0:nt], ALU.mult, ALU.mult)
            P.dma(KN[h][:, koff:koff + nt], o1[:, 0:nt])
            o2 = nxt(ob, 'ob')
            P.tt('pool', o2[0:64, 0:nt], krot[0:64, 0:nt], f[0:64, 0:nt], ALU.mult)
            P.dma(KR[h][:, koff:koff + nt], o2[0:64, 0:nt])

        prev = None
        for h in range(H):
            cur = k_stage1(h)
            if prev is not None:
                k_stage2(prev)
            prev = cur
        k_stage2(prev)
        for t in range(nt // 128):
            vtt = nxt(vt, 'vt')
            for cb in range(2):
                pb = bank()
                for c in range(2):
                    P.mm(pb.v(), ckvn[:, c, t * 128:(t + 1) * 128], wuv[:, c, cb * 512:(cb + 1) * 512],
                         start=(c == 0), stop=(c == 1))
                P.cp('act', vtt[:, cb * 512:(cb + 1) * 512], pb.v())
            P.dma(VV[koff + t * 128: koff + (t + 1) * 128, :], vtt.v())

    WINv = WIN.v().re("(c p) n -> p c n", p=128)
    slab_i = [0]

    def load_slab(src_v, off, w):
        ws = wsl[slab_i[0] % 2]
        slab_i[0] += 1
        P.dma(ws[:, :, 0:w], src_v[:, :, off:off + w])
        return ws

    work = [(b_, s_) for b_ in range(NBLK) for s_ in range(10 if b_ < NSB else 11)]
    slab_q = []
    work_i = [0]

    def issue_next_slab():
        if work_i[0] < len(work):
            off_, w_ = SLABS[work[work_i[0]][1]]
            slab_q.append(load_slab(WINv, off_, w_))
            work_i[0] += 1

    def blk_info(b):
        sample = b < NSB
        g0 = b * TB
        return sample, g0, (xs if sample else xp), (g0 if sample else g0 - LS)

    def prepA(b, t):
        sample, g0, xsrc, x0 = blk_info(b)
        if t == 0 and (b == 0 or b == NSB):
            v = 0 if sample else 1
            P.dma(Bcur.v(), MODS[0 * 2 + v])
            P.dma(Acur.v(), MODS[1 * 2 + v])
        x = xt[t % 2]
        hh = hn[t % 2]
        P.dma(x.v(), xsrc[x0 + t * 128: x0 + (t + 1) * 128, :])
        s1 = ssc[:, (cnt['ssc'] % 8):(cnt['ssc'] % 8) + 1]
        cnt['ssc'] += 1
        P.act(hh.v(), x.v(), AF.Square, accum=s1)
        P.act(s1, s1, AF.Ln, scale=1.0 / D, bias=epsc[:, 0:1])
        P.act(s1, s1, AF.Exp, scale=-0.5)
        P.stt('dve', x.v(), x.v(), s1, Acur.v(), ALU.mult, ALU.mult)
        P.tt('pool', hh.v(), x.v(), Bcur.v(), ALU.add)

    def prepB(b, t):
        hh = hn[t % 2]
        hT = hTs[b % 2]
        for half in range(2):
            pbt = bank()
            pbv = View(psb_bf[psb.index(pbt)], pbt)
            for c8 in range(8):
                c = half * 8 + c8
                P.tr(pbv[:, c8 * 128:(c8 + 1) * 128], hh[:, c * 128:(c + 1) * 128], identb.v())
            src = pbv.re("p (c n) -> p c n", c=8)
            dst = hT[:, half * 8:(half + 1) * 8, t * 128:(t + 1) * 128]
            if half == 0:
                P.cp('act', dst, src)
            else:
                P.cp('dve', dst, src)

    def phase1_block(b, nb):
        sample, g0, xsrc, x0 = blk_info(b)
        koff = g0 if sample else g0 + PAST
        hT = hTs[b % 2]
        if sample:
            P.dma(mc.v(), mcos_d[:, g0:g0 + TB])
            P.dma(msn.v(), msin_d[:, g0:g0 + TB])
            P.dma(rc.v(), rcos_d[:, g0:g0 + TB])
            P.dma(rs.v(), rsin_d[:, g0:g0 + TB])
        nsl = 10 if sample else 11
        nsl = min(nsl, LIMITS.get('p1_slabs', nsl))
        for s in range(nsl):
            off, w = SLABS[s]
            ws = slab_q.pop(0)
            issue_next_slab()

            def group(c0, M, ws=ws):
                pb = bank()
                for c in range(16):
                    P.mm(pb[0:M, :], ws[:, c, c0:c0 + M], hT[:, c, :], start=(c == 0), stop=(c == 15))
                return pb

            if s == 0:
                sqs = []
                for g in range(3):
                    pb = group(g * 128, 128)
                    sqa = nxt(sq, 'sq')
                    P.act(sqa.v(), pb.v(), AF.Square)
                    P.cp('dve', qraw[:, g, :], pb.v())
                    sqs.append((sqa.v(), 128))
                if LIMITS.get('s0_stage', 9) < 2:
                    continue
                pb = group(384, 64)
                P.act(sqkr.v(), pb[0:64, :], AF.Square)
                P.cp('dve', krraw.v(), pb[0:64, :])
                if sample:
                    pb = group(448, 64)
                    P.cp('dve', kpraw.v(), pb[0:64, :])
                if LIMITS.get('s0_stage', 9) < 3:
                    continue
                f = rms_bc(sqs, 384, TB)
                for g in range(3):
                    P.stt('dve', qln[:, g, :], qraw[:, g, :], vcol(V_QNW + g), f.v(), ALU.mult, ALU.mult)
                if LIMITS.get('s0_stage', 9) < 4:
                    continue
                def q_stage1(h):
                    bn = bank()
                    for c in range(3):
                        P.mm(bn.v(), wuq[:, c, h * 256:h * 256 + 128], qln[:, c, :], start=(c == 0), stop=(c == 2))
                    br = bank()
                    for c in range(3):
                        P.mm(br[0:64, :], wuq[:, c, h * 256 + 128:h * 256 + 192], qln[:, c, :], start=(c == 0), stop=(c == 2))
                    bp = None
                    if sample:
                        bp = bank()
                        for c in range(3):
                            P.mm(bp[0:64, :], wuq[:, c, h * 256 + 192:h * 256 + 256], qln[:, c, :], start=(c == 0), stop=(c == 2))
                    sqa = nxt(sq, 'sq')
                    sqb = nxt(sq, 'sq')
                    P.act(sqa.v(), bn.v(), AF.Square)
                    P.act(sqb[0:64, :], br[0:64, :], AF.Square)
                    return (h, bn, br, bp, sqa, sqb)

                def q_stage2(ctx):
                    h, bn, br, bp, sqa, sqb = ctx
                    f = rms_bc([(sqa.v(), 128), (sqb[0:64, :], 64)], 192, TB)
                    o1 = nxt(ob, 'ob')
                    P.stt('dve', o1.v(), bn.v(), vcol(V_WQN), f.v(), ALU.mult, ALU.mult)
                    P.dma(QN[h][:, g0:g0 + TB], o1.v())
                    o2 = nxt(ob, 'ob')
                    if not sample:
                        P.stt('dve', o2[0:64, :], br[0:64, :], vcol(V_WQR, 64), f[0:64, :], ALU.mult, ALU.mult)
                    else:
                        a = nxt(t1, 't')
                        b_ = t2[(cnt['t'] - 1) % 2]
                        P.stt('dve', a[0:64, :], br[0:64, :], vcol(V_WQR, 64), mc.v(), ALU.mult, ALU.mult)
                        P.stt('dve', b_[0:64, :], bp[0:64, :], vcol(V_WQP, 64), msn.v(), ALU.mult, ALU.mult)
                        P.tt('pool', a[0:64, :], a[0:64, :], b_[0:64, :], ALU.add)
                        P.tt('pool', o2[0:64, :], a[0:64, :], f[0:64, :], ALU.mult)
                    P.dma(QR[h][:, g0:g0 + TB], o2[0:64, :])

                for h in range(H):
                    q_stage2(q_stage1(h))
            elif s == 1:
                sqs = []
                for g in range(2):
                    pb = group(g * 128, 128)
                    sqa = nxt(sq, 'sq')
                    P.act(sqa.v(), pb.v(), AF.Square)
                    P.cp('dve', ckraw[:, g, :], pb.v())
                    sqs.append((sqa.v(), 128))
                f = rms_bc(sqs, 256, TB)
                for g in range(2):
                    P.stt('dve', ckvn[:, g, :], ckraw[:, g, :], vcol(V_KVW + g), f.v(), ALU.mult, ALU.mult)
                kv_heads(TB, sample, koff)
            elif s in (2, 3, 6, 7):
                dstT = GA if s in (2, 3) else GB
                r0 = (s % 2) * 512
                for g in range(4):
                    pb = group(g * 128, 128)
                    o1 = nxt(ob, 'ob')
                    P.act(o1.v(), pb.v(), AF.Silu)
                    P.dma(dstT[r0 + g * 128: r0 + (g + 1) * 128, g0:g0 + TB], o1.v())
            elif s in (4, 5):
                dstT = RQ if s == 4 else RK
                scl = None if s == 4 else 128.0 ** -0.5
                for g in range(4):
                    pb = group(g * 128, 128)
                    o1 = nxt(ob, 'ob')
                    if not sample:
                        P.cp('act', o1.v(), pb.v(), scale=scl)
                    else:
                        rb = nxt(rawb, 'rawb')
                        P.cp('act', rb.v(), pb.v(), scale=scl)
                        pp = bank()
                        P.mm(pp.v(), permb.v(), rb.v())
                        a = nxt(t1, 't')
                        b_ = t2[(cnt['t'] - 1) % 2]
                        P.tt('pool', a.v(), rb.v(), rc.v(), ALU.mult)
                        P.tt('dve', b_.v(), pp.v(), rs.v(), ALU.mult)
                        P.tt('pool', o1.v(), a.v(), b_.v(), ALU.add)
                    P.dma(dstT[g][:, g0:g0 + TB], o1.v())
            elif s in (8, 9):
                for t in range(4):
                    pb = bank()
                    for c in range(16):
                        P.mm(pb.v(), hT[:, c, t * 128:(t + 1) * 128], ws[:, c, 0:512], start=(c == 0), stop=(c == 15))
                    r = nxt(rvt, 'rvt')
                    P.cp('act' if t % 2 == 0 else 'dve', r.v(), pb.v())
                    P.dma(RV[g0 + t * 128: g0 + (t + 1) * 128, (s - 8) * 512:(s - 7) * 512], r.v())
            else:
                for t in range(4):
                    pb = bank()
                    for c in range(16):
                        P.mm(pb[:, 0:320], hT[:, c, t * 128:(t + 1) * 128], ws[:, c, 0:320], start=(c == 0), stop=(c == 15))
                    s1 = ssc[:, (cnt['ssc'] % 8):(cnt['ssc'] % 8) + 1]
                    cnt['ssc'] += 1
                    oj = nxt(ob, 'ob')
                    P.act(oj[:, 0:256], pb[:, 0:256], AF.Square, accum=s1)
                    P.act(s1, s1, AF.Ln, scale=1.0 / 256, bias=epsc[:, 0:1])
                    P.act(s1, s1, AF.Exp, scale=-0.5)
                    co = nxt(ckout, 'ck')
                    P.stt('dve', co[:, 0:256], pb[:, 0:256], s1, kvwbc.v(), ALU.mult, ALU.mult)
                    P.cp('act', co[:, 256:320], pb[:, 256:320])
                    r0 = x0 + t * 128
                    P.dma(o_ckv[r0:r0 + 128, :], co[:, 0:256])
                    P.dma(o_kr[r0:r0 + 128, :], co[:, 256:320])
            if nb is not None:
                if 2 <= s <= 5:
                    prepA(nb, s - 2)
                if 3 <= s <= 6:
                    prepB(nb, s - 3)

    def ctx_keys():
        for t in range(2):
            ci = ctx_in[t]
            P.dma(ci[:, 0:256], cache_ckv[t * 128:(t + 1) * 128, :])
            P.dma(ci[:, 256:320], cache_kr[t * 128:(t + 1) * 128, :])
            for c in range(2):
                pb = bank()
                P.tr(pb[:, 0:128], ci[:, c * 128:(c + 1) * 128], identf.v())
                P.cp('dve', ckvn[:, c, t * 128:(t + 1) * 128], pb[:, 0:128])
            pb = bank()
            P.tr(pb[0:64, 0:128], ci[:, 256:320], identf.v())
            P.cp('dve', krraw[0:64, t * 128:(t + 1) * 128], pb[0:64, 0:128])
            P.act(sqkr[0:64, t * 128:(t + 1) * 128], pb[0:64, 0:128], AF.Square)
        kv_heads(256, False, LS)

    p1_list = LIMITS.get('p1_blocks', list(range(NBLK)))
    work[:] = [(b_, s_) for (b_, s_) in work if b_ in p1_list and s_ < LIMITS.get('p1_slabs', 99)]
    issue_next_slab()
    for t in range(4):
        prepA(p1_list[0], t)
        prepB(p1_list[0], t)
    for bi, b in enumerate(p1_list):
        phase1_block(b, p1_list[bi + 1] if bi + 1 < len(p1_list) else None)
        if b == NSB - 1:
            ctx_keys()

    P.barrier()
    bump[0] = persist_mark
    if stop_after == 'p1':
        P.emit()
        st.close()
        return nc

    LKMAX = LS + PAST
    stg_f = [alloc(f"stgf{i}", [128, 2048], F32) for i in range(3)]
    stg_b = [alloc(f"stgb{i}", [128, 2048], BF16) for i in range(3)]
    wout_pieces = convert_pieces(w_out, WOUT, D, D)
    kvs = [dict(kn=alloc(f"kn_sb{j}", [128, LKMAX], BF16), kr=alloc(f"kr_sb{j}", [128, LKMAX], BF16),
                v=alloc(f"v_sb{j}", [128, LKMAX // 128, 128], BF16)) for j in range(2)]
    qn_sb = [alloc(f"qn_sb{i}", [128, 512], BF16) for i in range(2)]
    qr_sb = [alloc(f"qr_sb{i}", [128, 512], BF16) for i in range(2)]
    pT = [alloc(f"pT{i}", [128, 512], BF16) for i in range(4)]
    acc = [[alloc(f"acc{i}_{j}", [128, 512], F32) for j in range(2)] for i in range(2)]
    onesf = alloc("onesf", [128, 128], F32)
    rD = alloc("rD", [128, 512], F32)
    on = alloc("on", [128, 512], F32)
    ga_sb = [alloc(f"ga_sb{i}", [128, 512], BF16) for i in range(3)]
    mixo = [alloc(f"mixo{i}", [128, 512], BF16) for i in range(2)]
    SCALE = 192.0 ** -0.5
    P.memset('pool', onesf.v(), 1.0)
    for j in range(2):
        P.memset('pool', kvs[j]['kr'][64:128, :], 0.0)
        P.memset('pool', qr_sb[j][64:128, :], 0.0)

    seqs = [(0, LS, 0, LS + PAST, 512)] + [(LS + s_ * LP, LP, LS + PAST + s_ * LP, LP, 256) for s_ in range(NPS)]
    units = []
    for (q0, Lq, k0, Lk, QB) in seqs:
        for h in range(H):
            for qb in range(Lq // QB):
                units.append((q0, Lq, k0, Lk, QB, h, qb))
    hc = [0]
    kvmap = {}

    def issue_loads(idx):
        q0, Lq, k0, Lk, QB, h, qb = units[idx]
        if qb == 0:
            kv = kvs[hc[0] % 2]
            hc[0] += 1
            kvmap[(q0, h)] = kv
            nkt = Lk // 128
            P.dma(kv['kn'][:, 0:Lk], KN[h][:, k0:k0 + Lk])
            P.dma(kv['kr'][0:64, 0:Lk], KR[h][:, k0:k0 + Lk])
            P.dma(kv['v'][:, 0:nkt, :], VV[k0:k0 + Lk, h * 128:(h + 1) * 128].re("(t p) e -> p t e", p=128))
        i = idx % 2
        qs = q0 + qb * QB
        P.dma(qn_sb[i][:, 0:QB], QN[h][:, qs:qs + QB])
        P.dma(qr_sb[i][0:64, 0:QB], QR[h][:, qs:qs + QB])
        P.dma(ga_sb[idx % 3][:, 0:QB], GA[h * 128:(h + 1) * 128, qs:qs + QB])

    def make_epilogue(idx):
        q0, Lq, k0, Lk, QB, h, qb = units[idx]
        i = idx % 2
        qs = q0 + qb * QB
        Ob = psb[3 + i]
        Db = psb[5 + i]

        def epi():
            P.mm(Db[:, 0:QB], onesf.v(), acc[i][0][:, 0:QB], start=(Lk // 128 < 8), stop=False)
            P.mm(Db[:, 0:QB], onesf.v(), acc[i][1][:, 0:QB], start=False, stop=True)
            P.recip(rD[:, 0:QB], Db[:, 0:QB])
            P.tt('dve', on[:, 0:QB], Ob[:, 0:QB], rD[:, 0:QB], ALU.mult)
            P.tt('pool', mixo[i][:, 0:QB], on[:, 0:QB], ga_sb[idx % 3][:, 0:QB], ALU.mult)
            P.dma(MIX[h * 128:(h + 1) * 128, qs:qs + QB], mixo[i][:, 0:QB])
        return epi

    issue_loads(0)
    pending = [None]
    for idx, u in enumerate(units):
        q0, Lq, k0, Lk, QB, h, qb = u
        nkt = Lk // 128
        i = idx % 2
        kv = kvmap[(q0, h)]
        if idx + 1 < len(units):
            issue_loads(idx + 1)
        if wout_pieces:
            wout_pieces.pop(0)()
        Ob = psb[3 + i]

        SB_ = [0, 1, 2, 7]

        def smm(kt, kv=kv, i=i, QB=QB):
            sb_ = psb[SB_[kt % 4]]
            P.mm(sb_[:, 0:QB], kv['kn'][:, kt * 128:(kt + 1) * 128], qn_sb[i][:, 0:QB], start=True, stop=False)
            P.mm(sb_[:, 0:QB], kv['kr'][:, kt * 128:(kt + 1) * 128], qr_sb[i][:, 0:QB], start=False, stop=True)
        LA = 2
        for k0_ in range(min(LA, nkt)):
            smm(k0_)
        dstart = [False]
        for kt in range(nkt):
            if kt + LA < nkt:
                smm(kt + LA)
            p = pT[kt % 4]
            P.act(p[:, 0:QB], psb[SB_[kt % 4]][:, 0:QB], AF.Exp, scale=SCALE)
            P.mm(Ob[:, 0:QB], kv['v'][:, kt, :], p[:, 0:QB], start=(kt == 0), stop=(kt == nkt - 1))
            if kt % 8 == 7:
                P.mm(psb[5 + i][:, 0:QB], onesb.v(), p[:, 0:QB], start=(not dstart[0]), stop=False)
                dstart[0] = True
            else:
                a_ = acc[i][kt % 2]
                if kt < 2:
                    P.cp('dve', a_[:, 0:QB], p[:, 0:QB])
                else:
                    P.tt('dve', a_[:, 0:QB], a_[:, 0:QB], p[:, 0:QB], ALU.add)
            if kt == min(3, nkt - 1) and pending[0] is not None:
                pending[0]()
                pending[0] = None
        pending[0] = make_epilogue(idx)
    pending[0]()
    while wout_pieces:
        wout_pieces.pop(0)()

    P.barrier()
    bump[0] = persist_mark
    if stop_after == 'attn':
        P.emit()
        st.close()
        return nc

    NCMAX = LS // 128
    rset = [dict(rk=alloc(f"rk_sb{j}", [128, LS], BF16), rq=alloc(f"rq_sb{j}", [128, LS], BF16),
                 rv=alloc(f"rv_sb{j}", [128, NCMAX, 256], BF16)) for j in range(2)]
    KDF = alloc("KDF", [128, NCMAX, 128], BF16)
    KDB = alloc("KDB", [128, NCMAX, 128], BF16)
    snapF = alloc("snapF", [128, NCMAX, 256], BF16)
    snapB = alloc("snapB", [128, NCMAX, 256], BF16)
    SfP = [alloc(f"SfP{i}", [128, 256], F32) for i in range(2)]
    SbP = [alloc(f"SbP{i}", [128, 256], F32) for i in range(2)]
    Sx = [alloc(f"Sx{i}", [128, 256], F32) for i in range(4 * (NPS - 1))]
    smt = [alloc(f"smt{i}", [128, 128], BF16) for i in range(3)]
    qd = [alloc(f"qd{i}", [128, 2, 512], BF16) for i in range(2)]
    rsq = [alloc(f"rsq{i}", [128, 512], BF16) for i in range(4)]
    rms_ = [alloc(f"rms{i}", [128, 512], F32) for i in range(2)]
    rff = [alloc(f"rff{i}", [128, 512], F32) for i in range(2)]
    rno = [alloc(f"rno{i}", [128, 512], F32) for i in range(4)]
    gb_sb = [alloc(f"gb_sb{i}", [128, 2, 512], BF16) for i in range(2)]
    rmo = [alloc(f"rmo{i}", [128, 512], BF16) for i in range(4)]
    rc_ = {'sm': 0, 'g': 0, 'rsq': 0, 'rmo': 0}

    rseqs = [(0, LS, True, 1), (LS, NPS * LP, False, NPS)]
    runits = [(q0, L, smp, nsq, h) for (q0, L, smp, nsq) in rseqs for h in range(RH)]

    def ret_loads(u):
        q0, L, smp, sidx, h = runits[u]
        n = L // 128
        bf_ = rset[u % 2]
        P.dma(bf_['rk'][:, 0:L], RK[h][:, q0:q0 + L])
        P.dma(bf_['rv'][:, 0:n, :], RV[q0:q0 + L, h * 256:(h + 1) * 256].re("(t p) e -> p t e", p=128))
        P.dma(bf_['rq'][:, 0:L], RQ[h][:, q0:q0 + L])

    def ret_unit(u):
        q0, L, sample, nseq, h = runits[u]
        n = L // 128
        bf_ = rset[u % 2]
        rk_sb, rq_sb, rv_sb = bf_['rk'], bf_['rq'], bf_['rv']
        kdf = dcols[:, h:h + 1]
        kdb = dcols[:, 4 + h:5 + h]
        cdf = dcols[:, 8 + h:9 + h]
        cdb = dcols[:, 12 + h:13 + h]
        nsub = n // nseq
        SfC = [[SfP[0], SfP[1]]] + [[Sx[4 * j_], Sx[4 * j_ + 1]] for j_ in range(nseq - 1)]
        SbC = [[SbP[0], SbP[1]]] + [[Sx[4 * j_ + 2], Sx[4 * j_ + 3]] for j_ in range(nseq - 1)]
        for j_ in range(nseq):
            if sample:
                P.dma(SfC[j_][0].v(), st_f[h])
                P.dma(SbC[j_][0].v(), st_b[h])
            else:
                P.memset('pool', SfC[j_][0].v(), 0.0)
                P.memset('pool', SbC[j_][0].v(), 0.0)
        if u + 1 < len(runits):
            ret_loads(u + 1)
        for g8 in range((n + 7) // 8):
            nb = min(8, n - g8 * 8)
            bi = 7 if g8 % 2 == 0 else 0
            pbv = View(psb_bf[bi], psb[bi])
            for j in range(nb):
                c = g8 * 8 + j
                P.tr(pbv[:, j * 128:(j + 1) * 128], rk_sb[:, c * 128:(c + 1) * 128], identb.v())
            src = pbv[:, 0:nb * 128].re("p (c d) -> p c d", d=128)
            P.ts('dve', KDF[:, g8 * 8:g8 * 8 + nb, :], src, kdf, None, ALU.mult)
            P.ts('dve', KDB[:, g8 * 8:g8 * 8 + nb, :], src, kdb, None, ALU.mult)
        ui = 0
        for k in range(nsub):
            for j_ in range(nseq):
                cf, cb_ = j_ * nsub + k, j_ * nsub + (nsub - 1 - k)
                Sf_cur, Sf_n = SfC[j_][k % 2], SfC[j_][(k + 1) % 2]
                Sb_cur, Sb_n = SbC[j_][k % 2], SbC[j_][(k + 1) % 2]
                P.cp('act', snapF[:, cf, :], Sf_cur.v())
                upf = psb[3 + ui % 2]
                P.mm(upf[:, 0:256], KDF[:, cf, :], rv_sb[:, cf, :])
                P.stt('dve', Sf_n.v(), Sf_cur.v(), cdf, upf[:, 0:256], ALU.mult, ALU.add)
                P.cp('act', snapB[:, cb_, :], Sb_cur.v())
                upb = psb[5 + ui % 2]
                P.mm(upb[:, 0:256], KDB[:, cb_, :], rv_sb[:, cb_, :])
                P.stt('dve', Sb_n.v(), Sb_cur.v(), cdb, upb[:, 0:256], ALU.mult, ALU.add)
                ui += 1
        if not sample:
            for j_ in range(nseq):
                P.dma(o_sf[j_][h], SfC[j_][nsub % 2].v())
                P.dma(o_sb[j_][h], SbC[j_][nsub % 2].v())
        ngrp = (n + 3) // 4
        gbase = rc_['g']
        rc_['g'] += ngrp

        def prep_group(g):
            c_lo = g * 4
            nc_ = min(4, n - c_lo)
            W = nc_ * 128
            gi = (gbase + g) % 2
            tok0 = q0 + c_lo * 128
            qv = rq_sb[:, c_lo * 128: c_lo * 128 + W].re("p (c i) -> p c i", i=128)
            P.tt('dve', qd[gi][:, 0, 0:W].re("p (c i) -> p c i", i=128), qv,
                 QDF[:, h, :].un(1).bc([128, nc_, 128]), ALU.mult)
            P.tt('pool', qd[gi][:, 1, 0:W].re("p (c i) -> p c i", i=128), qv,
                 QDB[:, h, :].un(1).bc([128, nc_, 128]), ALU.mult)
            P.dma(gb_sb[gi][:, :, 0:W], GB[h * 256:(h + 1) * 256, tok0:tok0 + W].re("(e p) t -> p e t", p=128))

        prep_group(0)
        for g in range(ngrp):
            c_lo = g * 4
            nc_ = min(4, n - c_lo)
            W = nc_ * 128
            gi = (gbase + g) % 2
            tok0 = q0 + c_lo * 128
            if g + 1 < ngrp:
                prep_group(g + 1)
            OA = psb[1 + 2 * gi]
            OB = psb[2 + 2 * gi]
            def st_mm(c):
                cs = slice(c * 128, (c + 1) * 128)
                sp_ = psb[7 if c % 2 == 0 else 0]
                P.mm(sp_[:, 0:128], rk_sb[:, cs], rq_sb[:, cs])
                sm_ = smt[c % 3]
                P.tt('dve', sm_.v(), sp_[:, 0:128], Mh[:, h, :], ALU.mult)

            if g == 0:
                st_mm(0)
            for ci in range(nc_):
                c = c_lo + ci
                if c + 1 < n:
                    st_mm(c + 1)
                sm = smt[c % 3]
                for e2, Ot in enumerate((OA, OB)):
                    es = slice(e2 * 128, (e2 + 1) * 128)
                    oc = Ot[:, ci * 128:(ci + 1) * 128]
                    P.mm(oc, rv_sb[:, c, es], sm.v(), start=True, stop=False)
                    P.mm(oc, snapF[:, c, es], qd[gi][:, 0, ci * 128:(ci + 1) * 128], start=False, stop=False)
                    P.mm(oc, snapB[:, c, es], qd[gi][:, 1, ci * 128:(ci + 1) * 128], start=False, stop=True)
            osb = [rno[2 * gi], rno[2 * gi + 1]]
            P.cp('act', osb[0][:, 0:W], OA[:, 0:W])
            P.cp('act', osb[1][:, 0:W], OB[:, 0:W])
            sqs = []
            for e2 in range(2):
                r = rsq[rc_['rsq'] % 4]
                rc_['rsq'] += 1
                P.act(r[:, 0:W], osb[e2][:, 0:W], AF.Square)
                sqs.append(r)
            sb_ = psb[5 + gi]
            P.mm(sb_[:, 0:W], onesb.v(), sqs[0][:, 0:W], start=True, stop=False)
            P.mm(sb_[:, 0:W], onesb.v(), sqs[1][:, 0:W], start=False, stop=True)
            ms = rms_[gi]
            P.act(ms[:, 0:W], sb_[:, 0:W], AF.Ln, scale=1.0 / 256, bias=epsc[:, 0:1])
            f = rff[gi]
            P.act(f[:, 0:W], ms[:, 0:W], AF.Exp, scale=-0.5)
            for e2 in range(2):
                no = osb[e2]
                P.stt('dve', no[:, 0:W], no[:, 0:W], vcol(V_GNW + h * 2 + e2), f[:, 0:W], ALU.mult, ALU.mult)
                mo_ = rmo[rc_['rmo'] % 4]
                rc_['rmo'] += 1
                P.tt('pool', mo_[:, 0:W], no[:, 0:W], gb_sb[gi][:, e2, 0:W], ALU.mult)
                r0 = 1024 + h * 256 + e2 * 128
                P.dma(MIX[r0:r0 + 128, tok0:tok0 + W], mo_[:, 0:W])

    ret_loads(0)
    for u in range(len(runits)):
        ret_unit(u)

    P.barrier()
    bump[0] = persist_mark
    if stop_after == 'ret':
        P.emit()
        st.close()
        return nc

    mx = alloc("mx", [128, 16, TB], BF16)
    Gcur = alloc("Gcur", [128, D], F32)
    xo = [alloc(f"xo{i}", [128, 4, 512], F32) for i in range(2)]
    yt_ = [alloc(f"yt{i}", [128, 512], F32) for i in range(3)]
    WOv = WOUT.v().re("(c p) n -> p c n", p=128)
    MIXv = MIX.v().re("(c p) t -> p c t", p=128)
    yi = [0]
    steps = [(b, db) for b in range(NBLK) for db in range(4)]
    mxs = [mx, alloc("mx2", [128, 16, TB], BF16)]

    woq = [alloc(f"woq{i}", [128, 16, 512], BF16) for i in range(4)]

    def p3_mx_load(b):
        P.dma(mxs[b % 2].v(), MIXv[:, :, b * TB:(b + 1) * TB])

    def p3_loads(k):
        b, db = steps[k]
        sample = b < NSB
        g0 = b * TB
        xsrc = xs if sample else xp
        x0 = g0 if sample else g0 - LS
        P.dma(xo[k % 2].v(), xsrc[x0:x0 + TB, db * 512:(db + 1) * 512].re("(t p) n -> p t n", p=128))

    P.dma(Gcur.v(), MODS[2 * 2 + 0])
    p3_mx_load(0)
    P.dma(woq[0].v(), WOv[:, :, 0:512])
    p3_loads(0)
    for db in range(1, 4):
        P.dma(woq[db].v(), WOv[:, :, db * 512:(db + 1) * 512])
    for k, (b, db) in enumerate(steps):
        sample = b < NSB
        g0 = b * TB
        ydst = y_s if sample else y_p
        x0 = g0 if sample else g0 - LS
        if b == NSB and db == 0:
            P.dma(Gcur.v(), MODS[2 * 2 + 1])
        if k + 1 < len(steps):
            p3_loads(k + 1)
        if db == 0 and b + 1 < NBLK:
            p3_mx_load(b + 1)
        w_ = woq[db]
        xo_ = xo[k % 2]
        mx_ = mxs[b % 2]
        for t in range(4):
            pb = bank(0, 8, 'o')
            for c in range(16):
                P.mm(pb.v(), mx_[:, c, t * 128:(t + 1) * 128], w_[:, c, :], start=(c == 0), stop=(c == 15))
            y_ = yt_[yi[0] % 3]
            yi[0] += 1
            P.tt('dve', y_.v(), pb.v(), Gcur[:, db * 512:(db + 1) * 512], ALU.mult)
            P.tt('pool', y_.v(), y_.v(), xo_[:, t, :], ALU.add)
            P.dma(ydst[x0 + t * 128: x0 + (t + 1) * 128, db * 512:(db + 1) * 512], y_.v())

    P.emit()
    st.close()
    return nc


_CACHE = {}


def _rope_tables():
    f64 = np.float64
    pos = np.arange(LS)
    row = (pos // 64).astype(f64)
    col = (pos % 64).astype(f64)
    fr16 = 10000.0 ** (-(np.arange(16, dtype=f64)) / 16.0)
    fr64 = 10000.0 ** (-(np.arange(64, dtype=f64)) / 64.0)
    ang_row = row[None, :] * fr16[:, None]
    ang_col = col[None, :] * fr16[:, None]
    ang_ret = pos.astype(f64)[None, :] * fr64[:, None]
    mcos = np.concatenate([np.cos(ang_row), np.cos(ang_row), np.cos(ang_col), np.cos(ang_col)], 0)
    msin = np.concatenate([-np.sin(ang_row), np.sin(ang_row), -np.sin(ang_col), np.sin(ang_col)], 0)
    rcos = np.concatenate([np.cos(ang_ret), np.cos(ang_ret)], 0)
    rsin = np.concatenate([-np.sin(ang_ret), np.sin(ang_ret)], 0)
    return [np.ascontiguousarray(a, dtype=np.float32) for a in (mcos, msin, rcos, rsin)]


def _consts():
    if 'c' in _CACHE:
        return _CACHE['c']
    f32 = np.float32
    ident = np.eye(128, dtype=f32)
    perm = np.zeros((128, 128), f32)
    for m in range(128):
        perm[(m + 64) % 128, m] = 1.0
    j = np.arange(128)[:, None].astype(f32)
    i = np.arange(128)[None, :].astype(f32)
    rt = np.zeros((128, NRT), f32)
    rt[:, RT_D1:RT_D1 + 128] = np.maximum(i - j, 0)
    rt[:, RT_D2:RT_D2 + 128] = np.maximum(j - i, 0)
    rt[:, RT_QDF:RT_QDF + 128] = i + 1
    rt[:, RT_QDB:RT_QDB + 128] = 128 - i
    rt[:, RT_KDF] = 127 - j[:, 0]
    rt[:, RT_KDB] = j[:, 0]
    rt[:, RT_C] = 128
    mcos, msin, rcos, rsin = _rope_tables()
    c = dict(ident=ident, permret=perm, rettab=rt, mcos=mcos, msin=msin, rcos=rcos, rsin=rsin)
    _CACHE['c'] = c
    return c


def _prep_shared(inp):
    f32 = np.float32
    w_in = np.asarray(inp['w_in'], f32)[0]
    o = np.cumsum([0, 384, 256, 64, 1024, 512, 512, 1024, 1024])
    q_lat, ckv, krope, g_a, rq, rk, rv, g_b = [w_in[:, o[k]:o[k + 1]] for k in range(8)]
    p64 = np.array([(i + 16) if (i % 32) < 16 else (i - 16) for i in range(64)])
    w_in2 = np.concatenate([q_lat, krope, krope[:, p64], ckv, g_a, rq, rk, g_b, rv, ckv, krope], axis=1)
    assert w_in2.shape[1] == WIN_COLS
    w_uq = np.asarray(inp['mla_w_uq'], f32)[0].reshape(384, 8, 192)
    w_uq2 = np.concatenate([w_uq[:, :, :128], w_uq[:, :, 128:], w_uq[:, :, 128:][:, :, p64]], axis=2).reshape(384, 2048)
    qw = np.asarray(inp['mla_qk_q_w'], f32)[0]
    kw = np.asarray(inp['mla_qk_k_w'], f32)[0]
    vecs = np.zeros((128, NV), f32)
    vecs[:, V_QNW:V_QNW + 3] = np.asarray(inp['mla_q_norm_w'], f32)[0].reshape(3, 128).T
    vecs[:, V_KVW:V_KVW + 2] = np.asarray(inp['mla_kv_norm_w'], f32)[0].reshape(2, 128).T
    vecs[:, V_WQN] = qw[:128]
    vecs[:64, V_WQR] = qw[128:]
    vecs[:64, V_WQP] = qw[128:][p64]
    vecs[:, V_WKN] = kw[:128]
    vecs[:64, V_WKR] = kw[128:]
    vecs[:64, V_WKP] = kw[128:][p64]
    vecs[:, V_GNW:V_GNW + 8] = np.asarray(inp['ret_gn_w'], f32)[0].reshape(8, 128).T
    lgd = np.concatenate([np.asarray(inp['ret_log_decay_fwd'], f32)[0], np.asarray(inp['ret_log_decay_bwd'], f32)[0]])[None, :]
    sh = dict(
        w_mod=np.ascontiguousarray(np.asarray(inp['w_mod'], f32)[0]),
        b_mod=np.ascontiguousarray(np.asarray(inp['b_mod'], f32)[0][None, :]),
        norm_w=np.ascontiguousarray(np.asarray(inp['norm_w'], f32)[0][None, :]),
        w_in=np.ascontiguousarray(w_in2),
        w_uq=np.ascontiguousarray(w_uq2),
        w_uk=np.ascontiguousarray(np.asarray(inp['mla_w_uk'], f32)[0]),
        w_uv=np.ascontiguousarray(np.asarray(inp['mla_w_uv'], f32)[0]),
        w_out=np.ascontiguousarray(np.asarray(inp['w_out'], f32)[0]),
        vecs=vecs,
        kvw_row=np.ascontiguousarray(np.asarray(inp['mla_kv_norm_w'], f32)[0][None, :]),
        lgd=np.ascontiguousarray(lgd),
    )
    sh.update(_consts())
    return sh


def make_in_maps(inp, cores=range(8)):
    f32 = np.float32
    sh = _prep_shared(inp)
    xs = np.asarray(inp['x_sample'], f32)
    xp = np.asarray(inp['x_prompt'], f32)
    c = np.asarray(inp['c'], f32)
    c_ctx = np.asarray(inp['c_ctx'], f32)
    maps = []
    for i in cores:
        m = dict(sh)
        m['xs'] = np.ascontiguousarray(xs[i])
        m['xp'] = np.ascontiguousarray(xp[4 * i:4 * i + 4].reshape(NPS * LP, D))
        cv = np.stack([c[i], c_ctx], 0)
        m['cT'] = np.ascontiguousarray(cv.reshape(2, 16, 128).transpose(2, 0, 1).reshape(128, 32))
        m['cache_ckv'] = np.ascontiguousarray(np.asarray(inp['cache_mla_ckv'], f32)[i, 0])
        m['cache_kr'] = np.ascontiguousarray(np.asarray(inp['cache_mla_krope'], f32)[i, 0])
        m['st_f'] = np.ascontiguousarray(np.asarray(inp['state_ret_fwd'], f32)[i, 0])
        m['st_b'] = np.ascontiguousarray(np.asarray(inp['state_ret_bwd'], f32)[i, 0])
        maps.append(m)
    return maps


def kernel(**inputs):
    if 'nc' not in _CACHE:
        _CACHE['nc'] = build_program(False)
    nc = _CACHE['nc']
    in_maps = make_in_maps(inputs)
    res = run_bass_kernel_spmd(nc, in_maps, core_ids=list(range(8)))
    R = res.results
    y_p = np.concatenate([r['y_p'].reshape(NPS, LP, D) for r in R], 0)
    y_s = np.stack([r['y_s'] for r in R], 0)
    ckv = np.concatenate([r['o_ckv'].reshape(NPS, 1, LP, 256) for r in R], 0)
    kr = np.concatenate([r['o_kr'].reshape(NPS, 1, LP, 64) for r in R], 0)
    sf = np.concatenate([r['o_sf'].reshape(NPS, 1, RH, 128, 256) for r in R], 0)
    sb = np.concatenate([r['o_sb'].reshape(NPS, 1, RH, 128, 256) for r in R], 0)
    return (y_p.astype(np.float32), y_s.astype(np.float32), ckv.astype(np.float32), kr.astype(np.float32),
            sf.astype(np.float32), sb.astype(np.float32))
```

```python
import numpy as np
from contextlib import ExitStack
import concourse.bass as bass
import concourse.mybir as mybir
from concourse.bass_utils import run_bass_kernel_spmd

F32 = mybir.dt.float32
BF16 = mybir.dt.bfloat16
F32R = mybir.dt.float32r
ALU = mybir.AluOpType
AF = mybir.ActivationFunctionType

EPS = 1e-6
NDSEM = 40
ENGS = ['pe', 'act', 'dve', 'pool', 'sp']
DEBUG = False
LIMITS = {}


class Tile:
    __slots__ = ('ap', 'name', 'lw', 'rd', 'excl')

    def __init__(self, ap, name, excl=False):
        self.ap = ap
        self.name = name
        self.lw = None
        self.rd = {}
        self.excl = excl

    def __getitem__(self, k):
        return View(self.ap[k], self)

    def v(self):
        return View(self.ap, self)


class View:
    __slots__ = ('ap', 'tile')

    def __init__(self, ap, tile):
        self.ap = ap
        self.tile = tile

    def __getitem__(self, k):
        return View(self.ap[k], self.tile)

    def re(self, pat, **kw):
        return View(self.ap.rearrange(pat, **kw), self.tile)

    def bc(self, shape):
        return View(self.ap.to_broadcast(list(shape)), self.tile)

    def un(self, axis):
        return View(self.ap.unsqueeze(axis), self.tile)


class Op:
    __slots__ = ('id', 'eng', 'fn', 'deps', 'sig', 'dma', 'dsem', 'dval')


class Prog:
    def __init__(self, nc):
        self.nc = nc
        self.ops = []
        self.by_eng = {e: [] for e in ENGS}
        self.ndma = 0
        self.dma_ops = []
        self.pending = {e: set() for e in ENGS}
        self.last_compute = {e: None for e in ENGS}

    def add(self, eng, fn, reads=(), writes=(), dma=False):
        op = Op()
        op.id = len(self.ops)
        op.eng = eng
        op.fn = fn
        op.dma = dma
        op.sig = None
        deps = set()
        for t in reads:
            if t.lw is not None:
                deps.add(t.lw)
            if t.excl:
                for k, v in t.rd.items():
                    if k != eng and k != 'dma':
                        deps.add(v)
        for t in writes:
            if t.lw is not None:
                deps.add(t.lw)
            for k, v in t.rd.items():
                if k == 'dma':
                    deps.update(v)
                else:
                    deps.add(v)
        deps |= self.pending[eng]
        self.pending[eng] = set()
        if dma:
            j = self.ndma
            self.ndma += 1
            op.dsem = j % NDSEM
            op.dval = 16 * (j // NDSEM + 1)
            if j >= NDSEM:
                deps.add(self.dma_ops[j - NDSEM].id)
            self.dma_ops.append(op)
        deps.discard(op.id)
        if (not dma) and eng == 'pe':
            deps = {d for d in deps if self.ops[d].dma or self.ops[d].eng != 'pe'}
        op.deps = deps
        for t in reads:
            if dma:
                t.rd.setdefault('dma', []).append(op.id)
            else:
                t.rd[eng] = op.id
        for t in writes:
            t.lw = op.id
            t.rd = {}
        self.ops.append(op)
        self.by_eng[eng].append(op)
        if not dma:
            self.last_compute[eng] = op.id
        return op

    def barrier(self):
        s = set()
        for e in ENGS:
            if self.last_compute[e] is not None:
                s.add(self.last_compute[e])
        for op in self.dma_ops[-NDSEM:]:
            s.add(op.id)
        for e in ENGS:
            self.pending[e] |= s

    def mm(self, out, lhsT, rhs, start=True, stop=True):
        o, l, r = out.ap, lhsT.ap, rhs.ap
        rd = [lhsT.tile, rhs.tile] + ([] if start else [out.tile])
        self.add('pe', lambda e: e.matmul(o, l, r, start=start, stop=stop), rd, [out.tile])

    def tr(self, out, in_, ident):
        o, i, d = out.ap, in_.ap, ident.ap
        self.add('pe', lambda e: e.transpose(o, i, d), [in_.tile, ident.tile], [out.tile])

    def act(self, out, in_, func, bias=None, scale=None, accum=None):
        o, i = out.ap, in_.ap
        kw = {}
        rd = [in_.tile]
        wr = [out.tile]
        if bias is not None:
            if isinstance(bias, View):
                kw['bias'] = bias.ap
                rd.append(bias.tile)
            else:
                kw['bias'] = bias
        if scale is not None:
            if isinstance(scale, View):
                kw['scale'] = scale.ap
                rd.append(scale.tile)
            else:
                kw['scale'] = scale
        if accum is not None:
            kw['accum_out'] = accum.ap
            wr.append(accum.tile)
        self.add('act', lambda e: e.activation(o, i, func, **kw), rd, wr)

    def tt(self, eng, out, a, b, op):
        o, x, y = out.ap, a.ap, b.ap
        self.add(eng, lambda e: e.tensor_tensor(out=o, in0=x, in1=y, op=op), [a.tile, b.tile], [out.tile])

    def ts(self, eng, out, a, s1, s2, op0, op1=None):
        o, x = out.ap, a.ap
        rd = [a.tile]

        def cv(s):
            if isinstance(s, View):
                rd.append(s.tile)
                return s.ap
            return s
        v1 = cv(s1)
        v2 = cv(s2) if s2 is not None else None
        if op1 is None:
            self.add(eng, lambda e: e.tensor_scalar(o, x, v1, None, op0), rd, [out.tile])
        else:
            self.add(eng, lambda e: e.tensor_scalar(o, x, v1, v2, op0, op1), rd, [out.tile])

    def stt(self, eng, out, in0, scalar, in1, op0, op1):
        o, x, y = out.ap, in0.ap, in1.ap
        rd = [in0.tile, in1.tile]
        if isinstance(scalar, View):
            rd.append(scalar.tile)
            sc = scalar.ap
        else:
            sc = scalar
        self.add(eng, lambda e: e.scalar_tensor_tensor(out=o, in0=x, scalar=sc, in1=y, op0=op0, op1=op1),
                 rd, [out.tile])

    def cp(self, eng, out, in_, scale=None):
        o, i = out.ap, in_.ap
        if eng == 'act':
            if scale is None:
                self.add('act', lambda e: e.activation(o, i, AF.Copy), [in_.tile], [out.tile])
            else:
                self.add('act', lambda e: e.activation(o, i, AF.Copy, scale=scale), [in_.tile], [out.tile])
        else:
            if scale is None:
                self.add(eng, lambda e: e.tensor_copy(out=o, in_=i), [in_.tile], [out.tile])
            else:
                self.add(eng, lambda e: e.tensor_scalar(o, i, scale, None, ALU.mult), [in_.tile], [out.tile])

    def recip(self, out, in_):
        o, i = out.ap, in_.ap
        self.add('dve', lambda e: e.reciprocal(out=o, in_=i), [in_.tile], [out.tile])

    def memset(self, eng, out, val):
        o = out.ap
        self.add(eng, lambda e: e.memset(o, val), [], [out.tile])

    def dma(self, out, in_, q='sp'):
        o, i = out.ap, in_.ap
        self.add(q, lambda e: e.dma_start(out=o, in_=i), [in_.tile], [out.tile], dma=True)

    def emit(self):
        nc = self.nc
        self.barrier()
        self.add('sp', lambda e: e.nop(), [], [])
        needed = set()
        for op in self.ops:
            needed |= op.deps
        cnt = {e: 0 for e in ENGS}
        for op in self.ops:
            if op.dma:
                continue
            if op.id in needed:
                cnt[op.eng] += 1
                op.sig = cnt[op.eng]
        with ExitStack() as st:
            sems = {e: st.enter_context(nc.semaphore(f"sem_{e}")) for e in ENGS}
            dsems = [st.enter_context(nc.semaphore(f"dsem{i}")) for i in range(NDSEM)]
            block = st.enter_context(nc.Block())
            ops = self.ops

            def run_engine(ename, h):
                waited = {}
                for op in self.by_eng[ename]:
                    req = {}
                    for d in op.deps:
                        p = ops[d]
                        if p.dma:
                            key = ('d', p.dsem)
                            val = p.dval
                        else:
                            key = ('e', p.eng)
                            val = p.sig
                        if val > req.get(key, 0):
                            req[key] = val
                    for key, val in req.items():
                        if waited.get(key, 0) < val:
                            sem = dsems[key[1]] if key[0] == 'd' else sems[key[1]]
                            h.wait_ge(sem, val)
                            waited[key] = val
                    ins = op.fn(h)
                    if op.dma:
                        ins.then_inc(dsems[op.dsem], 16)
                    elif op.sig is not None:
                        ins.then_inc(sems[ename], 1)

            @block.tensor
            def _(h):
                run_engine('pe', h)

            @block.scalar
            def _(h):
                run_engine('act', h)

            @block.vector
            def _(h):
                run_engine('dve', h)

            @block.gpsimd
            def _(h):
                run_engine('pool', h)

            @block.sync
            def _(h):
                run_engine('sp', h)


D = 2048
LS = 4096
LP = 256
NPS = 4
NTOK = LS + NPS * LP
PAST = 256
NKEY = NTOK + PAST
TB = 512
NBLK = NTOK // TB
NSB = LS // TB
H = 8
RH = 4
WIN_COLS = 5184
SLABS = [(0, 512), (512, 256), (768, 512), (1280, 512), (1792, 512), (2304, 512),
         (2816, 512), (3328, 512), (3840, 512), (4352, 512), (4864, 320)]
ARENA_BYTES = 207 * 1024

V_QNW = 0
V_KVW = 3
V_WQN = 5
V_WQR = 6
V_WQP = 7
V_WKN = 8
V_WKR = 9
V_WKP = 10
V_GNW = 11
NV = 19
RT_D1 = 0
RT_D2 = 128
RT_QDF = 256
RT_QDB = 384
RT_KDF = 512
RT_KDB = 513
RT_C = 514
NRT = 515


def build_program(debug=False, stop_after=None):
    nc = bass.Bass("TRN2", target_bir_lowering=False)
    P = Prog(nc)

    def din(name, shape, dt=F32):
        return Tile(nc.dram_tensor(name, list(shape), dt, kind="ExternalInput").ap(), name)

    def dout(name, shape, dt=F32):
        return Tile(nc.dram_tensor(name, list(shape), dt, kind="ExternalOutput").ap(), name)

    def dscr(name, shape, dt=BF16):
        kind = "ExternalOutput" if debug else "Internal"
        return Tile(nc.dram_tensor(name, list(shape), dt, kind=kind).ap(), name)

    xs = din("xs", [LS, D])
    xp = din("xp", [NPS * LP, D])
    cT = din("cT", [128, 32])
    w_mod = din("w_mod", [D, 3 * D])
    b_mod = din("b_mod", [1, 3 * D])
    norm_w = din("norm_w", [1, D])
    w_in = din("w_in", [D, WIN_COLS])
    w_uq = din("w_uq", [384, 2048])
    w_uk = din("w_uk", [256, 1024])
    w_uv = din("w_uv", [256, 1024])
    w_out = din("w_out", [D, D])
    vecs_d = din("vecs", [128, NV])
    kvw_row = din("kvw_row", [1, 256])
    cache_ckv = din("cache_ckv", [PAST, 256])
    cache_kr = din("cache_kr", [PAST, 64])
    st_f = din("st_f", [RH, 128, 256])
    st_b = din("st_b", [RH, 128, 256])
    lgd = din("lgd", [1, 8])
    ident_d = din("ident", [128, 128])
    perm_d = din("permret", [128, 128])
    rettab_d = din("rettab", [128, NRT])
    mcos_d = din("mcos", [64, LS])
    msin_d = din("msin", [64, LS])
    rcos_d = din("rcos", [128, LS])
    rsin_d = din("rsin", [128, LS])
    y_s = dout("y_s", [LS, D])
    y_p = dout("y_p", [NPS * LP, D])
    o_ckv = dout("o_ckv", [NPS * LP, 256])
    o_kr = dout("o_kr", [NPS * LP, 64])
    o_sf = dout("o_sf", [NPS, RH, 128, 256])
    o_sb = dout("o_sb", [NPS, RH, 128, 256])
    WIN = dscr("WIN", [D, WIN_COLS])
    WUQ = dscr("WUQ", [384, 2048])
    WUK = dscr("WUK", [256, 1024])
    WUV = dscr("WUV", [256, 1024])
    WOUT = dscr("WOUT", [D, D])
    MODS = dscr("MODS", [6, 128, D], F32)
    QN = dscr("QN", [H, 128, NTOK])
    QR = dscr("QR", [H, 64, NTOK])
    KN = dscr("KN", [H, 128, NKEY])
    KR = dscr("KR", [H, 64, NKEY])
    VV = dscr("VV", [NKEY, 1024])
    GA = dscr("GA", [1024, NTOK])
    GB = dscr("GB", [1024, NTOK])
    RQ = dscr("RQ", [RH, 128, NTOK])
    RK = dscr("RK", [RH, 128, NTOK])
    RV = dscr("RV", [NTOK, 1024])
    MIX = dscr("MIX", [D, NTOK])

    st = ExitStack()
    arena = st.enter_context(nc.sbuf_tensor("arena", [128, ARENA_BYTES // 2], BF16))
    psh = [st.enter_context(nc.psum_tensor(f"ps{i}", [128, 512], F32)) for i in range(8)]
    psb = [Tile(h_.ap(), f"ps{i}", excl=True) for i, h_ in enumerate(psh)]
    psb_bf = [h_.bitcast(BF16).ap() for h_ in psh]

    bump = [0]

    def alloc(name, shape, dt=F32):
        esz = 4 if dt == F32 else 2
        n = 1
        for s in shape[1:]:
            n *= s
        nbytes = (n * esz + 63) // 64 * 64
        off = bump[0]
        bump[0] += nbytes
        assert bump[0] <= ARENA_BYTES, f"arena overflow at {name}: {bump[0]}"
        ap = arena[0:shape[0], off // 2: off // 2 + n * esz // 2]
        if dt == F32:
            ap = ap.bitcast(F32)
        if len(shape) == 3:
            ap = ap.rearrange("p (a b) -> p a b", a=shape[1])
        elif len(shape) == 4:
            ap = ap.rearrange("p (a b c) -> p a b c", a=shape[1], b=shape[2])
        return Tile(ap, name)

    rot = {}

    def bank(lo=0, hi=6, key='g'):
        k = (key, lo, hi)
        i = rot.get(k, lo)
        rot[k] = lo + (i - lo + 1) % (hi - lo)
        return psb[i]

    eng_rr = [0]

    identf = alloc("identf", [128, 128], F32)
    identb = alloc("identb", [128, 128], BF16)
    onesb = alloc("onesb", [128, 128], BF16)
    permb = alloc("permb", [128, 128], BF16)
    vecs = alloc("vecs", [128, NV], F32)
    epsc = alloc("epsc", [128, 1], F32)
    rtab = alloc("rtab", [128, NRT], F32)
    Mh = alloc("Mh", [128, RH, 128], F32)
    QDF = alloc("QDF", [128, RH, 128], F32)
    QDB = alloc("QDB", [128, RH, 128], F32)
    dcols = alloc("dcols", [128, 16], F32)
    nee = alloc("nee", [128, 8], F32)
    wuq = alloc("wuq", [128, 3, 2048], BF16)
    wuk = alloc("wuk", [128, 2, 1024], BF16)
    wuv = alloc("wuv", [128, 2, 1024], BF16)
    persist_mark = bump[0]

    P.dma(identf.v(), ident_d.v())
    P.dma(vecs.v(), vecs_d.v())
    P.dma(rtab.v(), rettab_d.v())
    P.cp('dve', identb.v(), identf.v())
    P.memset('pool', onesb.v(), 1.0)
    P.memset('pool', epsc.v(), EPS)

    stg_f = [alloc(f"stgf{i}", [128, 2048], F32) for i in range(7)]
    stg_b = [alloc(f"stgb{i}", [128, 2048], BF16) for i in range(7)]
    cvi = [0]

    def convert_pieces(src, dst, R, C):
        out = []
        for r in range(R // 128):
            c0 = 0
            while c0 < C:
                w = min(2048, C - c0)

                def piece(r=r, c0=c0, w=w):
                    n = len(stg_f)
                    i = cvi[0] % n
                    cvi[0] += 1
                    P.dma(stg_f[i][:, 0:w], src[r * 128:(r + 1) * 128, c0:c0 + w])
                    P.cp(['dve', 'act'][i % 2], stg_b[i][:, 0:w], stg_f[i][:, 0:w])
                    P.dma(dst[r * 128:(r + 1) * 128, c0:c0 + w], stg_b[i][:, 0:w])
                out.append(piece)
                c0 += w
        return out

    def convert(src, dst, R, C):
        for p_ in convert_pieces(src, dst, R, C):
            p_()

    P.dma(stg_f[0][:, 0:128], perm_d.v())
    P.cp('dve', permb.v(), stg_f[0][:, 0:128])

    pieces = []

    def add_pieces(src, dst, R, C):
        for r in range(R // 128):
            c0 = 0
            while c0 < C:
                w = min(2048, C - c0)
                pieces.append((src, dst, r, c0, w))
                c0 += w
    add_pieces(w_in, WIN, D, WIN_COLS)
    add_pieces(w_uq, WUQ, 384, 2048)
    add_pieces(w_uk, WUK, 256, 1024)
    add_pieces(w_uv, WUV, 256, 1024)
    NST = len(stg_f)
    ld_i = [0]
    cv_i = [0]

    def conv_load():
        if ld_i[0] < len(pieces):
            src, dst, r, c0, w = pieces[ld_i[0]]
            b_ = ld_i[0] % NST
            P.dma(stg_f[b_][:, 0:w], src[r * 128:(r + 1) * 128, c0:c0 + w])
            ld_i[0] += 1

    def conv_step():
        if cv_i[0] < len(pieces):
            src, dst, r, c0, w = pieces[cv_i[0]]
            b_ = cv_i[0] % NST
            P.cp(['dve', 'act'][cv_i[0] % 2], stg_b[b_][:, 0:w], stg_f[b_][:, 0:w])
            P.dma(dst[r * 128:(r + 1) * 128, c0:c0 + w], stg_b[b_][:, 0:w])
            cv_i[0] += 1
        conv_load()

    ct = alloc("ct", [128, 32], F32)
    sc = alloc("sc", [128, 32], F32)
    screp = alloc("screp", [128, 2, 16, 128], BF16)
    MW = 256
    NMB = 3 * D // MW
    wm = [alloc(f"wm{i}", [128, 16, MW], F32) for i in range(3)]
    wmb = [alloc(f"wmb{i}", [128, 16, MW], BF16) for i in range(2)]
    bm = [alloc(f"bm{i}", [128, MW], F32) for i in range(3)]
    nwb = [alloc(f"nwb{i}", [128, MW], F32) for i in range(3)]
    mo = [alloc(f"mo{i}", [128, MW], F32) for i in range(4)]
    P.dma(ct.v(), cT.v())
    P.act(sc.v(), ct.v(), AF.Silu)
    for v in range(2):
        P.cp('dve', screp[:, v], sc[:, v * 16:(v + 1) * 16].un(2).bc([128, 16, 128]))
    w_mod_v = w_mod.v().re("(c p) n -> p c n", p=128)

    def mod_load(cb):
        c0 = (cb * MW) % D
        P.dma(wm[cb % 3].v(), w_mod_v[:, :, cb * MW:(cb + 1) * MW])
        P.dma(bm[cb % 3].v(), View(b_mod.ap[0:1, cb * MW:(cb + 1) * MW].partition_broadcast(128), b_mod).re("p o n -> p (o n)"))
        if (cb * MW) // D == 1:
            P.dma(nwb[cb % 3].v(), View(norm_w.ap[0:1, c0:c0 + MW].partition_broadcast(128), norm_w).re("p o n -> p (o n)"))

    mod_load(0)
    mod_load(1)
    for _ in range(NST - 1):
        conv_load()
    moi = 0
    for cb in range(NMB):
        kind = (cb * MW) // D
        c0 = (cb * MW) % D
        if cb + 2 < NMB:
            mod_load(cb + 2)
        for _ in range(3):
            conv_step()
        wmt, bmt, nwt, wb_ = wm[cb % 3], bm[cb % 3], nwb[cb % 3], wmb[cb % 2]
        P.cp('dve', wb_[:, 0:8, :], wmt[:, 0:8, :])
        P.cp('act', wb_[:, 8:16, :], wmt[:, 8:16, :])
        for v in range(2):
            pb = bank()
            for c in range(16):
                P.mm(pb[:, 0:MW], screp[:, v, c, :], wb_[:, c, :], start=(c == 0), stop=(c == 15))
            m = mo[moi % 4]
            moi += 1
            P.tt('dve', m.v(), pb[:, 0:MW], bmt.v(), ALU.add)
            if kind == 1:
                P.stt('dve', m.v(), m.v(), 1.0, nwt.v(), ALU.add, ALU.mult)
            P.dma(MODS[kind * 2 + v][:, c0:c0 + MW], m.v())
    while cv_i[0] < len(pieces):
        conv_step()
    P.dma(wuq.v(), WUQ.v().re("(c p) n -> p c n", p=128))
    P.dma(wuk.v(), WUK.v().re("(c p) n -> p c n", p=128))
    P.dma(wuv.v(), WUV.v().re("(c p) n -> p c n", p=128))

    lgt = alloc("lgt", [128, 8], F32)
    ee = alloc("ee", [128, 8], F32)
    e1 = alloc("e1", [128, 128], F32)
    e2 = alloc("e2", [128, 128], F32)
    P.dma(lgt.v(), View(lgd.ap[0:1, :].partition_broadcast(128), lgd).re("p o n -> p (o n)"))
    P.act(ee.v(), lgt.v(), AF.Exp)
    P.ts('dve', nee.v(), ee.v(), -1.0, None, ALU.mult)
    for h in range(RH):
        nf = nee[:, h:h + 1]
        nb_ = nee[:, 4 + h:5 + h]
        P.act(e1.v(), rtab[:, RT_D1:RT_D1 + 128], AF.Exp, scale=nf)
        P.act(e2.v(), rtab[:, RT_D2:RT_D2 + 128], AF.Exp, scale=nb_)
        P.tt('dve', Mh[:, h, :], e1.v(), e2.v(), ALU.mult)
        P.act(QDF[:, h, :], rtab[:, RT_QDF:RT_QDF + 128], AF.Exp, scale=nf)
        P.act(QDB[:, h, :], rtab[:, RT_QDB:RT_QDB + 128], AF.Exp, scale=nb_)
        P.act(dcols[:, h:h + 1], rtab[:, RT_KDF:RT_KDF + 1], AF.Exp, scale=nf)
        P.act(dcols[:, 4 + h:5 + h], rtab[:, RT_KDB:RT_KDB + 1], AF.Exp, scale=nb_)
        P.act(dcols[:, 8 + h:9 + h], rtab[:, RT_C:RT_C + 1], AF.Exp, scale=nf)
        P.act(dcols[:, 12 + h:13 + h], rtab[:, RT_C:RT_C + 1], AF.Exp, scale=nb_)

    P.barrier()
    bump[0] = persist_mark
    if stop_after == 'setup':
        P.emit()
        st.close()
        return nc

    xt = [alloc(f"xt{i}", [128, D], F32) for i in range(2)]
    hn = [alloc(f"hn{i}", [128, D], BF16) for i in range(2)]
    hTs = [alloc(f"hT{i}", [128, 16, TB], BF16) for i in range(2)]
    wsl = [alloc(f"wsl{i}", [128, 16, 512], BF16) for i in range(2)]
    Acur = alloc("Acur", [128, D], F32)
    Bcur = alloc("Bcur", [128, D], F32)
    qraw = alloc("qraw", [128, 3, TB], F32)
    ckraw = alloc("ckraw", [128, 2, TB], F32)
    krraw = alloc("krraw", [64, TB], F32)
    kpraw = alloc("kpraw", [64, TB], F32)
    krot = alloc("krot", [64, TB], F32)
    kp2 = alloc("kp2", [64, TB], F32)
    sq = [alloc(f"sq{i}", [128, TB], BF16) for i in range(4)]
    sqkr = alloc("sqkr", [64, TB], BF16)
    msq = [alloc(f"msq{i}", [128, TB], F32) for i in range(2)]
    ff = [alloc(f"ff{i}", [128, TB], F32) for i in range(3)]
    qln = alloc("qln", [128, 3, TB], BF16)
    ckvn = alloc("ckvn", [128, 2, TB], BF16)
    ob = [alloc(f"ob{i}", [128, TB], BF16) for i in range(6)]
    t1 = [alloc(f"t1{i}", [128, TB], F32) for i in range(2)]
    t2 = [alloc(f"t2{i}", [128, TB], F32) for i in range(2)]
    mc = alloc("mc", [64, TB], F32)
    msn = alloc("msn", [64, TB], F32)
    rc = alloc("rc", [128, TB], F32)
    rs = alloc("rs", [128, TB], F32)
    vt = [alloc(f"vt{i}", [128, 1024], BF16) for i in range(2)]
    rawb = [alloc(f"rawb{i}", [128, TB], BF16) for i in range(2)]
    rvt = [alloc(f"rvt{i}", [128, 512], BF16) for i in range(2)]
    ckout = [alloc(f"ckout{i}", [128, 320], F32) for i in range(2)]
    kvwbc = alloc("kvwbc", [128, 256], F32)
    ssc = alloc("ssc", [128, 8], F32)
    ctx_in = ckout

    cnt = {'sq': 0, 'ms': 0, 'ff': 0, 'ob': 0, 't': 0, 'vt': 0, 'rawb': 0, 'rvt': 0, 'ck': 0, 'ssc': 0}

    def nxt(lst, key):
        i = cnt[key]
        cnt[key] += 1
        return lst[i % len(lst)]

    P.dma(kvwbc.v(), View(kvw_row.ap[0:1, :].partition_broadcast(128), kvw_row).re("p o n -> p (o n)"))

    def vcol(c, np_=128):
        return vecs[0:np_, c:c + 1]

    def rms_bc(sqs, nfeat, nt):
        sb = bank(6, 8, 's')
        for i, (v, K) in enumerate(sqs):
            P.mm(sb[:, 0:nt], onesb[0:K, :], v, start=(i == 0), stop=(i == len(sqs) - 1))
        ms = nxt(msq, 'ms')
        P.act(ms[:, 0:nt], sb[:, 0:nt], AF.Ln, scale=1.0 / nfeat, bias=epsc[:, 0:1])
        f = nxt(ff, 'ff')
        P.act(f[:, 0:nt], ms[:, 0:nt], AF.Exp, scale=-0.5)
        return f

    def kv_heads(nt, rope, koff):
        if rope:
            P.stt('dve', krot[0:64, 0:nt], krraw[0:64, 0:nt], vcol(V_WKR, 64), mc[0:64, 0:nt], ALU.mult, ALU.mult)
            P.stt('dve', kp2[0:64, 0:nt], kpraw[0:64, 0:nt], vcol(V_WKP, 64), msn[0:64, 0:nt], ALU.mult, ALU.mult)
            P.tt('pool', krot[0:64, 0:nt], krot[0:64, 0:nt], kp2[0:64, 0:nt], ALU.add)
        else:
            P.ts('dve', krot[0:64, 0:nt], krraw[0:64, 0:nt], vcol(V_WKR, 64), None, ALU.mult)
        def k_stage1(h):
            bn = bank()
            for c in range(2):
                P.mm(bn[:, 0:nt], wuk[:, c, h * 128:(h + 1) * 128], ckvn[:, c, 0:nt], start=(c == 0), stop=(c == 1))
            sqa = nxt(sq, 'sq')
            P.act(sqa[:, 0:nt], bn[:, 0:nt], AF.Square)
            return (h, bn, sqa)

        def k_stage2(ctx):
            h, bn, sqa = ctx
            f = rms_bc([(sqa[:, 0:nt], 128), (sqkr[0:64, 0:nt], 64)], 192, nt)
            o1 = nxt(ob, 'ob')
            P.stt('dve', o1[:, 0:nt], bn[:, 0:nt], vcol(V_WKN), f[:, 0:nt], ALU.mult, ALU.mult)
            P.dma(KN[h][:, koff:koff + nt], o1[:, 0:nt])
            o2 = nxt(ob, 'ob')
            P.tt('pool', o2[0:64, 0:nt], krot[0:64, 0:nt], f[0:64, 0:nt], ALU.mult)
            P.dma(KR[h][:, koff:koff + nt], o2[0:64, 0:nt])

        prev = None
        for h in range(H):
            cur = k_stage1(h)
            if prev is not None:
                k_stage2(prev)
            prev = cur
        k_stage2(prev)
        for t in range(nt // 128):
            vtt = nxt(vt, 'vt')
            for cb in range(2):
                pb = bank()
                for c in range(2):
                    P.mm(pb.v(), ckvn[:, c, t * 128:(t + 1) * 128], wuv[:, c, cb * 512:(cb + 1) * 512],
                         start=(c == 0), stop=(c == 1))
                P.cp('act', vtt[:, cb * 512:(cb + 1) * 512], pb.v())
            P.dma(VV[koff + t * 128: koff + (t + 1) * 128, :], vtt.v())

    WINv = WIN.v().re("(c p) n -> p c n", p=128)
    slab_i = [0]

    def load_slab(src_v, off, w):
        ws = wsl[slab_i[0] % 2]
        slab_i[0] += 1
        P.dma(ws[:, :, 0:w], src_v[:, :, off:off + w])
        return ws

    work = [(b_, s_) for b_ in range(NBLK) for s_ in range(10 if b_ < NSB else 11)]
    slab_q = []
    work_i = [0]

    def issue_next_slab():
        if work_i[0] < len(work):
            off_, w_ = SLABS[work[work_i[0]][1]]
            slab_q.append(load_slab(WINv, off_, w_))
            work_i[0] += 1

    def blk_info(b):
        sample = b < NSB
        g0 = b * TB
        return sample, g0, (xs if sample else xp), (g0 if sample else g0 - LS)

    def prepA(b, t):
        sample, g0, xsrc, x0 = blk_info(b)
        if t == 0 and (b == 0 or b == NSB):
            v = 0 if sample else 1
            P.dma(Bcur.v(), MODS[0 * 2 + v])
            P.dma(Acur.v(), MODS[1 * 2 + v])
        x = xt[t % 2]
        hh = hn[t % 2]
        P.dma(x.v(), xsrc[x0 + t * 128: x0 + (t + 1) * 128, :])
        s1 = ssc[:, (cnt['ssc'] % 8):(cnt['ssc'] % 8) + 1]
        cnt['ssc'] += 1
        P.act(hh.v(), x.v(), AF.Square, accum=s1)
        P.act(s1, s1, AF.Ln, scale=1.0 / D, bias=epsc[:, 0:1])
        P.act(s1, s1, AF.Exp, scale=-0.5)
        P.stt('dve', x.v(), x.v(), s1, Acur.v(), ALU.mult, ALU.mult)
        P.tt('pool', hh.v(), x.v(), Bcur.v(), ALU.add)

    def prepB(b, t):
        hh = hn[t % 2]
        hT = hTs[b % 2]
        for half in range(2):
            pbt = bank()
            pbv = View(psb_bf[psb.index(pbt)], pbt)
            for c8 in range(8):
                c = half * 8 + c8
                P.tr(pbv[:, c8 * 128:(c8 + 1) * 128], hh[:, c * 128:(c + 1) * 128], identb.v())
            src = pbv.re("p (c n) -> p c n", c=8)
            dst = hT[:, half * 8:(half + 1) * 8, t * 128:(t + 1) * 128]
            if half == 0:
                P.cp('act', dst, src)
            else:
                P.cp('dve', dst, src)

    def phase1_block(b, nb):
        sample, g0, xsrc, x0 = blk_info(b)
        koff = g0 if sample else g0 + PAST
        hT = hTs[b % 2]
        if sample:
            P.dma(mc.v(), mcos_d[:, g0:g0 + TB])
            P.dma(msn.v(), msin_d[:, g0:g0 + TB])
            P.dma(rc.v(), rcos_d[:, g0:g0 + TB])
            P.dma(rs.v(), rsin_d[:, g0:g0 + TB])
        nsl = 10 if sample else 11
        nsl = min(nsl, LIMITS.get('p1_slabs', nsl))
        for s in range(nsl):
            off, w = SLABS[s]
            ws = slab_q.pop(0)
            issue_next_slab()

            def group(c0, M, ws=ws):
                pb = bank()
                for c in range(16):
                    P.mm(pb[0:M, :], ws[:, c, c0:c0 + M], hT[:, c, :], start=(c == 0), stop=(c == 15))
                return pb

            if s == 0:
                sqs = []
                for g in range(3):
                    pb = group(g * 128, 128)
                    sqa = nxt(sq, 'sq')
                    P.act(sqa.v(), pb.v(), AF.Square)
                    P.cp('dve', qraw[:, g, :], pb.v())
                    sqs.append((sqa.v(), 128))
                if LIMITS.get('s0_stage', 9) < 2:
                    continue
                pb = group(384, 64)
                P.act(sqkr.v(), pb[0:64, :], AF.Square)
                P.cp('dve', krraw.v(), pb[0:64, :])
                if sample:
                    pb = group(448, 64)
                    P.cp('dve', kpraw.v(), pb[0:64, :])
                if LIMITS.get('s0_stage', 9) < 3:
                    continue
                f = rms_bc(sqs, 384, TB)
                for g in range(3):
                    P.stt('dve', qln[:, g, :], qraw[:, g, :], vcol(V_QNW + g), f.v(), ALU.mult, ALU.mult)
                if LIMITS.get('s0_stage', 9) < 4:
                    continue
                def q_stage1(h):
                    bn = bank()
                    for c in range(3):
                        P.mm(bn.v(), wuq[:, c, h * 256:h * 256 + 128], qln[:, c, :], start=(c == 0), stop=(c == 2))
                    br = bank()
                    for c in range(3):
                        P.mm(br[0:64, :], wuq[:, c, h * 256 + 128:h * 256 + 192], qln[:, c, :], start=(c == 0), stop=(c == 2))
                    bp = None
                    if sample:
                        bp = bank()
                        for c in range(3):
                            P.mm(bp[0:64, :], wuq[:, c, h * 256 + 192:h * 256 + 256], qln[:, c, :], start=(c == 0), stop=(c == 2))
                    sqa = nxt(sq, 'sq')
                    sqb = nxt(sq, 'sq')
                    P.act(sqa.v(), bn.v(), AF.Square)
                    P.act(sqb[0:64, :], br[0:64, :], AF.Square)
                    return (h, bn, br, bp, sqa, sqb)

                def q_stage2(ctx):
                    h, bn, br, bp, sqa, sqb = ctx
                    f = rms_bc([(sqa.v(), 128), (sqb[0:64, :], 64)], 192, TB)
                    o1 = nxt(ob, 'ob')
                    P.stt('dve', o1.v(), bn.v(), vcol(V_WQN), f.v(), ALU.mult, ALU.mult)
                    P.dma(QN[h][:, g0:g0 + TB], o1.v())
                    o2 = nxt(ob, 'ob')
                    if not sample:
                        P.stt('dve', o2[0:64, :], br[0:64, :], vcol(V_WQR, 64), f[0:64, :], ALU.mult, ALU.mult)
                    else:
                        a = nxt(t1, 't')
                        b_ = t2[(cnt['t'] - 1) % 2]
                        P.stt('dve', a[0:64, :], br[0:64, :], vcol(V_WQR, 64), mc.v(), ALU.mult, ALU.mult)
                        P.stt('dve', b_[0:64, :], bp[0:64, :], vcol(V_WQP, 64), msn.v(), ALU.mult, ALU.mult)
                        P.tt('pool', a[0:64, :], a[0:64, :], b_[0:64, :], ALU.add)
                        P.tt('pool', o2[0:64, :], a[0:64, :], f[0:64, :], ALU.mult)
                    P.dma(QR[h][:, g0:g0 + TB], o2[0:64, :])

                for h in range(H):
                    q_stage2(q_stage1(h))
            elif s == 1:
                sqs = []
                for g in range(2):
                    pb = group(g * 128, 128)
                    sqa = nxt(sq, 'sq')
                    P.act(sqa.v(), pb.v(), AF.Square)
                    P.cp('dve', ckraw[:, g, :], pb.v())
                    sqs.append((sqa.v(), 128))
                f = rms_bc(sqs, 256, TB)
                for g in range(2):
                    P.stt('dve', ckvn[:, g, :], ckraw[:, g, :], vcol(V_KVW + g), f.v(), ALU.mult, ALU.mult)
                kv_heads(TB, sample, koff)
            elif s in (2, 3, 6, 7):
                dstT = GA if s in (2, 3) else GB
                r0 = (s % 2) * 512
                for g in range(4):
                    pb = group(g * 128, 128)
                    o1 = nxt(ob, 'ob')
                    P.act(o1.v(), pb.v(), AF.Silu)
                    P.dma(dstT[r0 + g * 128: r0 + (g + 1) * 128, g0:g0 + TB], o1.v())
            elif s in (4, 5):
                dstT = RQ if s == 4 else RK
                scl = None if s == 4 else 128.0 ** -0.5
                for g in range(4):
                    pb = group(g * 128, 128)
                    o1 = nxt(ob, 'ob')
                    if not sample:
                        P.cp('act', o1.v(), pb.v(), scale=scl)
                    else:
                        rb = nxt(rawb, 'rawb')
                        P.cp('act', rb.v(), pb.v(), scale=scl)
                        pp = bank()
                        P.mm(pp.v(), permb.v(), rb.v())
                        a = nxt(t1, 't')
                        b_ = t2[(cnt['t'] - 1) % 2]
                        P.tt('pool', a.v(), rb.v(), rc.v(), ALU.mult)
                        P.tt('dve', b_.v(), pp.v(), rs.v(), ALU.mult)
                        P.tt('pool', o1.v(), a.v(), b_.v(), ALU.add)
                    P.dma(dstT[g][:, g0:g0 + TB], o1.v())
            elif s in (8, 9):
                for t in range(4):
                    pb = bank()
                    for c in range(16):
                        P.mm(pb.v(), hT[:, c, t * 128:(t + 1) * 128], ws[:, c, 0:512], start=(c == 0), stop=(c == 15))
                    r = nxt(rvt, 'rvt')
                    P.cp('act' if t % 2 == 0 else 'dve', r.v(), pb.v())
                    P.dma(RV[g0 + t * 128: g0 + (t + 1) * 128, (s - 8) * 512:(s - 7) * 512], r.v())
            else:
                for t in range(4):
                    pb = bank()
                    for c in range(16):
                        P.mm(pb[:, 0:320], hT[:, c, t * 128:(t + 1) * 128], ws[:, c, 0:320], start=(c == 0), stop=(c == 15))
                    s1 = ssc[:, (cnt['ssc'] % 8):(cnt['ssc'] % 8) + 1]
                    cnt['ssc'] += 1
                    oj = nxt(ob, 'ob')
                    P.act(oj[:, 0:256], pb[:, 0:256], AF.Square, accum=s1)
                    P.act(s1, s1, AF.Ln, scale=1.0 / 256, bias=epsc[:, 0:1])
                    P.act(s1, s1, AF.Exp, scale=-0.5)
                    co = nxt(ckout, 'ck')
                    P.stt('dve', co[:, 0:256], pb[:, 0:256], s1, kvwbc.v(), ALU.mult, ALU.mult)
                    P.cp('act', co[:, 256:320], pb[:, 256:320])
                    r0 = x0 + t * 128
                    P.dma(o_ckv[r0:r0 + 128, :], co[:, 0:256])
                    P.dma(o_kr[r0:r0 + 128, :], co[:, 256:320])
            if nb is not None:
                if 2 <= s <= 5:
                    prepA(nb, s - 2)
                if 3 <= s <= 6:
                    prepB(nb, s - 3)

    def ctx_keys():
        for t in range(2):
            ci = ctx_in[t]
            P.dma(ci[:, 0:256], cache_ckv[t * 128:(t + 1) * 128, :])
            P.dma(ci[:, 256:320], cache_kr[t * 128:(t + 1) * 128, :])
            for c in range(2):
                pb = bank()
                P.tr(pb[:, 0:128], ci[:, c * 128:(c + 1) * 128], identf.v())
                P.cp('dve', ckvn[:, c, t * 128:(t + 1) * 128], pb[:, 0:128])
            pb = bank()
            P.tr(pb[0:64, 0:128], ci[:, 256:320], identf.v())
            P.cp('dve', krraw[0:64, t * 128:(t + 1) * 128], pb[0:64, 0:128])
            P.act(sqkr[0:64, t * 128:(t + 1) * 128], pb[0:64, 0:128], AF.Square)
        kv_heads(256, False, LS)

    p1_list = LIMITS.get('p1_blocks', list(range(NBLK)))
    work[:] = [(b_, s_) for (b_, s_) in work if b_ in p1_list and s_ < LIMITS.get('p1_slabs', 99)]
    for t in range(4):
        prepA(p1_list[0], t)
        if t == 0:
            issue_next_slab()
        prepB(p1_list[0], t)
    for bi, b in enumerate(p1_list):
        phase1_block(b, p1_list[bi + 1] if bi + 1 < len(p1_list) else None)
        if b == NSB - 1:
            ctx_keys()

    P.barrier()
    bump[0] = persist_mark
    if stop_after == 'p1':
        P.emit()
        st.close()
        return nc

    LKMAX = LS + PAST
    stg_f = [alloc(f"stgf{i}", [128, 2048], F32) for i in range(3)]
    stg_b = [alloc(f"stgb{i}", [128, 2048], BF16) for i in range(3)]
    wout_pieces = convert_pieces(w_out, WOUT, D, D)
    kvs = [dict(kn=alloc(f"kn_sb{j}", [128, LKMAX], BF16), kr=alloc(f"kr_sb{j}", [128, LKMAX], BF16),
                v=alloc(f"v_sb{j}", [128, LKMAX // 128, 128], BF16)) for j in range(2)]
    qn_sb = [alloc(f"qn_sb{i}", [128, 512], BF16) for i in range(2)]
    qr_sb = [alloc(f"qr_sb{i}", [128, 512], BF16) for i in range(2)]
    pT = [alloc(f"pT{i}", [128, 512], BF16) for i in range(4)]
    acc = [[alloc(f"acc{i}_{j}", [128, 512], F32) for j in range(2)] for i in range(2)]
    onesf = alloc("onesf", [128, 128], F32)
    rD = alloc("rD", [128, 512], F32)
    on = alloc("on", [128, 512], F32)
    ga_sb = [alloc(f"ga_sb{i}", [128, 512], BF16) for i in range(3)]
    mixo = [alloc(f"mixo{i}", [128, 512], BF16) for i in range(2)]
    SCALE = 192.0 ** -0.5
    P.memset('pool', onesf.v(), 1.0)
    for j in range(2):
        P.memset('pool', kvs[j]['kr'][64:128, :], 0.0)
        P.memset('pool', qr_sb[j][64:128, :], 0.0)

    seqs = [(0, LS, 0, LS + PAST, 512)] + [(LS + s_ * LP, LP, LS + PAST + s_ * LP, LP, 256) for s_ in range(NPS)]
    units = []
    for (q0, Lq, k0, Lk, QB) in seqs:
        for h in range(H):
            for qb in range(Lq // QB):
                units.append((q0, Lq, k0, Lk, QB, h, qb))
    hc = [0]
    kvmap = {}

    def issue_loads(idx):
        q0, Lq, k0, Lk, QB, h, qb = units[idx]
        if qb == 0:
            kv = kvs[hc[0] % 2]
            hc[0] += 1
            kvmap[(q0, h)] = kv
            nkt = Lk // 128
            P.dma(kv['kn'][:, 0:Lk], KN[h][:, k0:k0 + Lk])
            P.dma(kv['kr'][0:64, 0:Lk], KR[h][:, k0:k0 + Lk])
            P.dma(kv['v'][:, 0:nkt, :], VV[k0:k0 + Lk, h * 128:(h + 1) * 128].re("(t p) e -> p t e", p=128))
        i = idx % 2
        qs = q0 + qb * QB
        P.dma(qn_sb[i][:, 0:QB], QN[h][:, qs:qs + QB])
        P.dma(qr_sb[i][0:64, 0:QB], QR[h][:, qs:qs + QB])
        P.dma(ga_sb[idx % 3][:, 0:QB], GA[h * 128:(h + 1) * 128, qs:qs + QB])

    def make_epilogue(idx):
        q0, Lq, k0, Lk, QB, h, qb = units[idx]
        i = idx % 2
        qs = q0 + qb * QB
        Ob = psb[3 + i]
        Db = psb[5 + i]

        def epi():
            P.mm(Db[:, 0:QB], onesf.v(), acc[i][0][:, 0:QB], start=(Lk // 128 < 8), stop=False)
            P.mm(Db[:, 0:QB], onesf.v(), acc[i][1][:, 0:QB], start=False, stop=True)
            P.recip(rD[:, 0:QB], Db[:, 0:QB])
            P.tt('dve', on[:, 0:QB], Ob[:, 0:QB], rD[:, 0:QB], ALU.mult)
            P.tt('pool', mixo[i][:, 0:QB], on[:, 0:QB], ga_sb[idx % 3][:, 0:QB], ALU.mult)
            P.dma(MIX[h * 128:(h + 1) * 128, qs:qs + QB], mixo[i][:, 0:QB])
        return epi

    issue_loads(0)
    pending = [None]
    for idx, u in enumerate(units):
        q0, Lq, k0, Lk, QB, h, qb = u
        nkt = Lk // 128
        i = idx % 2
        kv = kvmap[(q0, h)]
        if idx + 1 < len(units):
            issue_loads(idx + 1)
        if wout_pieces:
            wout_pieces.pop(0)()
        Ob = psb[3 + i]

        SB_ = [0, 1, 2, 7]

        def smm(kt, kv=kv, i=i, QB=QB):
            sb_ = psb[SB_[kt % 4]]
            P.mm(sb_[:, 0:QB], kv['kn'][:, kt * 128:(kt + 1) * 128], qn_sb[i][:, 0:QB], start=True, stop=False)
            P.mm(sb_[:, 0:QB], kv['kr'][:, kt * 128:(kt + 1) * 128], qr_sb[i][:, 0:QB], start=False, stop=True)
        LA = 2
        for k0_ in range(min(LA, nkt)):
            smm(k0_)
        dstart = [False]
        for kt in range(nkt):
            if kt + LA < nkt:
                smm(kt + LA)
            p = pT[kt % 4]
            P.act(p[:, 0:QB], psb[SB_[kt % 4]][:, 0:QB], AF.Exp, scale=SCALE)
            P.mm(Ob[:, 0:QB], kv['v'][:, kt, :], p[:, 0:QB], start=(kt == 0), stop=(kt == nkt - 1))
            if kt % 8 == 7:
                P.mm(psb[5 + i][:, 0:QB], onesb.v(), p[:, 0:QB], start=(not dstart[0]), stop=False)
                dstart[0] = True
            else:
                a_ = acc[i][kt % 2]
                if kt < 2:
                    P.cp('dve', a_[:, 0:QB], p[:, 0:QB])
                else:
                    P.tt('dve', a_[:, 0:QB], a_[:, 0:QB], p[:, 0:QB], ALU.add)
            if kt == min(3, nkt - 1) and pending[0] is not None:
                pending[0]()
                pending[0] = None
        pending[0] = make_epilogue(idx)
    pending[0]()
    while wout_pieces:
        wout_pieces.pop(0)()

    P.barrier()
    bump[0] = persist_mark
    if stop_after == 'attn':
        P.emit()
        st.close()
        return nc

    NCMAX = LS // 128
    rset = [dict(rk=alloc(f"rk_sb{j}", [128, LS], BF16), rq=alloc(f"rq_sb{j}", [128, LS], BF16),
                 rv=alloc(f"rv_sb{j}", [128, NCMAX, 256], BF16)) for j in range(2)]
    KDF = alloc("KDF", [128, NCMAX, 128], BF16)
    KDB = alloc("KDB", [128, NCMAX, 128], BF16)
    snapF = alloc("snapF", [128, NCMAX, 256], BF16)
    snapB = alloc("snapB", [128, NCMAX, 256], BF16)
    SfP = [alloc(f"SfP{i}", [128, 256], F32) for i in range(2)]
    SbP = [alloc(f"SbP{i}", [128, 256], F32) for i in range(2)]
    Sx = [alloc(f"Sx{i}", [128, 256], F32) for i in range(4 * (NPS - 1))]
    smt = [alloc(f"smt{i}", [128, 128], BF16) for i in range(3)]
    qd = [alloc(f"qd{i}", [128, 2, 512], BF16) for i in range(2)]
    rsq = [alloc(f"rsq{i}", [128, 512], BF16) for i in range(4)]
    rms_ = [alloc(f"rms{i}", [128, 512], F32) for i in range(2)]
    rff = [alloc(f"rff{i}", [128, 512], F32) for i in range(2)]
    rno = [alloc(f"rno{i}", [128, 512], F32) for i in range(4)]
    gb_sb = [alloc(f"gb_sb{i}", [128, 2, 512], BF16) for i in range(2)]
    rmo = [alloc(f"rmo{i}", [128, 512], BF16) for i in range(4)]
    rc_ = {'sm': 0, 'g': 0, 'rsq': 0, 'rmo': 0}

    rseqs = [(0, LS, True, 1), (LS, NPS * LP, False, NPS)]
    runits = [(q0, L, smp, nsq, h) for (q0, L, smp, nsq) in rseqs for h in range(RH)]

    def ret_loads(u):
        q0, L, smp, sidx, h = runits[u]
        n = L // 128
        bf_ = rset[u % 2]
        P.dma(bf_['rk'][:, 0:L], RK[h][:, q0:q0 + L])
        P.dma(bf_['rv'][:, 0:n, :], RV[q0:q0 + L, h * 256:(h + 1) * 256].re("(t p) e -> p t e", p=128))
        P.dma(bf_['rq'][:, 0:L], RQ[h][:, q0:q0 + L])

    def ret_unit(u):
        q0, L, sample, nseq, h = runits[u]
        n = L // 128
        bf_ = rset[u % 2]
        rk_sb, rq_sb, rv_sb = bf_['rk'], bf_['rq'], bf_['rv']
        kdf = dcols[:, h:h + 1]
        kdb = dcols[:, 4 + h:5 + h]
        cdf = dcols[:, 8 + h:9 + h]
        cdb = dcols[:, 12 + h:13 + h]
        nsub = n // nseq
        SfC = [[SfP[0], SfP[1]]] + [[Sx[4 * j_], Sx[4 * j_ + 1]] for j_ in range(nseq - 1)]
        SbC = [[SbP[0], SbP[1]]] + [[Sx[4 * j_ + 2], Sx[4 * j_ + 3]] for j_ in range(nseq - 1)]
        for j_ in range(nseq):
            if sample:
                P.dma(SfC[j_][0].v(), st_f[h])
                P.dma(SbC[j_][0].v(), st_b[h])
            else:
                P.memset('pool', SfC[j_][0].v(), 0.0)
                P.memset('pool', SbC[j_][0].v(), 0.0)
        if u + 1 < len(runits):
            ret_loads(u + 1)
        for g8 in range((n + 7) // 8):
            nb = min(8, n - g8 * 8)
            bi = 7 if g8 % 2 == 0 else 0
            pbv = View(psb_bf[bi], psb[bi])
            for j in range(nb):
                c = g8 * 8 + j
                P.tr(pbv[:, j * 128:(j + 1) * 128], rk_sb[:, c * 128:(c + 1) * 128], identb.v())
            src = pbv[:, 0:nb * 128].re("p (c d) -> p c d", d=128)
            P.ts('dve', KDF[:, g8 * 8:g8 * 8 + nb, :], src, kdf, None, ALU.mult)
            P.ts('dve', KDB[:, g8 * 8:g8 * 8 + nb, :], src, kdb, None, ALU.mult)
        ui = 0
        for k in range(nsub):
            for j_ in range(nseq):
                cf, cb_ = j_ * nsub + k, j_ * nsub + (nsub - 1 - k)
                Sf_cur, Sf_n = SfC[j_][k % 2], SfC[j_][(k + 1) % 2]
                Sb_cur, Sb_n = SbC[j_][k % 2], SbC[j_][(k + 1) % 2]
                P.cp('act', snapF[:, cf, :], Sf_cur.v())
                upf = psb[3 + ui % 2]
                P.mm(upf[:, 0:256], KDF[:, cf, :], rv_sb[:, cf, :])
                P.stt('dve', Sf_n.v(), Sf_cur.v(), cdf, upf[:, 0:256], ALU.mult, ALU.add)
                P.cp('act', snapB[:, cb_, :], Sb_cur.v())
                upb = psb[5 + ui % 2]
                P.mm(upb[:, 0:256], KDB[:, cb_, :], rv_sb[:, cb_, :])
                P.stt('dve', Sb_n.v(), Sb_cur.v(), cdb, upb[:, 0:256], ALU.mult, ALU.add)
                ui += 1
        if not sample:
            for j_ in range(nseq):
                P.dma(o_sf[j_][h], SfC[j_][nsub % 2].v())
                P.dma(o_sb[j_][h], SbC[j_][nsub % 2].v())
        ngrp = (n + 3) // 4
        gbase = rc_['g']
        rc_['g'] += ngrp

        def prep_group(g):
            c_lo = g * 4
            nc_ = min(4, n - c_lo)
            W = nc_ * 128
            gi = (gbase + g) % 2
            tok0 = q0 + c_lo * 128
            qv = rq_sb[:, c_lo * 128: c_lo * 128 + W].re("p (c i) -> p c i", i=128)
            P.tt('dve', qd[gi][:, 0, 0:W].re("p (c i) -> p c i", i=128), qv,
                 QDF[:, h, :].un(1).bc([128, nc_, 128]), ALU.mult)
            P.tt('pool', qd[gi][:, 1, 0:W].re("p (c i) -> p c i", i=128), qv,
                 QDB[:, h, :].un(1).bc([128, nc_, 128]), ALU.mult)
            P.dma(gb_sb[gi][:, :, 0:W], GB[h * 256:(h + 1) * 256, tok0:tok0 + W].re("(e p) t -> p e t", p=128))

        prep_group(0)
        for g in range(ngrp):
            c_lo = g * 4
            nc_ = min(4, n - c_lo)
            W = nc_ * 128
            gi = (gbase + g) % 2
            tok0 = q0 + c_lo * 128
            if g + 1 < ngrp:
                prep_group(g + 1)
            OA = psb[1 + 2 * gi]
            OB = psb[2 + 2 * gi]
            def st_mm(c):
                cs = slice(c * 128, (c + 1) * 128)
                sp_ = psb[7 if c % 2 == 0 else 0]
                P.mm(sp_[:, 0:128], rk_sb[:, cs], rq_sb[:, cs])
                sm_ = smt[c % 3]
                P.tt('dve', sm_.v(), sp_[:, 0:128], Mh[:, h, :], ALU.mult)

            if g == 0:
                st_mm(0)
            for ci in range(nc_):
                c = c_lo + ci
                if c + 1 < n:
                    st_mm(c + 1)
                sm = smt[c % 3]
                for e2, Ot in enumerate((OA, OB)):
                    es = slice(e2 * 128, (e2 + 1) * 128)
                    oc = Ot[:, ci * 128:(ci + 1) * 128]
                    P.mm(oc, rv_sb[:, c, es], sm.v(), start=True, stop=False)
                    P.mm(oc, snapF[:, c, es], qd[gi][:, 0, ci * 128:(ci + 1) * 128], start=False, stop=False)
                    P.mm(oc, snapB[:, c, es], qd[gi][:, 1, ci * 128:(ci + 1) * 128], start=False, stop=True)
            osb = [rno[2 * gi], rno[2 * gi + 1]]
            P.cp('act', osb[0][:, 0:W], OA[:, 0:W])
            P.cp('act', osb[1][:, 0:W], OB[:, 0:W])
            sqs = []
            for e2 in range(2):
                r = rsq[rc_['rsq'] % 4]
                rc_['rsq'] += 1
                P.act(r[:, 0:W], osb[e2][:, 0:W], AF.Square)
                sqs.append(r)
            sb_ = psb[5 + gi]
            P.mm(sb_[:, 0:W], onesb.v(), sqs[0][:, 0:W], start=True, stop=False)
            P.mm(sb_[:, 0:W], onesb.v(), sqs[1][:, 0:W], start=False, stop=True)
            ms = rms_[gi]
            P.act(ms[:, 0:W], sb_[:, 0:W], AF.Ln, scale=1.0 / 256, bias=epsc[:, 0:1])
            f = rff[gi]
            P.act(f[:, 0:W], ms[:, 0:W], AF.Exp, scale=-0.5)
            for e2 in range(2):
                no = osb[e2]
                P.stt('dve', no[:, 0:W], no[:, 0:W], vcol(V_GNW + h * 2 + e2), f[:, 0:W], ALU.mult, ALU.mult)
                mo_ = rmo[rc_['rmo'] % 4]
                rc_['rmo'] += 1
                P.tt('pool', mo_[:, 0:W], no[:, 0:W], gb_sb[gi][:, e2, 0:W], ALU.mult)
                r0 = 1024 + h * 256 + e2 * 128
                P.dma(MIX[r0:r0 + 128, tok0:tok0 + W], mo_[:, 0:W])

    ret_loads(0)
    for u in range(len(runits)):
        ret_unit(u)

    P.barrier()
    bump[0] = persist_mark
    if stop_after == 'ret':
        P.emit()
        st.close()
        return nc

    mx = alloc("mx", [128, 16, TB], BF16)
    Gcur = alloc("Gcur", [128, D], F32)
    xo = [alloc(f"xo{i}", [128, 4, 512], F32) for i in range(2)]
    yt_ = [alloc(f"yt{i}", [128, 512], F32) for i in range(3)]
    WOv = WOUT.v().re("(c p) n -> p c n", p=128)
    MIXv = MIX.v().re("(c p) t -> p c t", p=128)
    yi = [0]
    steps = [(b, db) for b in range(NBLK) for db in range(4)]
    mxs = [mx, alloc("mx2", [128, 16, TB], BF16)]

    woq = [alloc(f"woq{i}", [128, 16, 512], BF16) for i in range(4)]

    def p3_mx_load(b):
        P.dma(mxs[b % 2].v(), MIXv[:, :, b * TB:(b + 1) * TB])

    def p3_loads(k):
        b, db = steps[k]
        sample = b < NSB
        g0 = b * TB
        xsrc = xs if sample else xp
        x0 = g0 if sample else g0 - LS
        P.dma(xo[k % 2].v(), xsrc[x0:x0 + TB, db * 512:(db + 1) * 512].re("(t p) n -> p t n", p=128))

    P.dma(Gcur.v(), MODS[2 * 2 + 0])
    p3_mx_load(0)
    P.dma(woq[0].v(), WOv[:, :, 0:512])
    p3_loads(0)
    for db in range(1, 4):
        P.dma(woq[db].v(), WOv[:, :, db * 512:(db + 1) * 512])
    for k, (b, db) in enumerate(steps):
        sample = b < NSB
        g0 = b * TB
        ydst = y_s if sample else y_p
        x0 = g0 if sample else g0 - LS
        if b == NSB and db == 0:
            P.dma(Gcur.v(), MODS[2 * 2 + 1])
        if k + 1 < len(steps):
            p3_loads(k + 1)
        if db == 0 and b + 1 < NBLK:
            p3_mx_load(b + 1)
        w_ = woq[db]
        xo_ = xo[k % 2]
        mx_ = mxs[b % 2]
        for t in range(4):
            pb = bank(0, 8, 'o')
            for c in range(16):
                P.mm(pb.v(), mx_[:, c, t * 128:(t + 1) * 128], w_[:, c, :], start=(c == 0), stop=(c == 15))
            y_ = yt_[yi[0] % 3]
            yi[0] += 1
            P.tt('dve', y_.v(), pb.v(), Gcur[:, db * 512:(db + 1) * 512], ALU.mult)
            P.tt('pool', y_.v(), y_.v(), xo_[:, t, :], ALU.add)
            P.dma(ydst[x0 + t * 128: x0 + (t + 1) * 128, db * 512:(db + 1) * 512], y_.v())

    P.emit()
    st.close()
    return nc


_CACHE = {}


def _rope_tables():
    f64 = np.float64
    pos = np.arange(LS)
    row = (pos // 64).astype(f64)
    col = (pos % 64).astype(f64)
    fr16 = 10000.0 ** (-(np.arange(16, dtype=f64)) / 16.0)
    fr64 = 10000.0 ** (-(np.arange(64, dtype=f64)) / 64.0)
    ang_row = row[None, :] * fr16[:, None]
    ang_col = col[None, :] * fr16[:, None]
    ang_ret = pos.astype(f64)[None, :] * fr64[:, None]
    mcos = np.concatenate([np.cos(ang_row), np.cos(ang_row), np.cos(ang_col), np.cos(ang_col)], 0)
    msin = np.concatenate([-np.sin(ang_row), np.sin(ang_row), -np.sin(ang_col), np.sin(ang_col)], 0)
    rcos = np.concatenate([np.cos(ang_ret), np.cos(ang_ret)], 0)
    rsin = np.concatenate([-np.sin(ang_ret), np.sin(ang_ret)], 0)
    return [np.ascontiguousarray(a, dtype=np.float32) for a in (mcos, msin, rcos, rsin)]


def _consts():
    if 'c' in _CACHE:
        return _CACHE['c']
    f32 = np.float32
    ident = np.eye(128, dtype=f32)
    perm = np.zeros((128, 128), f32)
    for m in range(128):
        perm[(m + 64) % 128, m] = 1.0
    j = np.arange(128)[:, None].astype(f32)
    i = np.arange(128)[None, :].astype(f32)
    rt = np.zeros((128, NRT), f32)
    rt[:, RT_D1:RT_D1 + 128] = np.maximum(i - j, 0)
    rt[:, RT_D2:RT_D2 + 128] = np.maximum(j - i, 0)
    rt[:, RT_QDF:RT_QDF + 128] = i + 1
    rt[:, RT_QDB:RT_QDB + 128] = 128 - i
    rt[:, RT_KDF] = 127 - j[:, 0]
    rt[:, RT_KDB] = j[:, 0]
    rt[:, RT_C] = 128
    mcos, msin, rcos, rsin = _rope_tables()
    c = dict(ident=ident, permret=perm, rettab=rt, mcos=mcos, msin=msin, rcos=rcos, rsin=rsin)
    _CACHE['c'] = c
    return c


def _prep_shared(inp):
    f32 = np.float32
    w_in = np.asarray(inp['w_in'], f32)[0]
    o = np.cumsum([0, 384, 256, 64, 1024, 512, 512, 1024, 1024])
    q_lat, ckv, krope, g_a, rq, rk, rv, g_b = [w_in[:, o[k]:o[k + 1]] for k in range(8)]
    p64 = np.array([(i + 16) if (i % 32) < 16 else (i - 16) for i in range(64)])
    w_in2 = np.concatenate([q_lat, krope, krope[:, p64], ckv, g_a, rq, rk, g_b, rv, ckv, krope], axis=1)
    assert w_in2.shape[1] == WIN_COLS
    w_uq = np.asarray(inp['mla_w_uq'], f32)[0].reshape(384, 8, 192)
    w_uq2 = np.concatenate([w_uq[:, :, :128], w_uq[:, :, 128:], w_uq[:, :, 128:][:, :, p64]], axis=2).reshape(384, 2048)
    qw = np.asarray(inp['mla_qk_q_w'], f32)[0]
    kw = np.asarray(inp['mla_qk_k_w'], f32)[0]
    vecs = np.zeros((128, NV), f32)
    vecs[:, V_QNW:V_QNW + 3] = np.asarray(inp['mla_q_norm_w'], f32)[0].reshape(3, 128).T
    vecs[:, V_KVW:V_KVW + 2] = np.asarray(inp['mla_kv_norm_w'], f32)[0].reshape(2, 128).T
    vecs[:, V_WQN] = qw[:128]
    vecs[:64, V_WQR] = qw[128:]
    vecs[:64, V_WQP] = qw[128:][p64]
    vecs[:, V_WKN] = kw[:128]
    vecs[:64, V_WKR] = kw[128:]
    vecs[:64, V_WKP] = kw[128:][p64]
    vecs[:, V_GNW:V_GNW + 8] = np.asarray(inp['ret_gn_w'], f32)[0].reshape(8, 128).T
    lgd = np.concatenate([np.asarray(inp['ret_log_decay_fwd'], f32)[0], np.asarray(inp['ret_log_decay_bwd'], f32)[0]])[None, :]
    sh = dict(
        w_mod=np.ascontiguousarray(np.asarray(inp['w_mod'], f32)[0]),
        b_mod=np.ascontiguousarray(np.asarray(inp['b_mod'], f32)[0][None, :]),
        norm_w=np.ascontiguousarray(np.asarray(inp['norm_w'], f32)[0][None, :]),
        w_in=np.ascontiguousarray(w_in2),
        w_uq=np.ascontiguousarray(w_uq2),
        w_uk=np.ascontiguousarray(np.asarray(inp['mla_w_uk'], f32)[0]),
        w_uv=np.ascontiguousarray(np.asarray(inp['mla_w_uv'], f32)[0]),
        w_out=np.ascontiguousarray(np.asarray(inp['w_out'], f32)[0]),
        vecs=vecs,
        kvw_row=np.ascontiguousarray(np.asarray(inp['mla_kv_norm_w'], f32)[0][None, :]),
        lgd=np.ascontiguousarray(lgd),
    )
    sh.update(_consts())
    return sh


def make_in_maps(inp, cores=range(8)):
    f32 = np.float32
    sh = _prep_shared(inp)
    xs = np.asarray(inp['x_sample'], f32)
    xp = np.asarray(inp['x_prompt'], f32)
    c = np.asarray(inp['c'], f32)
    c_ctx = np.asarray(inp['c_ctx'], f32)
    maps = []
    for i in cores:
        m = dict(sh)
        m['xs'] = np.ascontiguousarray(xs[i])
        m['xp'] = np.ascontiguousarray(xp[4 * i:4 * i + 4].reshape(NPS * LP, D))
        cv = np.stack([c[i], c_ctx], 0)
        m['cT'] = np.ascontiguousarray(cv.reshape(2, 16, 128).transpose(2, 0, 1).reshape(128, 32))
        m['cache_ckv'] = np.ascontiguousarray(np.asarray(inp['cache_mla_ckv'], f32)[i, 0])
        m['cache_kr'] = np.ascontiguousarray(np.asarray(inp['cache_mla_krope'], f32)[i, 0])
        m['st_f'] = np.ascontiguousarray(np.asarray(inp['state_ret_fwd'], f32)[i, 0])
        m['st_b'] = np.ascontiguousarray(np.asarray(inp['state_ret_bwd'], f32)[i, 0])
        maps.append(m)
    return maps


def kernel(**inputs):
    if 'nc' not in _CACHE:
        _CACHE['nc'] = build_program(False)
    nc = _CACHE['nc']
    in_maps = make_in_maps(inputs)
    res = run_bass_kernel_spmd(nc, in_maps, core_ids=list(range(8)))
    R = res.results
    y_p = np.concatenate([r['y_p'].reshape(NPS, LP, D) for r in R], 0)
    y_s = np.stack([r['y_s'] for r in R], 0)
    ckv = np.concatenate([r['o_ckv'].reshape(NPS, 1, LP, 256) for r in R], 0)
    kr = np.concatenate([r['o_kr'].reshape(NPS, 1, LP, 64) for r in R], 0)
    sf = np.concatenate([r['o_sf'].reshape(NPS, 1, RH, 128, 256) for r in R], 0)
    sb = np.concatenate([r['o_sb'].reshape(NPS, 1, RH, 128, 256) for r in R], 0)
    return (y_p.astype(np.float32), y_s.astype(np.float32), ckv.astype(np.float32), kr.astype(np.float32),
            sf.astype(np.float32), sb.astype(np.float32))
```

```python
import numpy as np
from contextlib import ExitStack
import concourse.bass as bass
import concourse.mybir as mybir
from concourse.bass_utils import run_bass_kernel_spmd

F32 = mybir.dt.float32
BF16 = mybir.dt.bfloat16
F32R = mybir.dt.float32r
ALU = mybir.AluOpType
AF = mybir.ActivationFunctionType

EPS = 1e-6
NDSEM = 40
ENGS = ['pe', 'act', 'dve', 'pool', 'sp']
DEBUG = False
LIMITS = {}


class Tile:
    __slots__ = ('ap', 'name', 'lw', 'rd', 'excl')

    def __init__(self, ap, name, excl=False):
        self.ap = ap
        self.name = name
        self.lw = None
        self.rd = {}
        self.excl = excl

    def __getitem__(self, k):
        return View(self.ap[k], self)

    def v(self):
        return View(self.ap, self)


class View:
    __slots__ = ('ap', 'tile')

    def __init__(self, ap, tile):
        self.ap = ap
        self.tile = tile

    def __getitem__(self, k):
        return View(self.ap[k], self.tile)

    def re(self, pat, **kw):
        return View(self.ap.rearrange(pat, **kw), self.tile)

    def bc(self, shape):
        return View(self.ap.to_broadcast(list(shape)), self.tile)

    def un(self, axis):
        return View(self.ap.unsqueeze(axis), self.tile)


class Op:
    __slots__ = ('id', 'eng', 'fn', 'deps', 'sig', 'dma', 'dsem', 'dval')


class Prog:
    def __init__(self, nc):
        self.nc = nc
        self.ops = []
        self.by_eng = {e: [] for e in ENGS}
        self.ndma = 0
        self.dma_ops = []
        self.pending = {e: set() for e in ENGS}
        self.last_compute = {e: None for e in ENGS}

    def add(self, eng, fn, reads=(), writes=(), dma=False):
        op = Op()
        op.id = len(self.ops)
        op.eng = eng
        op.fn = fn
        op.dma = dma
        op.sig = None
        deps = set()
        for t in reads:
            if t.lw is not None:
                deps.add(t.lw)
            if t.excl:
                for k, v in t.rd.items():
                    if k != eng and k != 'dma':
                        deps.add(v)
        for t in writes:
            if t.lw is not None:
                deps.add(t.lw)
            for k, v in t.rd.items():
                if k == 'dma':
                    deps.update(v)
                else:
                    deps.add(v)
        deps |= self.pending[eng]
        self.pending[eng] = set()
        if dma:
            j = self.ndma
            self.ndma += 1
            op.dsem = j % NDSEM
            op.dval = 16 * (j // NDSEM + 1)
            if j >= NDSEM:
                deps.add(self.dma_ops[j - NDSEM].id)
            self.dma_ops.append(op)
        deps.discard(op.id)
        if (not dma) and eng == 'pe':
            deps = {d for d in deps if self.ops[d].dma or self.ops[d].eng != 'pe'}
        op.deps = deps
        for t in reads:
            if dma:
                t.rd.setdefault('dma', []).append(op.id)
            else:
                t.rd[eng] = op.id
        for t in writes:
            t.lw = op.id
            t.rd = {}
        self.ops.append(op)
        self.by_eng[eng].append(op)
        if not dma:
            self.last_compute[eng] = op.id
        return op

    def barrier(self):
        s = set()
        for e in ENGS:
            if self.last_compute[e] is not None:
                s.add(self.last_compute[e])
        for op in self.dma_ops[-NDSEM:]:
            s.add(op.id)
        for e in ENGS:
            self.pending[e] |= s

    def mm(self, out, lhsT, rhs, start=True, stop=True):
        o, l, r = out.ap, lhsT.ap, rhs.ap
        rd = [lhsT.tile, rhs.tile] + ([] if start else [out.tile])
        self.add('pe', lambda e: e.matmul(o, l, r, start=start, stop=stop), rd, [out.tile])

    def tr(self, out, in_, ident):
        o, i, d = out.ap, in_.ap, ident.ap
        self.add('pe', lambda e: e.transpose(o, i, d), [in_.tile, ident.tile], [out.tile])

    def act(self, out, in_, func, bias=None, scale=None, accum=None):
        o, i = out.ap, in_.ap
        kw = {}
        rd = [in_.tile]
        wr = [out.tile]
        if bias is not None:
            if isinstance(bias, View):
                kw['bias'] = bias.ap
                rd.append(bias.tile)
            else:
                kw['bias'] = bias
        if scale is not None:
            if isinstance(scale, View):
                kw['scale'] = scale.ap
                rd.append(scale.tile)
            else:
                kw['scale'] = scale
        if accum is not None:
            kw['accum_out'] = accum.ap
            wr.append(accum.tile)
        self.add('act', lambda e: e.activation(o, i, func, **kw), rd, wr)

    def tt(self, eng, out, a, b, op):
        o, x, y = out.ap, a.ap, b.ap
        self.add(eng, lambda e: e.tensor_tensor(out=o, in0=x, in1=y, op=op), [a.tile, b.tile], [out.tile])

    def ts(self, eng, out, a, s1, s2, op0, op1=None):
        o, x = out.ap, a.ap
        rd = [a.tile]

        def cv(s):
            if isinstance(s, View):
                rd.append(s.tile)
                return s.ap
            return s
        v1 = cv(s1)
        v2 = cv(s2) if s2 is not None else None
        if op1 is None:
            self.add(eng, lambda e: e.tensor_scalar(o, x, v1, None, op0), rd, [out.tile])
        else:
            self.add(eng, lambda e: e.tensor_scalar(o, x, v1, v2, op0, op1), rd, [out.tile])

    def stt(self, eng, out, in0, scalar, in1, op0, op1):
        o, x, y = out.ap, in0.ap, in1.ap
        rd = [in0.tile, in1.tile]
        if isinstance(scalar, View):
            rd.append(scalar.tile)
            sc = scalar.ap
        else:
            sc = scalar
        self.add(eng, lambda e: e.scalar_tensor_tensor(out=o, in0=x, scalar=sc, in1=y, op0=op0, op1=op1),
                 rd, [out.tile])

    def cp(self, eng, out, in_, scale=None):
        o, i = out.ap, in_.ap
        if eng == 'act':
            if scale is None:
                self.add('act', lambda e: e.activation(o, i, AF.Copy), [in_.tile], [out.tile])
            else:
                self.add('act', lambda e: e.activation(o, i, AF.Copy, scale=scale), [in_.tile], [out.tile])
        else:
            if scale is None:
                self.add(eng, lambda e: e.tensor_copy(out=o, in_=i), [in_.tile], [out.tile])
            else:
                self.add(eng, lambda e: e.tensor_scalar(o, i, scale, None, ALU.mult), [in_.tile], [out.tile])

    def recip(self, out, in_):
        o, i = out.ap, in_.ap
        self.add('dve', lambda e: e.reciprocal(out=o, in_=i), [in_.tile], [out.tile])

    def memset(self, eng, out, val):
        o = out.ap
        self.add(eng, lambda e: e.memset(o, val), [], [out.tile])

    def dma(self, out, in_, q='sp'):
        o, i = out.ap, in_.ap
        self.add(q, lambda e: e.dma_start(out=o, in_=i), [in_.tile], [out.tile], dma=True)

    def emit(self):
        nc = self.nc
        self.barrier()
        self.add('sp', lambda e: e.nop(), [], [])
        needed = set()
        for op in self.ops:
            needed |= op.deps
        cnt = {e: 0 for e in ENGS}
        for op in self.ops:
            if op.dma:
                continue
            if op.id in needed:
                cnt[op.eng] += 1
                op.sig = cnt[op.eng]
        with ExitStack() as st:
            sems = {e: st.enter_context(nc.semaphore(f"sem_{e}")) for e in ENGS}
            dsems = [st.enter_context(nc.semaphore(f"dsem{i}")) for i in range(NDSEM)]
            block = st.enter_context(nc.Block())
            ops = self.ops

            def run_engine(ename, h):
                waited = {}
                for op in self.by_eng[ename]:
                    req = {}
                    for d in op.deps:
                        p = ops[d]
                        if p.dma:
                            key = ('d', p.dsem)
                            val = p.dval
                        else:
                            key = ('e', p.eng)
                            val = p.sig
                        if val > req.get(key, 0):
                            req[key] = val
                    for key, val in req.items():
                        if waited.get(key, 0) < val:
                            sem = dsems[key[1]] if key[0] == 'd' else sems[key[1]]
                            h.wait_ge(sem, val)
                            waited[key] = val
                    ins = op.fn(h)
                    if op.dma:
                        ins.then_inc(dsems[op.dsem], 16)
                    elif op.sig is not None:
                        ins.then_inc(sems[ename], 1)

            @block.tensor
            def _(h):
                run_engine('pe', h)

            @block.scalar
            def _(h):
                run_engine('act', h)

            @block.vector
            def _(h):
                run_engine('dve', h)

            @block.gpsimd
            def _(h):
                run_engine('pool', h)

            @block.sync
            def _(h):
                run_engine('sp', h)


D = 2048
LS = 4096
LP = 256
NPS = 4
NTOK = LS + NPS * LP
PAST = 256
NKEY = NTOK + PAST
TB = 512
NBLK = NTOK // TB
NSB = LS // TB
H = 8
RH = 4
WIN_COLS = 5184
SLABS = [(0, 512), (512, 256), (768, 512), (1280, 512), (1792, 512), (2304, 512),
         (2816, 512), (3328, 512), (3840, 512), (4352, 512), (4864, 320)]
ARENA_BYTES = 207 * 1024

V_QNW = 0
V_KVW = 3
V_WQN = 5
V_WQR = 6
V_WQP = 7
V_WKN = 8
V_WKR = 9
V_WKP = 10
V_GNW = 11
NV = 19
RT_D1 = 0
RT_D2 = 128
RT_QDF = 256
RT_QDB = 384
RT_KDF = 512
RT_KDB = 513
RT_C = 514
NRT = 515


def build_program(debug=False, stop_after=None):
    nc = bass.Bass("TRN2", target_bir_lowering=False)
    P = Prog(nc)

    def din(name, shape, dt=F32):
        return Tile(nc.dram_tensor(name, list(shape), dt, kind="ExternalInput").ap(), name)

    def dout(name, shape, dt=F32):
        return Tile(nc.dram_tensor(name, list(shape), dt, kind="ExternalOutput").ap(), name)

    def dscr(name, shape, dt=BF16):
        kind = "ExternalOutput" if debug else "Internal"
        return Tile(nc.dram_tensor(name, list(shape), dt, kind=kind).ap(), name)

    xs = din("xs", [LS, D])
    xp = din("xp", [NPS * LP, D])
    cT = din("cT", [128, 32])
    w_mod = din("w_mod", [D, 3 * D])
    b_mod = din("b_mod", [1, 3 * D])
    norm_w = din("norm_w", [1, D])
    w_in = din("w_in", [D, WIN_COLS])
    w_uq = din("w_uq", [384, 2048])
    w_uk = din("w_uk", [256, 1024])
    w_uv = din("w_uv", [256, 1024])
    w_out = din("w_out", [D, D])
    vecs_d = din("vecs", [128, NV])
    kvw_row = din("kvw_row", [1, 256])
    cache_ckv = din("cache_ckv", [PAST, 256])
    cache_kr = din("cache_kr", [PAST, 64])
    st_f = din("st_f", [RH, 128, 256])
    st_b = din("st_b", [RH, 128, 256])
    lgd = din("lgd", [1, 8])
    ident_d = din("ident", [128, 128])
    perm_d = din("permret", [128, 128])
    rettab_d = din("rettab", [128, NRT])
    mcos_d = din("mcos", [64, LS])
    msin_d = din("msin", [64, LS])
    rcos_d = din("rcos", [128, LS])
    rsin_d = din("rsin", [128, LS])
    y_s = dout("y_s", [LS, D])
    y_p = dout("y_p", [NPS * LP, D])
    o_ckv = dout("o_ckv", [NPS * LP, 256])
    o_kr = dout("o_kr", [NPS * LP, 64])
    o_sf = dout("o_sf", [NPS, RH, 128, 256])
    o_sb = dout("o_sb", [NPS, RH, 128, 256])
    WIN = dscr("WIN", [D, WIN_COLS])
    WUQ = dscr("WUQ", [384, 2048])
    WUK = dscr("WUK", [256, 1024])
    WUV = dscr("WUV", [256, 1024])
    WOUT = dscr("WOUT", [D, D])
    MODS = dscr("MODS", [6, 128, D], F32)
    QN = dscr("QN", [H, 128, NTOK])
    QR = dscr("QR", [H, 64, NTOK])
    KN = dscr("KN", [H, 128, NKEY])
    KR = dscr("KR", [H, 64, NKEY])
    VV = dscr("VV", [NKEY, 1024])
    GA = dscr("GA", [1024, NTOK])
    GB = dscr("GB", [1024, NTOK])
    RQ = dscr("RQ", [RH, 128, NTOK])
    RK = dscr("RK", [RH, 128, NTOK])
    RV = dscr("RV", [NTOK, 1024])
    MIX = dscr("MIX", [D, NTOK])

    st = ExitStack()
    arena = st.enter_context(nc.sbuf_tensor("arena", [128, ARENA_BYTES // 2], BF16))
    psh = [st.enter_context(nc.psum_tensor(f"ps{i}", [128, 512], F32)) for i in range(8)]
    psb = [Tile(h_.ap(), f"ps{i}", excl=True) for i, h_ in enumerate(psh)]
    psb_bf = [h_.bitcast(BF16).ap() for h_ in psh]

    bump = [0]

    def alloc(name, shape, dt=F32):
        esz = 4 if dt == F32 else 2
        n = 1
        for s in shape[1:]:
            n *= s
        nbytes = (n * esz + 63) // 64 * 64
        off = bump[0]
        bump[0] += nbytes
        assert bump[0] <= ARENA_BYTES, f"arena overflow at {name}: {bump[0]}"
        ap = arena[0:shape[0], off // 2: off // 2 + n * esz // 2]
        if dt == F32:
            ap = ap.bitcast(F32)
        if len(shape) == 3:
            ap = ap.rearrange("p (a b) -> p a b", a=shape[1])
        elif len(shape) == 4:
            ap = ap.rearrange("p (a b c) -> p a b c", a=shape[1], b=shape[2])
        return Tile(ap, name)

    rot = {}

    def bank(lo=0, hi=6, key='g'):
        k = (key, lo, hi)
        i = rot.get(k, lo)
        rot[k] = lo + (i - lo + 1) % (hi - lo)
        return psb[i]

    eng_rr = [0]

    identf = alloc("identf", [128, 128], F32)
    identb = alloc("identb", [128, 128], BF16)
    onesb = alloc("onesb", [128, 128], BF16)
    permb = alloc("permb", [128, 128], BF16)
    vecs = alloc("vecs", [128, NV], F32)
    epsc = alloc("epsc", [128, 1], F32)
    rtab = alloc("rtab", [128, NRT], F32)
    Mh = alloc("Mh", [128, RH, 128], F32)
    QDF = alloc("QDF", [128, RH, 128], F32)
    QDB = alloc("QDB", [128, RH, 128], F32)
    dcols = alloc("dcols", [128, 16], F32)
    nee = alloc("nee", [128, 8], F32)
    wuq = alloc("wuq", [128, 3, 2048], BF16)
    wuk = alloc("wuk", [128, 2, 1024], BF16)
    wuv = alloc("wuv", [128, 2, 1024], BF16)
    persist_mark = bump[0]

    P.dma(identf.v(), ident_d.v())
    P.dma(vecs.v(), vecs_d.v())
    P.dma(rtab.v(), rettab_d.v())
    P.cp('dve', identb.v(), identf.v())
    P.memset('pool', onesb.v(), 1.0)
    P.memset('pool', epsc.v(), EPS)

    stg_f = [alloc(f"stgf{i}", [128, 2048], F32) for i in range(7)]
    stg_b = [alloc(f"stgb{i}", [128, 2048], BF16) for i in range(7)]
    cvi = [0]

    def convert_pieces(src, dst, R, C):
        out = []
        for r in range(R // 128):
            c0 = 0
            while c0 < C:
                w = min(2048, C - c0)

                def piece(r=r, c0=c0, w=w):
                    n = len(stg_f)
                    i = cvi[0] % n
                    cvi[0] += 1
                    P.dma(stg_f[i][:, 0:w], src[r * 128:(r + 1) * 128, c0:c0 + w])
                    P.cp(['dve', 'act'][i % 2], stg_b[i][:, 0:w], stg_f[i][:, 0:w])
                    P.dma(dst[r * 128:(r + 1) * 128, c0:c0 + w], stg_b[i][:, 0:w])
                out.append(piece)
                c0 += w
        return out

    def convert(src, dst, R, C):
        for p_ in convert_pieces(src, dst, R, C):
            p_()

    P.dma(stg_f[0][:, 0:128], perm_d.v())
    P.cp('dve', permb.v(), stg_f[0][:, 0:128])

    pieces = []

    def add_pieces(src, dst, R, C):
        for r in range(R // 128):
            c0 = 0
            while c0 < C:
                w = min(2048, C - c0)
                pieces.append((src, dst, r, c0, w))
                c0 += w
    add_pieces(w_in, WIN, D, WIN_COLS)
    add_pieces(w_uq, WUQ, 384, 2048)
    add_pieces(w_uk, WUK, 256, 1024)
    add_pieces(w_uv, WUV, 256, 1024)
    NST = len(stg_f)
    ld_i = [0]
    cv_i = [0]

    def conv_load():
        if ld_i[0] < len(pieces):
            src, dst, r, c0, w = pieces[ld_i[0]]
            b_ = ld_i[0] % NST
            P.dma(stg_f[b_][:, 0:w], src[r * 128:(r + 1) * 128, c0:c0 + w])
            ld_i[0] += 1

    def conv_step():
        if cv_i[0] < len(pieces):
            src, dst, r, c0, w = pieces[cv_i[0]]
            b_ = cv_i[0] % NST
            P.cp(['dve', 'act'][cv_i[0] % 2], stg_b[b_][:, 0:w], stg_f[b_][:, 0:w])
            P.dma(dst[r * 128:(r + 1) * 128, c0:c0 + w], stg_b[b_][:, 0:w])
            cv_i[0] += 1
        conv_load()

    ct = alloc("ct", [128, 32], F32)
    sc = alloc("sc", [128, 32], F32)
    screp = alloc("screp", [128, 2, 16, 128], BF16)
    MW = 256
    NMB = 3 * D // MW
    wm = [alloc(f"wm{i}", [128, 16, MW], F32) for i in range(3)]
    wmb = [alloc(f"wmb{i}", [128, 16, MW], BF16) for i in range(2)]
    bm = [alloc(f"bm{i}", [128, MW], F32) for i in range(3)]
    nwb = [alloc(f"nwb{i}", [128, MW], F32) for i in range(3)]
    mo = [alloc(f"mo{i}", [128, MW], F32) for i in range(4)]
    P.dma(ct.v(), cT.v())
    P.act(sc.v(), ct.v(), AF.Silu)
    for v in range(2):
        P.cp('dve', screp[:, v], sc[:, v * 16:(v + 1) * 16].un(2).bc([128, 16, 128]))
    w_mod_v = w_mod.v().re("(c p) n -> p c n", p=128)

    def mod_load(cb):
        c0 = (cb * MW) % D
        P.dma(wm[cb % 3].v(), w_mod_v[:, :, cb * MW:(cb + 1) * MW])
        P.dma(bm[cb % 3].v(), View(b_mod.ap[0:1, cb * MW:(cb + 1) * MW].partition_broadcast(128), b_mod).re("p o n -> p (o n)"))
        if (cb * MW) // D == 1:
            P.dma(nwb[cb % 3].v(), View(norm_w.ap[0:1, c0:c0 + MW].partition_broadcast(128), norm_w).re("p o n -> p (o n)"))

    mod_load(0)
    mod_load(1)
    for _ in range(NST - 1):
        conv_load()
    moi = 0
    for cb in range(NMB):
        kind = (cb * MW) // D
        c0 = (cb * MW) % D
        if cb + 2 < NMB:
            mod_load(cb + 2)
        for _ in range(3):
            conv_step()
        wmt, bmt, nwt, wb_ = wm[cb % 3], bm[cb % 3], nwb[cb % 3], wmb[cb % 2]
        P.cp('dve', wb_[:, 0:8, :], wmt[:, 0:8, :])
        P.cp('act', wb_[:, 8:16, :], wmt[:, 8:16, :])
        for v in range(2):
            pb = bank()
            for c in range(16):
                P.mm(pb[:, 0:MW], screp[:, v, c, :], wb_[:, c, :], start=(c == 0), stop=(c == 15))
            m = mo[moi % 4]
            moi += 1
            P.tt('dve', m.v(), pb[:, 0:MW], bmt.v(), ALU.add)
            if kind == 1:
                P.stt('dve', m.v(), m.v(), 1.0, nwt.v(), ALU.add, ALU.mult)
            P.dma(MODS[kind * 2 + v][:, c0:c0 + MW], m.v())
    while cv_i[0] < len(pieces):
        conv_step()
    P.dma(wuq.v(), WUQ.v().re("(c p) n -> p c n", p=128))
    P.dma(wuk.v(), WUK.v().re("(c p) n -> p c n", p=128))
    P.dma(wuv.v(), WUV.v().re("(c p) n -> p c n", p=128))

    lgt = alloc("lgt", [128, 8], F32)
    ee = alloc("ee", [128, 8], F32)
    e1 = alloc("e1", [128, 128], F32)
    e2 = alloc("e2", [128, 128], F32)
    P.dma(lgt.v(), View(lgd.ap[0:1, :].partition_broadcast(128), lgd).re("p o n -> p (o n)"))
    P.act(ee.v(), lgt.v(), AF.Exp)
    P.ts('dve', nee.v(), ee.v(), -1.0, None, ALU.mult)
    for h in range(RH):
        nf = nee[:, h:h + 1]
        nb_ = nee[:, 4 + h:5 + h]
        P.act(e1.v(), rtab[:, RT_D1:RT_D1 + 128], AF.Exp, scale=nf)
        P.act(e2.v(), rtab[:, RT_D2:RT_D2 + 128], AF.Exp, scale=nb_)
        P.tt('dve', Mh[:, h, :], e1.v(), e2.v(), ALU.mult)
        P.act(QDF[:, h, :], rtab[:, RT_QDF:RT_QDF + 128], AF.Exp, scale=nf)
        P.act(QDB[:, h, :], rtab[:, RT_QDB:RT_QDB + 128], AF.Exp, scale=nb_)
        P.act(dcols[:, h:h + 1], rtab[:, RT_KDF:RT_KDF + 1], AF.Exp, scale=nf)
        P.act(dcols[:, 4 + h:5 + h], rtab[:, RT_KDB:RT_KDB + 1], AF.Exp, scale=nb_)
        P.act(dcols[:, 8 + h:9 + h], rtab[:, RT_C:RT_C + 1], AF.Exp, scale=nf)
        P.act(dcols[:, 12 + h:13 + h], rtab[:, RT_C:RT_C + 1], AF.Exp, scale=nb_)

    P.barrier()
    bump[0] = persist_mark
    if stop_after == 'setup':
        P.emit()
        st.close()
        return nc

    xt = [alloc(f"xt{i}", [128, D], F32) for i in range(2)]
    hn = [alloc(f"hn{i}", [128, D], BF16) for i in range(2)]
    hTs = [alloc(f"hT{i}", [128, 16, TB], BF16) for i in range(2)]
    wsl = [alloc(f"wsl{i}", [128, 16, 512], BF16) for i in range(2)]
    Acur = alloc("Acur", [128, D], F32)
    Bcur = alloc("Bcur", [128, D], F32)
    qraw = alloc("qraw", [128, 3, TB], F32)
    ckraw = alloc("ckraw", [128, 2, TB], F32)
    krraw = alloc("krraw", [64, TB], F32)
    kpraw = alloc("kpraw", [64, TB], F32)
    krot = alloc("krot", [64, TB], F32)
    kp2 = alloc("kp2", [64, TB], F32)
    sq = [alloc(f"sq{i}", [128, TB], BF16) for i in range(4)]
    sqkr = alloc("sqkr", [64, TB], BF16)
    msq = [alloc(f"msq{i}", [128, TB], F32) for i in range(2)]
    ff = [alloc(f"ff{i}", [128, TB], F32) for i in range(3)]
    qln = alloc("qln", [128, 3, TB], BF16)
    ckvn = alloc("ckvn", [128, 2, TB], BF16)
    ob = [alloc(f"ob{i}", [128, TB], BF16) for i in range(6)]
    t1 = [alloc(f"t1{i}", [128, TB], F32) for i in range(2)]
    t2 = [alloc(f"t2{i}", [128, TB], F32) for i in range(2)]
    mc = alloc("mc", [64, TB], F32)
    msn = alloc("msn", [64, TB], F32)
    rc = alloc("rc", [128, TB], F32)
    rs = alloc("rs", [128, TB], F32)
    vt = [alloc(f"vt{i}", [128, 1024], BF16) for i in range(2)]
    rawb = [alloc(f"rawb{i}", [128, TB], BF16) for i in range(2)]
    rvt = [alloc(f"rvt{i}", [128, 512], BF16) for i in range(2)]
    ckout = [alloc(f"ckout{i}", [128, 320], F32) for i in range(2)]
    kvwbc = alloc("kvwbc", [128, 256], F32)
    ssc = alloc("ssc", [128, 8], F32)
    ctx_in = ckout

    cnt = {'sq': 0, 'ms': 0, 'ff': 0, 'ob': 0, 't': 0, 'vt': 0, 'rawb': 0, 'rvt': 0, 'ck': 0, 'ssc': 0}

    def nxt(lst, key):
        i = cnt[key]
        cnt[key] += 1
        return lst[i % len(lst)]

    P.dma(kvwbc.v(), View(kvw_row.ap[0:1, :].partition_broadcast(128), kvw_row).re("p o n -> p (o n)"))

    def vcol(c, np_=128):
        return vecs[0:np_, c:c + 1]

    def rms_bc(sqs, nfeat, nt):
        sb = bank(6, 8, 's')
        for i, (v, K) in enumerate(sqs):
            P.mm(sb[:, 0:nt], onesb[0:K, :], v, start=(i == 0), stop=(i == len(sqs) - 1))
        ms = nxt(msq, 'ms')
        P.act(ms[:, 0:nt], sb[:, 0:nt], AF.Ln, scale=1.0 / nfeat, bias=epsc[:, 0:1])
        f = nxt(ff, 'ff')
        P.act(f[:, 0:nt], ms[:, 0:nt], AF.Exp, scale=-0.5)
        return f

    def kv_heads(nt, rope, koff):
        if rope:
            P.stt('dve', krot[0:64, 0:nt], krraw[0:64, 0:nt], vcol(V_WKR, 64), mc[0:64, 0:nt], ALU.mult, ALU.mult)
            P.stt('dve', kp2[0:64, 0:nt], kpraw[0:64, 0:nt], vcol(V_WKP, 64), msn[0:64, 0:nt], ALU.mult, ALU.mult)
            P.tt('pool', krot[0:64, 0:nt], krot[0:64, 0:nt], kp2[0:64, 0:nt], ALU.add)
        else:
            P.ts('dve', krot[0:64, 0:nt], krraw[0:64, 0:nt], vcol(V_WKR, 64), None, ALU.mult)
        def k_stage1(h):
            bn = bank()
            for c in range(2):
                P.mm(bn[:, 0:nt], wuk[:, c, h * 128:(h + 1) * 128], ckvn[:, c, 0:nt], start=(c == 0), stop=(c == 1))
            sqa = nxt(sq, 'sq')
            P.act(sqa[:, 0:nt], bn[:, 0:nt], AF.Square)
            return (h, bn, sqa)

        def k_stage2(ctx):
            h, bn, sqa = ctx
            f = rms_bc([(sqa[:, 0:nt], 128), (sqkr[0:64, 0:nt], 64)], 192, nt)
            o1 = nxt(ob, 'ob')
            P.stt('dve', o1[:, 0:nt], bn[:, 0:nt], vcol(V_WKN), f[:, 0:nt], ALU.mult, ALU.mult)
            P.dma(KN[h][:, koff:koff + nt], o1[:, 0:nt])
            o2 = nxt(ob, 'ob')
            P.tt('pool', o2[0:64, 0:nt], krot[0:64, 0:nt], f[0:64, 0:nt], ALU.mult)
            P.dma(KR[h][:, koff:koff + nt], o2[0:64, 0:nt])

        prev = None
        for h in range(H):
            cur = k_stage1(h)
            if prev is not None:
                k_stage2(prev)
            prev = cur
        k_stage2(prev)
        for t in range(nt // 128):
            vtt = nxt(vt, 'vt')
            for cb in range(2):
                pb = bank()
                for c in range(2):
                    P.mm(pb.v(), ckvn[:, c, t * 128:(t + 1) * 128], wuv[:, c, cb * 512:(cb + 1) * 512],
                         start=(c == 0), stop=(c == 1))
                P.cp('act', vtt[:, cb * 512:(cb + 1) * 512], pb.v())
            P.dma(VV[koff + t * 128: koff + (t + 1) * 128, :], vtt.v())

    WINv = WIN.v().re("(c p) n -> p c n", p=128)
    slab_i = [0]

    def load_slab(src_v, off, w):
        ws = wsl[slab_i[0] % 2]
        slab_i[0] += 1
        P.dma(ws[:, :, 0:w], src_v[:, :, off:off + w])
        return ws

    work = [(b_, s_) for b_ in range(NBLK) for s_ in range(10 if b_ < NSB else 11)]
    slab_q = []
    work_i = [0]

    def issue_next_slab():
        if work_i[0] < len(work):
            off_, w_ = SLABS[work[work_i[0]][1]]
            slab_q.append(load_slab(WINv, off_, w_))
            work_i[0] += 1

    def blk_info(b):
        sample = b < NSB
        g0 = b * TB
        return sample, g0, (xs if sample else xp), (g0 if sample else g0 - LS)

    def prepA(b, t):
        sample, g0, xsrc, x0 = blk_info(b)
        if t == 0 and (b == 0 or b == NSB):
            v = 0 if sample else 1
            P.dma(Bcur.v(), MODS[0 * 2 + v])
            P.dma(Acur.v(), MODS[1 * 2 + v])
        x = xt[t % 2]
        hh = hn[t % 2]
        P.dma(x.v(), xsrc[x0 + t * 128: x0 + (t + 1) * 128, :])
        s1 = ssc[:, (cnt['ssc'] % 8):(cnt['ssc'] % 8) + 1]
        cnt['ssc'] += 1
        P.act(hh.v(), x.v(), AF.Square, accum=s1)
        P.act(s1, s1, AF.Ln, scale=1.0 / D, bias=epsc[:, 0:1])
        P.act(s1, s1, AF.Exp, scale=-0.5)
        P.stt('dve', x.v(), x.v(), s1, Acur.v(), ALU.mult, ALU.mult)
        P.tt('pool', hh.v(), x.v(), Bcur.v(), ALU.add)

    def prepB(b, t):
        hh = hn[t % 2]
        hT = hTs[b % 2]
        for half in range(2):
            pbt = bank()
            pbv = View(psb_bf[psb.index(pbt)], pbt)
            for c8 in range(8):
                c = half * 8 + c8
                P.tr(pbv[:, c8 * 128:(c8 + 1) * 128], hh[:, c * 128:(c + 1) * 128], identb.v())
            src = pbv.re("p (c n) -> p c n", c=8)
            dst = hT[:, half * 8:(half + 1) * 8, t * 128:(t + 1) * 128]
            if half == 0:
                P.cp('act', dst, src)
            else:
                P.cp('dve', dst, src)

    def phase1_block(b, nb):
        sample, g0, xsrc, x0 = blk_info(b)
        koff = g0 if sample else g0 + PAST
        hT = hTs[b % 2]
        if sample:
            P.dma(mc.v(), mcos_d[:, g0:g0 + TB])
            P.dma(msn.v(), msin_d[:, g0:g0 + TB])
            P.dma(rc.v(), rcos_d[:, g0:g0 + TB])
            P.dma(rs.v(), rsin_d[:, g0:g0 + TB])
        nsl = 10 if sample else 11
        nsl = min(nsl, LIMITS.get('p1_slabs', nsl))
        for s in range(nsl):
            off, w = SLABS[s]
            ws = slab_q.pop(0)
            issue_next_slab()

            def group(c0, M, ws=ws):
                pb = bank()
                for c in range(16):
                    P.mm(pb[0:M, :], ws[:, c, c0:c0 + M], hT[:, c, :], start=(c == 0), stop=(c == 15))
                return pb

            if s == 0:
                sqs = []
                for g in range(3):
                    pb = group(g * 128, 128)
                    sqa = nxt(sq, 'sq')
                    P.act(sqa.v(), pb.v(), AF.Square)
                    P.cp('dve', qraw[:, g, :], pb.v())
                    sqs.append((sqa.v(), 128))
                if LIMITS.get('s0_stage', 9) < 2:
                    continue
                pb = group(384, 64)
                P.act(sqkr.v(), pb[0:64, :], AF.Square)
                P.cp('dve', krraw.v(), pb[0:64, :])
                if sample:
                    pb = group(448, 64)
                    P.cp('dve', kpraw.v(), pb[0:64, :])
                if LIMITS.get('s0_stage', 9) < 3:
                    continue
                f = rms_bc(sqs, 384, TB)
                for g in range(3):
                    P.stt('dve', qln[:, g, :], qraw[:, g, :], vcol(V_QNW + g), f.v(), ALU.mult, ALU.mult)
                if LIMITS.get('s0_stage', 9) < 4:
                    continue
                def q_stage1(h):
                    bn = bank()
                    for c in range(3):
                        P.mm(bn.v(), wuq[:, c, h * 256:h * 256 + 128], qln[:, c, :], start=(c == 0), stop=(c == 2))
                    br = bank()
                    for c in range(3):
                        P.mm(br[0:64, :], wuq[:, c, h * 256 + 128:h * 256 + 192], qln[:, c, :], start=(c == 0), stop=(c == 2))
                    bp = None
                    if sample:
                        bp = bank()
                        for c in range(3):
                            P.mm(bp[0:64, :], wuq[:, c, h * 256 + 192:h * 256 + 256], qln[:, c, :], start=(c == 0), stop=(c == 2))
                    sqa = nxt(sq, 'sq')
                    sqb = nxt(sq, 'sq')
                    P.act(sqa.v(), bn.v(), AF.Square)
                    P.act(sqb[0:64, :], br[0:64, :], AF.Square)
                    return (h, bn, br, bp, sqa, sqb)

                def q_stage2(ctx):
                    h, bn, br, bp, sqa, sqb = ctx
                    f = rms_bc([(sqa.v(), 128), (sqb[0:64, :], 64)], 192, TB)
                    o1 = nxt(ob, 'ob')
                    P.stt('dve', o1.v(), bn.v(), vcol(V_WQN), f.v(), ALU.mult, ALU.mult)
                    P.dma(QN[h][:, g0:g0 + TB], o1.v())
                    o2 = nxt(ob, 'ob')
                    if not sample:
                        P.stt('dve', o2[0:64, :], br[0:64, :], vcol(V_WQR, 64), f[0:64, :], ALU.mult, ALU.mult)
                    else:
                        a = nxt(t1, 't')
                        b_ = t2[(cnt['t'] - 1) % 2]
                        P.stt('dve', a[0:64, :], br[0:64, :], vcol(V_WQR, 64), mc.v(), ALU.mult, ALU.mult)
                        P.stt('dve', b_[0:64, :], bp[0:64, :], vcol(V_WQP, 64), msn.v(), ALU.mult, ALU.mult)
                        P.tt('pool', a[0:64, :], a[0:64, :], b_[0:64, :], ALU.add)
                        P.tt('pool', o2[0:64, :], a[0:64, :], f[0:64, :], ALU.mult)
                    P.dma(QR[h][:, g0:g0 + TB], o2[0:64, :])

                for h in range(H):
                    q_stage2(q_stage1(h))
            elif s == 1:
                sqs = []
                for g in range(2):
                    pb = group(g * 128, 128)
                    sqa = nxt(sq, 'sq')
                    P.act(sqa.v(), pb.v(), AF.Square)
                    P.cp('dve', ckraw[:, g, :], pb.v())
                    sqs.append((sqa.v(), 128))
                f = rms_bc(sqs, 256, TB)
                for g in range(2):
                    P.stt('dve', ckvn[:, g, :], ckraw[:, g, :], vcol(V_KVW + g), f.v(), ALU.mult, ALU.mult)
                kv_heads(TB, sample, koff)
            elif s in (2, 3, 6, 7):
                dstT = GA if s in (2, 3) else GB
                r0 = (s % 2) * 512
                for g in range(4):
                    pb = group(g * 128, 128)
                    o1 = nxt(ob, 'ob')
                    P.act(o1.v(), pb.v(), AF.Silu)
                    P.dma(dstT[r0 + g * 128: r0 + (g + 1) * 128, g0:g0 + TB], o1.v())
            elif s in (4, 5):
                dstT = RQ if s == 4 else RK
                scl = None if s == 4 else 128.0 ** -0.5
                for g in range(4):
                    pb = group(g * 128, 128)
                    o1 = nxt(ob, 'ob')
                    if not sample:
                        P.cp('act', o1.v(), pb.v(), scale=scl)
                    else:
                        rb = nxt(rawb, 'rawb')
                        P.cp('act', rb.v(), pb.v(), scale=scl)
                        pp = bank()
                        P.mm(pp.v(), permb.v(), rb.v())
                        a = nxt(t1, 't')
                        b_ = t2[(cnt['t'] - 1) % 2]
                        P.tt('pool', a.v(), rb.v(), rc.v(), ALU.mult)
                        P.tt('dve', b_.v(), pp.v(), rs.v(), ALU.mult)
                        P.tt('pool', o1.v(), a.v(), b_.v(), ALU.add)
                    P.dma(dstT[g][:, g0:g0 + TB], o1.v())
            elif s in (8, 9):
                for t in range(4):
                    pb = bank()
                    for c in range(16):
                        P.mm(pb.v(), hT[:, c, t * 128:(t + 1) * 128], ws[:, c, 0:512], start=(c == 0), stop=(c == 15))
                    r = nxt(rvt, 'rvt')
                    P.cp('act' if t % 2 == 0 else 'dve', r.v(), pb.v())
                    P.dma(RV[g0 + t * 128: g0 + (t + 1) * 128, (s - 8) * 512:(s - 7) * 512], r.v())
            else:
                for t in range(4):
                    pb = bank()
                    for c in range(16):
                        P.mm(pb[:, 0:320], hT[:, c, t * 128:(t + 1) * 128], ws[:, c, 0:320], start=(c == 0), stop=(c == 15))
                    s1 = ssc[:, (cnt['ssc'] % 8):(cnt['ssc'] % 8) + 1]
                    cnt['ssc'] += 1
                    oj = nxt(ob, 'ob')
                    P.act(oj[:, 0:256], pb[:, 0:256], AF.Square, accum=s1)
                    P.act(s1, s1, AF.Ln, scale=1.0 / 256, bias=epsc[:, 0:1])
                    P.act(s1, s1, AF.Exp, scale=-0.5)
                    co = nxt(ckout, 'ck')
                    P.stt('dve', co[:, 0:256], pb[:, 0:256], s1, kvwbc.v(), ALU.mult, ALU.mult)
                    P.cp('act', co[:, 256:320], pb[:, 256:320])
                    r0 = x0 + t * 128
                    P.dma(o_ckv[r0:r0 + 128, :], co[:, 0:256])
                    P.dma(o_kr[r0:r0 + 128, :], co[:, 256:320])
            if nb is not None:
                if 2 <= s <= 5:
                    prepA(nb, s - 2)
                if 3 <= s <= 6:
                    prepB(nb, s - 3)

    def ctx_keys():
        for t in range(2):
            ci = ctx_in[t]
            P.dma(ci[:, 0:256], cache_ckv[t * 128:(t + 1) * 128, :])
            P.dma(ci[:, 256:320], cache_kr[t * 128:(t + 1) * 128, :])
            for c in range(2):
                pb = bank()
                P.tr(pb[:, 0:128], ci[:, c * 128:(c + 1) * 128], identf.v())
                P.cp('dve', ckvn[:, c, t * 128:(t + 1) * 128], pb[:, 0:128])
            pb = bank()
            P.tr(pb[0:64, 0:128], ci[:, 256:320], identf.v())
            P.cp('dve', krraw[0:64, t * 128:(t + 1) * 128], pb[0:64, 0:128])
            P.act(sqkr[0:64, t * 128:(t + 1) * 128], pb[0:64, 0:128], AF.Square)
        kv_heads(256, False, LS)

    p1_list = LIMITS.get('p1_blocks', list(range(NBLK)))
    work[:] = [(b_, s_) for (b_, s_) in work if b_ in p1_list and s_ < LIMITS.get('p1_slabs', 99)]
    issue_next_slab()
    for t in range(4):
        prepA(p1_list[0], t)
        prepB(p1_list[0], t)
    for bi, b in enumerate(p1_list):
        phase1_block(b, p1_list[bi + 1] if bi + 1 < len(p1_list) else None)
        if b == NSB - 1:
            ctx_keys()

    P.barrier()
    bump[0] = persist_mark
    if stop_after == 'p1':
        P.emit()
        st.close()
        return nc

    LKMAX = LS + PAST
    stg_f = [alloc(f"stgf{i}", [128, 2048], F32) for i in range(3)]
    stg_b = [alloc(f"stgb{i}", [128, 2048], BF16) for i in range(3)]
    wout_pieces = convert_pieces(w_out, WOUT, D, D)
    kvs = [dict(kn=alloc(f"kn_sb{j}", [128, LKMAX], BF16), kr=alloc(f"kr_sb{j}", [128, LKMAX], BF16),
                v=alloc(f"v_sb{j}", [128, LKMAX // 128, 128], BF16)) for j in range(2)]
    qn_sb = [alloc(f"qn_sb{i}", [128, 512], BF16) for i in range(2)]
    qr_sb = [alloc(f"qr_sb{i}", [128, 512], BF16) for i in range(2)]
    pT = [alloc(f"pT{i}", [128, 512], BF16) for i in range(4)]
    acc = [[alloc(f"acc{i}_{j}", [128, 512], F32) for j in range(2)] for i in range(2)]
    onesf = alloc("onesf", [128, 128], F32)
    rD = alloc("rD", [128, 512], F32)
    on = alloc("on", [128, 512], F32)
    ga_sb = [alloc(f"ga_sb{i}", [128, 512], BF16) for i in range(3)]
    mixo = [alloc(f"mixo{i}", [128, 512], BF16) for i in range(2)]
    SCALE = 192.0 ** -0.5
    P.memset('pool', onesf.v(), 1.0)
    for j in range(2):
        P.memset('pool', kvs[j]['kr'][64:128, :], 0.0)
        P.memset('pool', qr_sb[j][64:128, :], 0.0)

    seqs = [(0, LS, 0, LS + PAST, 512)] + [(LS + s_ * LP, LP, LS + PAST + s_ * LP, LP, 256) for s_ in range(NPS)]
    units = []
    for (q0, Lq, k0, Lk, QB) in seqs:
        for h in range(H):
            for qb in range(Lq // QB):
                units.append((q0, Lq, k0, Lk, QB, h, qb))
    hc = [0]
    kvmap = {}

    def issue_loads(idx):
        q0, Lq, k0, Lk, QB, h, qb = units[idx]
        if qb == 0:
            kv = kvs[hc[0] % 2]
            hc[0] += 1
            kvmap[(q0, h)] = kv
            nkt = Lk // 128
            P.dma(kv['kn'][:, 0:Lk], KN[h][:, k0:k0 + Lk])
            P.dma(kv['kr'][0:64, 0:Lk], KR[h][:, k0:k0 + Lk])
            P.dma(kv['v'][:, 0:nkt, :], VV[k0:k0 + Lk, h * 128:(h + 1) * 128].re("(t p) e -> p t e", p=128))
        i = idx % 2
        qs = q0 + qb * QB
        P.dma(qn_sb[i][:, 0:QB], QN[h][:, qs:qs + QB])
        P.dma(qr_sb[i][0:64, 0:QB], QR[h][:, qs:qs + QB])
        P.dma(ga_sb[idx % 3][:, 0:QB], GA[h * 128:(h + 1) * 128, qs:qs + QB])

    def make_epilogue(idx):
        q0, Lq, k0, Lk, QB, h, qb = units[idx]
        i = idx % 2
        qs = q0 + qb * QB
        Ob = psb[3 + i]
        Db = psb[5 + i]

        def epi():
            P.mm(Db[:, 0:QB], onesf.v(), acc[i][0][:, 0:QB], start=(Lk // 128 < 16), stop=False)
            P.mm(Db[:, 0:QB], onesf.v(), acc[i][1][:, 0:QB], start=False, stop=True)
            P.recip(rD[:, 0:QB], Db[:, 0:QB])
            P.tt('dve', on[:, 0:QB], Ob[:, 0:QB], rD[:, 0:QB], ALU.mult)
            P.tt('pool', mixo[i][:, 0:QB], on[:, 0:QB], ga_sb[idx % 3][:, 0:QB], ALU.mult)
            P.dma(MIX[h * 128:(h + 1) * 128, qs:qs + QB], mixo[i][:, 0:QB])
        return epi

    issue_loads(0)
    pending = [None]
    for idx, u in enumerate(units):
        q0, Lq, k0, Lk, QB, h, qb = u
        nkt = Lk // 128
        i = idx % 2
        kv = kvmap[(q0, h)]
        if idx + 1 < len(units):
            issue_loads(idx + 1)
        if wout_pieces:
            wout_pieces.pop(0)()
        Ob = psb[3 + i]

        SB_ = [0, 1, 2, 7]

        def smm(kt, kv=kv, i=i, QB=QB):
            sb_ = psb[SB_[kt % 4]]
            P.mm(sb_[:, 0:QB], kv['kn'][:, kt * 128:(kt + 1) * 128], qn_sb[i][:, 0:QB], start=True, stop=False)
            P.mm(sb_[:, 0:QB], kv['kr'][:, kt * 128:(kt + 1) * 128], qr_sb[i][:, 0:QB], start=False, stop=True)
        LA = 2
        for k0_ in range(min(LA, nkt)):
            smm(k0_)
        dstart = [False]
        for kt in range(nkt):
            if kt + LA < nkt:
                smm(kt + LA)
            p = pT[kt % 4]
            P.act(p[:, 0:QB], psb[SB_[kt % 4]][:, 0:QB], AF.Exp, scale=SCALE)
            P.mm(Ob[:, 0:QB], kv['v'][:, kt, :], p[:, 0:QB], start=(kt == 0), stop=(kt == nkt - 1))
            if kt % 16 == 15:
                P.mm(psb[5 + i][:, 0:QB], onesb.v(), p[:, 0:QB], start=(not dstart[0]), stop=False)
                dstart[0] = True
            else:
                a_ = acc[i][kt % 2]
                if kt < 2:
                    P.cp('dve', a_[:, 0:QB], p[:, 0:QB])
                else:
                    P.tt('dve', a_[:, 0:QB], a_[:, 0:QB], p[:, 0:QB], ALU.add)
            if kt == min(3, nkt - 1) and pending[0] is not None:
                pending[0]()
                pending[0] = None
        pending[0] = make_epilogue(idx)
    pending[0]()
    while wout_pieces:
        wout_pieces.pop(0)()

    P.barrier()
    bump[0] = persist_mark
    if stop_after == 'attn':
        P.emit()
        st.close()
        return nc

    NCMAX = LS // 128
    rset = [dict(rk=alloc(f"rk_sb{j}", [128, LS], BF16), rq=alloc(f"rq_sb{j}", [128, LS], BF16),
                 rv=alloc(f"rv_sb{j}", [128, NCMAX, 256], BF16)) for j in range(2)]
    KDF = alloc("KDF", [128, NCMAX, 128], BF16)
    KDB = alloc("KDB", [128, NCMAX, 128], BF16)
    snapF = alloc("snapF", [128, NCMAX, 256], BF16)
    snapB = alloc("snapB", [128, NCMAX, 256], BF16)
    SfP = [alloc(f"SfP{i}", [128, 256], F32) for i in range(2)]
    SbP = [alloc(f"SbP{i}", [128, 256], F32) for i in range(2)]
    Sx = [alloc(f"Sx{i}", [128, 256], F32) for i in range(4 * (NPS - 1))]
    smt = [alloc(f"smt{i}", [128, 128], BF16) for i in range(3)]
    qd = [alloc(f"qd{i}", [128, 2, 512], BF16) for i in range(2)]
    rsq = [alloc(f"rsq{i}", [128, 512], BF16) for i in range(4)]
    rms_ = [alloc(f"rms{i}", [128, 512], F32) for i in range(2)]
    rff = [alloc(f"rff{i}", [128, 512], F32) for i in range(2)]
    rno = [alloc(f"rno{i}", [128, 512], F32) for i in range(4)]
    gb_sb = [alloc(f"gb_sb{i}", [128, 2, 512], BF16) for i in range(2)]
    rmo = [alloc(f"rmo{i}", [128, 512], BF16) for i in range(4)]
    rc_ = {'sm': 0, 'g': 0, 'rsq': 0, 'rmo': 0}

    rseqs = [(0, LS, True, 1), (LS, NPS * LP, False, NPS)]
    runits = [(q0, L, smp, nsq, h) for (q0, L, smp, nsq) in rseqs for h in range(RH)]

    def ret_loads(u):
        q0, L, smp, sidx, h = runits[u]
        n = L // 128
        bf_ = rset[u % 2]
        P.dma(bf_['rk'][:, 0:L], RK[h][:, q0:q0 + L])
        P.dma(bf_['rv'][:, 0:n, :], RV[q0:q0 + L, h * 256:(h + 1) * 256].re("(t p) e -> p t e", p=128))
        P.dma(bf_['rq'][:, 0:L], RQ[h][:, q0:q0 + L])

    def ret_unit(u):
        q0, L, sample, nseq, h = runits[u]
        n = L // 128
        bf_ = rset[u % 2]
        rk_sb, rq_sb, rv_sb = bf_['rk'], bf_['rq'], bf_['rv']
        kdf = dcols[:, h:h + 1]
        kdb = dcols[:, 4 + h:5 + h]
        cdf = dcols[:, 8 + h:9 + h]
        cdb = dcols[:, 12 + h:13 + h]
        nsub = n // nseq
        SfC = [[SfP[0], SfP[1]]] + [[Sx[4 * j_], Sx[4 * j_ + 1]] for j_ in range(nseq - 1)]
        SbC = [[SbP[0], SbP[1]]] + [[Sx[4 * j_ + 2], Sx[4 * j_ + 3]] for j_ in range(nseq - 1)]
        for j_ in range(nseq):
            if sample:
                P.dma(SfC[j_][0].v(), st_f[h])
                P.dma(SbC[j_][0].v(), st_b[h])
            else:
                P.memset('pool', SfC[j_][0].v(), 0.0)
                P.memset('pool', SbC[j_][0].v(), 0.0)
        if u + 1 < len(runits):
            ret_loads(u + 1)
        for g8 in range((n + 7) // 8):
            nb = min(8, n - g8 * 8)
            bi = 7 if g8 % 2 == 0 else 0
            pbv = View(psb_bf[bi], psb[bi])
            for j in range(nb):
                c = g8 * 8 + j
                P.tr(pbv[:, j * 128:(j + 1) * 128], rk_sb[:, c * 128:(c + 1) * 128], identb.v())
            src = pbv[:, 0:nb * 128].re("p (c d) -> p c d", d=128)
            P.ts('dve', KDF[:, g8 * 8:g8 * 8 + nb, :], src, kdf, None, ALU.mult)
            P.ts('dve', KDB[:, g8 * 8:g8 * 8 + nb, :], src, kdb, None, ALU.mult)
        ui = 0
        for k in range(nsub):
            for j_ in range(nseq):
                cf, cb_ = j_ * nsub + k, j_ * nsub + (nsub - 1 - k)
                Sf_cur, Sf_n = SfC[j_][k % 2], SfC[j_][(k + 1) % 2]
                Sb_cur, Sb_n = SbC[j_][k % 2], SbC[j_][(k + 1) % 2]
                P.cp('act', snapF[:, cf, :], Sf_cur.v())
                upf = psb[3 + ui % 2]
                P.mm(upf[:, 0:256], KDF[:, cf, :], rv_sb[:, cf, :])
                P.stt('dve', Sf_n.v(), Sf_cur.v(), cdf, upf[:, 0:256], ALU.mult, ALU.add)
                P.cp('act', snapB[:, cb_, :], Sb_cur.v())
                upb = psb[5 + ui % 2]
                P.mm(upb[:, 0:256], KDB[:, cb_, :], rv_sb[:, cb_, :])
                P.stt('dve', Sb_n.v(), Sb_cur.v(), cdb, upb[:, 0:256], ALU.mult, ALU.add)
                ui += 1
        if not sample:
            for j_ in range(nseq):
                P.dma(o_sf[j_][h], SfC[j_][nsub % 2].v())
                P.dma(o_sb[j_][h], SbC[j_][nsub % 2].v())
        ngrp = (n + 3) // 4
        gbase = rc_['g']
        rc_['g'] += ngrp

        def prep_group(g):
            c_lo = g * 4
            nc_ = min(4, n - c_lo)
            W = nc_ * 128
            gi = (gbase + g) % 2
            tok0 = q0 + c_lo * 128
            qv = rq_sb[:, c_lo * 128: c_lo * 128 + W].re("p (c i) -> p c i", i=128)
            P.tt('dve', qd[gi][:, 0, 0:W].re("p (c i) -> p c i", i=128), qv,
                 QDF[:, h, :].un(1).bc([128, nc_, 128]), ALU.mult)
            P.tt('pool', qd[gi][:, 1, 0:W].re("p (c i) -> p c i", i=128), qv,
                 QDB[:, h, :].un(1).bc([128, nc_, 128]), ALU.mult)
            P.dma(gb_sb[gi][:, :, 0:W], GB[h * 256:(h + 1) * 256, tok0:tok0 + W].re("(e p) t -> p e t", p=128))

        prep_group(0)
        for g in range(ngrp):
            c_lo = g * 4
            nc_ = min(4, n - c_lo)
            W = nc_ * 128
            gi = (gbase + g) % 2
            tok0 = q0 + c_lo * 128
            if g + 1 < ngrp:
                prep_group(g + 1)
            OA = psb[1 + 2 * gi]
            OB = psb[2 + 2 * gi]
            def st_mm(c):
                cs = slice(c * 128, (c + 1) * 128)
                sp_ = psb[7 if c % 2 == 0 else 0]
                P.mm(sp_[:, 0:128], rk_sb[:, cs], rq_sb[:, cs])
                sm_ = smt[c % 3]
                P.tt('dve', sm_.v(), sp_[:, 0:128], Mh[:, h, :], ALU.mult)

            if g == 0:
                st_mm(0)
            for ci in range(nc_):
                c = c_lo + ci
                if c + 1 < n:
                    st_mm(c + 1)
                sm = smt[c % 3]
                for e2, Ot in enumerate((OA, OB)):
                    es = slice(e2 * 128, (e2 + 1) * 128)
                    oc = Ot[:, ci * 128:(ci + 1) * 128]
                    P.mm(oc, rv_sb[:, c, es], sm.v(), start=True, stop=False)
                    P.mm(oc, snapF[:, c, es], qd[gi][:, 0, ci * 128:(ci + 1) * 128], start=False, stop=False)
                    P.mm(oc, snapB[:, c, es], qd[gi][:, 1, ci * 128:(ci + 1) * 128], start=False, stop=True)
            osb = [rno[2 * gi], rno[2 * gi + 1]]
            P.cp('act', osb[0][:, 0:W], OA[:, 0:W])
            P.cp('act', osb[1][:, 0:W], OB[:, 0:W])
            sqs = []
            for e2 in range(2):
                r = rsq[rc_['rsq'] % 4]
                rc_['rsq'] += 1
                P.act(r[:, 0:W], osb[e2][:, 0:W], AF.Square)
                sqs.append(r)
            sb_ = psb[5 + gi]
            P.mm(sb_[:, 0:W], onesb.v(), sqs[0][:, 0:W], start=True, stop=False)
            P.mm(sb_[:, 0:W], onesb.v(), sqs[1][:, 0:W], start=False, stop=True)
            ms = rms_[gi]
            P.act(ms[:, 0:W], sb_[:, 0:W], AF.Ln, scale=1.0 / 256, bias=epsc[:, 0:1])
            f = rff[gi]
            P.act(f[:, 0:W], ms[:, 0:W], AF.Exp, scale=-0.5)
            for e2 in range(2):
                no = osb[e2]
                P.stt('dve', no[:, 0:W], no[:, 0:W], vcol(V_GNW + h * 2 + e2), f[:, 0:W], ALU.mult, ALU.mult)
                mo_ = rmo[rc_['rmo'] % 4]
                rc_['rmo'] += 1
                P.tt('pool', mo_[:, 0:W], no[:, 0:W], gb_sb[gi][:, e2, 0:W], ALU.mult)
                r0 = 1024 + h * 256 + e2 * 128
                P.dma(MIX[r0:r0 + 128, tok0:tok0 + W], mo_[:, 0:W])

    ret_loads(0)
    for u in range(len(runits)):
        ret_unit(u)

    P.barrier()
    bump[0] = persist_mark
    if stop_after == 'ret':
        P.emit()
        st.close()
        return nc

    mx = alloc("mx", [128, 16, TB], BF16)
    Gcur = alloc("Gcur", [128, D], F32)
    xo = [alloc(f"xo{i}", [128, 4, 512], F32) for i in range(2)]
    yt_ = [alloc(f"yt{i}", [128, 512], F32) for i in range(3)]
    WOv = WOUT.v().re("(c p) n -> p c n", p=128)
    MIXv = MIX.v().re("(c p) t -> p c t", p=128)
    yi = [0]
    steps = [(b, db) for b in range(NBLK) for db in range(4)]
    mxs = [mx, alloc("mx2", [128, 16, TB], BF16)]

    woq = [alloc(f"woq{i}", [128, 16, 512], BF16) for i in range(4)]

    def p3_mx_load(b):
        P.dma(mxs[b % 2].v(), MIXv[:, :, b * TB:(b + 1) * TB])

    def p3_loads(k):
        b, db = steps[k]
        sample = b < NSB
        g0 = b * TB
        xsrc = xs if sample else xp
        x0 = g0 if sample else g0 - LS
        P.dma(xo[k % 2].v(), xsrc[x0:x0 + TB, db * 512:(db + 1) * 512].re("(t p) n -> p t n", p=128))

    P.dma(Gcur.v(), MODS[2 * 2 + 0])
    p3_mx_load(0)
    P.dma(woq[0].v(), WOv[:, :, 0:512])
    p3_loads(0)
    for db in range(1, 4):
        P.dma(woq[db].v(), WOv[:, :, db * 512:(db + 1) * 512])
    for k, (b, db) in enumerate(steps):
        sample = b < NSB
        g0 = b * TB
        ydst = y_s if sample else y_p
        x0 = g0 if sample else g0 - LS
        if b == NSB and db == 0:
            P.dma(Gcur.v(), MODS[2 * 2 + 1])
        if k + 1 < len(steps):
            p3_loads(k + 1)
        if db == 0 and b + 1 < NBLK:
            p3_mx_load(b + 1)
        w_ = woq[db]
        xo_ = xo[k % 2]
        mx_ = mxs[b % 2]
        for t in range(4):
            pb = bank(0, 8, 'o')
            for c in range(16):
                P.mm(pb.v(), mx_[:, c, t * 128:(t + 1) * 128], w_[:, c, :], start=(c == 0), stop=(c == 15))
            y_ = yt_[yi[0] % 3]
            yi[0] += 1
            P.tt('dve', y_.v(), pb.v(), Gcur[:, db * 512:(db + 1) * 512], ALU.mult)
            P.tt('pool', y_.v(), y_.v(), xo_[:, t, :], ALU.add)
            P.dma(ydst[x0 + t * 128: x0 + (t + 1) * 128, db * 512:(db + 1) * 512], y_.v())

    P.emit()
    st.close()
    return nc


_CACHE = {}


def _rope_tables():
    f64 = np.float64
    pos = np.arange(LS)
    row = (pos // 64).astype(f64)
    col = (pos % 64).astype(f64)
    fr16 = 10000.0 ** (-(np.arange(16, dtype=f64)) / 16.0)
    fr64 = 10000.0 ** (-(np.arange(64, dtype=f64)) / 64.0)
    ang_row = row[None, :] * fr16[:, None]
    ang_col = col[None, :] * fr16[:, None]
    ang_ret = pos.astype(f64)[None, :] * fr64[:, None]
    mcos = np.concatenate([np.cos(ang_row), np.cos(ang_row), np.cos(ang_col), np.cos(ang_col)], 0)
    msin = np.concatenate([-np.sin(ang_row), np.sin(ang_row), -np.sin(ang_col), np.sin(ang_col)], 0)
    rcos = np.concatenate([np.cos(ang_ret), np.cos(ang_ret)], 0)
    rsin = np.concatenate([-np.sin(ang_ret), np.sin(ang_ret)], 0)
    return [np.ascontiguousarray(a, dtype=np.float32) for a in (mcos, msin, rcos, rsin)]


def _consts():
    if 'c' in _CACHE:
        return _CACHE['c']
    f32 = np.float32
    ident = np.eye(128, dtype=f32)
    perm = np.zeros((128, 128), f32)
    for m in range(128):
        perm[(m + 64) % 128, m] = 1.0
    j = np.arange(128)[:, None].astype(f32)
    i = np.arange(128)[None, :].astype(f32)
    rt = np.zeros((128, NRT), f32)
    rt[:, RT_D1:RT_D1 + 128] = np.maximum(i - j, 0)
    rt[:, RT_D2:RT_D2 + 128] = np.maximum(j - i, 0)
    rt[:, RT_QDF:RT_QDF + 128] = i + 1
    rt[:, RT_QDB:RT_QDB + 128] = 128 - i
    rt[:, RT_KDF] = 127 - j[:, 0]
    rt[:, RT_KDB] = j[:, 0]
    rt[:, RT_C] = 128
    mcos, msin, rcos, rsin = _rope_tables()
    c = dict(ident=ident, permret=perm, rettab=rt, mcos=mcos, msin=msin, rcos=rcos, rsin=rsin)
    _CACHE['c'] = c
    return c


def _prep_shared(inp):
    f32 = np.float32
    w_in = np.asarray(inp['w_in'], f32)[0]
    o = np.cumsum([0, 384, 256, 64, 1024, 512, 512, 1024, 1024])
    q_lat, ckv, krope, g_a, rq, rk, rv, g_b = [w_in[:, o[k]:o[k + 1]] for k in range(8)]
    p64 = np.array([(i + 16) if (i % 32) < 16 else (i - 16) for i in range(64)])
    w_in2 = np.concatenate([q_lat, krope, krope[:, p64], ckv, g_a, rq, rk, g_b, rv, ckv, krope], axis=1)
    assert w_in2.shape[1] == WIN_COLS
    w_uq = np.asarray(inp['mla_w_uq'], f32)[0].reshape(384, 8, 192)
    w_uq2 = np.concatenate([w_uq[:, :, :128], w_uq[:, :, 128:], w_uq[:, :, 128:][:, :, p64]], axis=2).reshape(384, 2048)
    qw = np.asarray(inp['mla_qk_q_w'], f32)[0]
    kw = np.asarray(inp['mla_qk_k_w'], f32)[0]
    vecs = np.zeros((128, NV), f32)
    vecs[:, V_QNW:V_QNW + 3] = np.asarray(inp['mla_q_norm_w'], f32)[0].reshape(3, 128).T
    vecs[:, V_KVW:V_KVW + 2] = np.asarray(inp['mla_kv_norm_w'], f32)[0].reshape(2, 128).T
    vecs[:, V_WQN] = qw[:128]
    vecs[:64, V_WQR] = qw[128:]
    vecs[:64, V_WQP] = qw[128:][p64]
    vecs[:, V_WKN] = kw[:128]
    vecs[:64, V_WKR] = kw[128:]
    vecs[:64, V_WKP] = kw[128:][p64]
    vecs[:, V_GNW:V_GNW + 8] = np.asarray(inp['ret_gn_w'], f32)[0].reshape(8, 128).T
    lgd = np.concatenate([np.asarray(inp['ret_log_decay_fwd'], f32)[0], np.asarray(inp['ret_log_decay_bwd'], f32)[0]])[None, :]
    sh = dict(
        w_mod=np.ascontiguousarray(np.asarray(inp['w_mod'], f32)[0]),
        b_mod=np.ascontiguousarray(np.asarray(inp['b_mod'], f32)[0][None, :]),
        norm_w=np.ascontiguousarray(np.asarray(inp['norm_w'], f32)[0][None, :]),
        w_in=np.ascontiguousarray(w_in2),
        w_uq=np.ascontiguousarray(w_uq2),
        w_uk=np.ascontiguousarray(np.asarray(inp['mla_w_uk'], f32)[0]),
        w_uv=np.ascontiguousarray(np.asarray(inp['mla_w_uv'], f32)[0]),
        w_out=np.ascontiguousarray(np.asarray(inp['w_out'], f32)[0]),
        vecs=vecs,
        kvw_row=np.ascontiguousarray(np.asarray(inp['mla_kv_norm_w'], f32)[0][None, :]),
        lgd=np.ascontiguousarray(lgd),
    )
    sh.update(_consts())
    return sh


def make_in_maps(inp, cores=range(8)):
    f32 = np.float32
    sh = _prep_shared(inp)
    xs = np.asarray(inp['x_sample'], f32)
    xp = np.asarray(inp['x_prompt'], f32)
    c = np.asarray(inp['c'], f32)
    c_ctx = np.asarray(inp['c_ctx'], f32)
    maps = []
    for i in cores:
        m = dict(sh)
        m['xs'] = np.ascontiguousarray(xs[i])
        m['xp'] = np.ascontiguousarray(xp[4 * i:4 * i + 4].reshape(NPS * LP, D))
        cv = np.stack([c[i], c_ctx], 0)
        m['cT'] = np.ascontiguousarray(cv.reshape(2, 16, 128).transpose(2, 0, 1).reshape(128, 32))
        m['cache_ckv'] = np.ascontiguousarray(np.asarray(inp['cache_mla_ckv'], f32)[i, 0])
        m['cache_kr'] = np.ascontiguousarray(np.asarray(inp['cache_mla_krope'], f32)[i, 0])
        m['st_f'] = np.ascontiguousarray(np.asarray(inp['state_ret_fwd'], f32)[i, 0])
        m['st_b'] = np.ascontiguousarray(np.asarray(inp['state_ret_bwd'], f32)[i, 0])
        maps.append(m)
    return maps


def kernel(**inputs):
    if 'nc' not in _CACHE:
        _CACHE['nc'] = build_program(False)
    nc = _CACHE['nc']
    in_maps = make_in_maps(inputs)
    res = run_bass_kernel_spmd(nc, in_maps, core_ids=list(range(8)))
    R = res.results
    y_p = np.concatenate([r['y_p'].reshape(NPS, LP, D) for r in R], 0)
    y_s = np.stack([r['y_s'] for r in R], 0)
    ckv = np.concatenate([r['o_ckv'].reshape(NPS, 1, LP, 256) for r in R], 0)
    kr = np.concatenate([r['o_kr'].reshape(NPS, 1, LP, 64) for r in R], 0)
    sf = np.concatenate([r['o_sf'].reshape(NPS, 1, RH, 128, 256) for r in R], 0)
    sb = np.concatenate([r['o_sb'].reshape(NPS, 1, RH, 128, 256) for r in R], 0)
    return (y_p.astype(np.float32), y_s.astype(np.float32), ckv.astype(np.float32), kr.astype(np.float32),
            sf.astype(np.float32), sb.astype(np.float32))
```
